# Optimizing a Trainium2 kernel written in Bass

```python
import jax, jax.numpy as jnp
from jax import lax
import numpy as np

D_MODEL = 1024
BATCH = 2
SEQ = 8192
DEPTH = 1
DEC_BATCH = 16
DEC_SEQ = 64
PAST_LEN = 1024

CHUNK = 64
N_HEADS = 8
HEAD_DIM = 64
D_ATTN = N_HEADS * HEAD_DIM
D_RNN = 512
N_RNN_BLOCKS = 8
RNN_BLOCK = D_RNN // N_RNN_BLOCKS
CONV_WIDTH = 4
LRU_C = 8.0
D_FF = 4 * D_MODEL
Q_BLOCK = 128
EPS = 1e-6
SPLITS = (D_ATTN, 2 * D_ATTN, 3 * D_ATTN, 3 * D_ATTN + D_RNN, 3 * D_ATTN + 2 * D_RNN,
          3 * D_ATTN + 2 * D_RNN + D_MODEL)
D_IN = 3 * D_ATTN + 2 * D_RNN + 2 * D_MODEL

kernel_name = "hybrid_stickbreak_rglru_stream_step"


def rms_norm(x, g):
    xf = x.astype(jnp.float32)
    y = xf * lax.rsqrt(jnp.mean(xf * xf, axis=-1, keepdims=True) + EPS)
    return (y * g.astype(jnp.float32)).astype(x.dtype)


def _sb_block(q, k, v, q_pos, k_pos):
    z = jnp.einsum("bqhd,bkhd->bhqk", q, k).astype(jnp.float32) * (HEAD_DIM ** -0.5)
    causal = k_pos[None, :] < q_pos[:, None]
    log_keep = jnp.where(causal, jax.nn.log_sigmoid(-z), 0.0)
    rev = lax.cumsum(log_keep, axis=3, reverse=True)
    after = jnp.concatenate([rev[..., 1:], jnp.zeros_like(rev[..., :1])], axis=-1)
    w = jnp.where(causal, jnp.exp(jax.nn.log_sigmoid(z) + after), 0.0)
    return jnp.einsum("bhqk,bkhd->bqhd", w.astype(v.dtype), v)


def stick_breaking_attention(q, k, v, past_len):
    b, tq = q.shape[0], q.shape[1]
    k_pos = jnp.arange(k.shape[1])
    q_pos = past_len + jnp.arange(tq)
    if tq <= Q_BLOCK:
        return _sb_block(q, k, v, q_pos, k_pos)
    nb = tq // Q_BLOCK
    qb = q.reshape(b, nb, Q_BLOCK, N_HEADS, HEAD_DIM).transpose(1, 0, 2, 3, 4)
    pb = q_pos.reshape(nb, Q_BLOCK)
    out = lax.map(lambda a: _sb_block(a[0], k, v, a[1], k_pos), (qb, pb))
    return out.transpose(1, 0, 2, 3, 4).reshape(b, tq, N_HEADS, HEAD_DIM)


def causal_conv(x, past, w, bias):
    xp = jnp.concatenate([past, x], axis=1)
    t = x.shape[1]
    y = bias + xp[:, 0:t] * w[0]
    for j in range(1, CONV_WIDTH):
        y = y + xp[:, j:j + t] * w[j]
    return y, xp[:, -(CONV_WIDTH - 1):]


def _lin_combine(left, right):
    a1, b1 = left
    a2, b2 = right
    return a1 * a2, a2 * b1 + b2


def rg_lru(x, h0, w_r, b_r, w_i, b_i, lam):
    b, t, _ = x.shape
    xb = x.reshape(b, t, N_RNN_BLOCKS, RNN_BLOCK)
    r = jax.nn.sigmoid(jnp.einsum("btnc,ncd->btnd", xb, w_r).reshape(b, t, D_RNN) + b_r)
    i = jax.nn.sigmoid(jnp.einsum("btnc,ncd->btnd", xb, w_i).reshape(b, t, D_RNN) + b_i)
    log_a = -LRU_C * jax.nn.softplus(-lam.astype(jnp.float32)) * r.astype(jnp.float32)
    a = jnp.exp(log_a)
    u = jnp.sqrt(-jnp.expm1(2.0 * log_a)) * (i * x).astype(jnp.float32)
    a_cum, h = lax.associative_scan(_lin_combine, (a, u), axis=1)
    h = h + a_cum * h0[:, None, :].astype(jnp.float32)
    return h.astype(x.dtype), h[:, -1].astype(x.dtype)


def hybrid_layer(x, k_past, v_past, conv_past, h_past, w_in, g_pre_mix, w_conv, b_conv,
                 w_r, b_r, w_i, b_i, lam, w_a_out, w_b_out, w_o, g_post_mix,
                 g_pre_ffn, w_up, w_down, g_post_ffn):
    b, t, _ = x.shape
    past_len = k_past.shape[1]
    xn = rms_norm(x, g_pre_mix)
    proj = xn @ w_in
    q, k, v, u, g_rnn, g_a, g_b = jnp.split(proj, list(SPLITS), axis=-1)
    q = q.reshape(b, t, N_HEADS, HEAD_DIM)
    k = k.reshape(b, t, N_HEADS, HEAD_DIM)
    v = v.reshape(b, t, N_HEADS, HEAD_DIM)
    k_all = jnp.concatenate([k_past, k], axis=1)
    v_all = jnp.concatenate([v_past, v], axis=1)
    o = stick_breaking_attention(q, k_all, v_all, past_len)
    y_a = o.reshape(b, t, D_ATTN) @ w_a_out
    uc, conv_new = causal_conv(u, conv_past, w_conv, b_conv)
    hseq, h_last = rg_lru(uc, h_past, w_r, b_r, w_i, b_i, lam)
    y_b = (hseq * jax.nn.gelu(g_rnn)) @ w_b_out
    mix = (jax.nn.sigmoid(g_a) * y_a + jax.nn.sigmoid(g_b) * y_b) @ w_o
    x = x + rms_norm(mix, g_post_mix)
    hf = jnp.square(jax.nn.relu(rms_norm(x, g_pre_ffn) @ w_up))
    x = x + rms_norm(hf @ w_down, g_post_ffn)
    return x, k, v, conv_new, h_last


def setup_inputs(seed: int = 0) -> dict:
    key = jax.random.key(seed)
    ks = jax.random.split(key, 32)
    f32 = jnp.float32
    L = DEPTH

    def nrm(k, shape, scale):
        return jax.random.normal(k, shape, f32) * scale

    u = jax.random.uniform(ks[14], (L, D_RNN), f32, 0.9, 0.999)
    s = u ** (1.0 / LRU_C)
    lam = jnp.log(s) - jnp.log1p(-s)
    return {
        "x_prompt": nrm(ks[0], (BATCH, SEQ, D_MODEL), 1.0),
        "x_sample": nrm(ks[1], (DEC_BATCH, DEC_SEQ, D_MODEL), 1.0),
        "cache_k": nrm(ks[2], (L, DEC_BATCH, PAST_LEN, N_HEADS, HEAD_DIM), 1.0),
        "cache_v": nrm(ks[3], (L, DEC_BATCH, PAST_LEN, N_HEADS, HEAD_DIM), 1.0),
        "state_conv": nrm(ks[4], (L, DEC_BATCH, CONV_WIDTH - 1, D_RNN), 1.0),
        "state_h": nrm(ks[5], (L, DEC_BATCH, D_RNN), 0.5),
        "w_in": nrm(ks[6], (L, D_MODEL, D_IN), D_MODEL ** -0.5),
        "g_pre_mix": 1.0 + nrm(ks[7], (L, D_MODEL), 0.01),
        "w_conv": nrm(ks[8], (L, CONV_WIDTH, D_RNN), CONV_WIDTH ** -0.5),
        "b_conv": nrm(ks[9], (L, D_RNN), 0.01),
        "w_r": nrm(ks[10], (L, N_RNN_BLOCKS, RNN_BLOCK, RNN_BLOCK), RNN_BLOCK ** -0.5),
        "b_r": nrm(ks[11], (L, D_RNN), 0.01),
        "w_i": nrm(ks[12], (L, N_RNN_BLOCKS, RNN_BLOCK, RNN_BLOCK), RNN_BLOCK ** -0.5),
        "b_i": nrm(ks[13], (L, D_RNN), 0.01),
        "lam": lam,
        "w_a_out": nrm(ks[15], (L, D_ATTN, D_MODEL), D_ATTN ** -0.5),
        "w_b_out": nrm(ks[16], (L, D_RNN, D_MODEL), D_RNN ** -0.5),
        "w_o": nrm(ks[17], (L, D_MODEL, D_MODEL), D_MODEL ** -0.5),
        "g_post_mix": 1.0 + nrm(ks[18], (L, D_MODEL), 0.01),
        "g_pre_ffn": 1.0 + nrm(ks[19], (L, D_MODEL), 0.01),
        "w_up": nrm(ks[20], (L, D_MODEL, D_FF), D_MODEL ** -0.5),
        "w_down": nrm(ks[21], (L, D_FF, D_MODEL), D_FF ** -0.5),
        "g_post_ffn": 1.0 + nrm(ks[22], (L, D_MODEL), 0.01),
    }


def reference(x_prompt, x_sample, cache_k, cache_v, state_conv, state_h, w_in, g_pre_mix,
              w_conv, b_conv, w_r, b_r, w_i, b_i, lam, w_a_out, w_b_out, w_o, g_post_mix,
              g_pre_ffn, w_up, w_down, g_post_ffn):
    dt = x_prompt.dtype
    kp0 = jnp.zeros((BATCH, 0, N_HEADS, HEAD_DIM), dt)
    cp0 = jnp.zeros((BATCH, CONV_WIDTH - 1, D_RNN), dt)
    hp0 = jnp.zeros((BATCH, D_RNN), dt)
    xp, xs = x_prompt, x_sample
    kps, vps, cps, hps, kss, vss, css, hss = [], [], [], [], [], [], [], []
    for l in range(DEPTH):
        wl = (w_in[l], g_pre_mix[l], w_conv[l], b_conv[l], w_r[l], b_r[l], w_i[l], b_i[l],
              lam[l], w_a_out[l], w_b_out[l], w_o[l], g_post_mix[l], g_pre_ffn[l],
              w_up[l], w_down[l], g_post_ffn[l])
        xp, kp, vp, cp, hp = hybrid_layer(xp, kp0, kp0, cp0, hp0, *wl)
        xs, ksn, vsn, csn, hsn = hybrid_layer(xs, cache_k[l], cache_v[l], state_conv[l],
                                              state_h[l], *wl)
        kps.append(kp); vps.append(vp); cps.append(cp); hps.append(hp)
        kss.append(ksn); vss.append(vsn); css.append(csn); hss.append(hsn)
    return (xp, xs, jnp.stack(kps), jnp.stack(vps), jnp.stack(cps), jnp.stack(hps),
            jnp.stack(kss), jnp.stack(vss), jnp.stack(css), jnp.stack(hss))
```

```python
import contextlib
import os
import numpy as np
import concourse.bass as bass
import concourse.mybir as mybir
from concourse.bass_utils import run_bass_kernel_spmd

F32 = mybir.dt.float32
BF16 = mybir.dt.bfloat16
AF = mybir.ActivationFunctionType
ALU = mybir.AluOpType
AX = mybir.AxisListType

D = 1024
NOWN = 17
NTOK = NOWN * 128
EPS = 1e-6
GELU_C = 1.5957691216057308


class _Stop(Exception):
    pass


STOP = os.environ.get("MK_STOP", "")


def stop(tag):
    if STOP == tag:
        raise _Stop()


class Res:
    __slots__ = ("w", "r", "excl")

    def __init__(self):
        self.w = None
        self.r = {}
        self.excl = False


class RD(dict):
    def __missing__(self, k):
        v = Res()
        self[k] = v
        return v


class Q:
    def __init__(self, name, eng, sem):
        self.name = name
        self.eng = eng
        self.sem = sem
        self.cnt = 0
        self.seen = {}
        self.dsems = []
        self.dcnt = []
        self.di = 0

    def wait(self, tk):
        sem, val, key, qn = tk
        if qn == "pe" and self.name == "pe":
            return
        if self.seen.get(key, 0) >= val:
            return
        self.eng.wait_ge(sem, val)
        self.seen[key] = val


class Sched:
    def __init__(self, nc, es):
        self.nc = nc
        mk = lambda n: es.enter_context(nc.semaphore(n))
        self.pe = Q("pe", nc.tensor, mk("s_pe"))
        self.act = Q("act", nc.scalar, mk("s_act"))
        self.dve = Q("dve", nc.vector, mk("s_dve"))
        self.pool = Q("pool", nc.gpsimd, mk("s_pool"))
        self.sp = Q("sp", nc.sync, None)
        for q, pre, n in ((self.sp, "dsp", 32), (self.pool, "dpl", 32)):
            q.dsems = [mk(f"{pre}{i}") for i in range(n)]
            q.dcnt = [0] * n
        self.out_tk = []

    def _deps(self, q, reads, writes):
        for r in reads:
            if r.w is not None:
                q.wait(r.w)
        for w in writes:
            if w.w is not None:
                q.wait(w.w)
            for tk in w.r.values():
                q.wait(tk)

    def _mark(self, tk, reads, writes):
        for r in reads:
            r.r[tk[2]] = tk
        for w in writes:
            w.w = tk
            w.r = {}

    def op(self, q, fn, reads=(), writes=()):
        if any(r.excl for r in reads):
            writes = list(writes) + [r for r in reads if r.excl]
            reads = [r for r in reads if not r.excl]
        self._deps(q, reads, writes)
        ins = fn()
        q.cnt += 1
        ins.then_inc(q.sem, 1)
        self._mark((q.sem, q.cnt, q.name, q.name), reads, writes)

    def dma(self, q, out, in_, reads=(), writes=(), final=False):
        self._deps(q, reads, writes)
        i = q.di
        q.di = (i + 1) % len(q.dsems)
        key = f"{q.name}d{i}"
        if q.dcnt[i] > 0:
            q.wait((q.dsems[i], q.dcnt[i], key, "dma"))
        q.eng.dma_start(out=out, in_=in_).then_inc(q.dsems[i], 16)
        q.dcnt[i] += 16
        tk = (q.dsems[i], q.dcnt[i], key, "dma")
        self._mark(tk, reads, writes)
        if final:
            self.out_tk.append(tk)

    def finish(self):
        for tk in self.out_tk:
            self.sp.wait(tk)


class Rot:
    def __init__(self, bufs):
        self.bufs = [(b, Res()) for b in bufs]
        self.i = 0

    @classmethod
    def of(cls, pairs):
        r = cls([])
        r.bufs = list(pairs)
        return r

    def next(self):
        b = self.bufs[self.i]
        self.i = (self.i + 1) % len(self.bufs)
        return b


def build():
    nc = bass.Bass("TRN2", target_bir_lowering=False)

    def dt(name, shape, dtype=F32, kind="ExternalInput"):
        return nc.dram_tensor(name, shape, dtype, kind=kind).ap()

    xf = dt("xf", [8192, D])
    xsm = dt("xsm", [128, D])
    ck = dt("ck", [2, 1024, 512])
    cvd = dt("cv", [2, 1024, 512])
    sconvT = dt("sconvT", [128, 4, 2, 3])
    shT = dt("shT", [128, 4, 2])
    valid_d = dt("valid", [128, 4])
    gv = dt("gv", [4, 128, D])
    cvec_d = dt("cvec", [128, 4, 8])
    wrbd = dt("wrbd", [4, 128, 128])
    wibd = dt("wibd", [4, 128, 128])
    w_in = dt("w_in", [D, 4608])
    w_a = dt("w_a", [512, D])
    w_b = dt("w_b", [512, D])
    w_o = dt("w_o", [D, D])
    w_up = dt("w_up", [D, 4096])
    w_dn = dt("w_dn", [4096, D])
    O = "ExternalOutput"
    y_own = dt("y_own", [2048, D], kind=O)
    y_s = dt("y_s", [128, D], kind=O)
    k_own = dt("k_own", [2048, 512], kind=O)
    v_own = dt("v_own", [2048, 512], kind=O)
    k_s = dt("k_s", [128, 512], kind=O)
    v_s = dt("v_s", [128, 512], kind=O)
    convT_p = dt("convT_p", [128, 4, 3], kind=O)
    hT_p = dt("hT_p", [128, 4], kind=O)
    convT_s = dt("convT_s", [128, 4, 2, 3], kind=O)
    hT_s = dt("hT_s", [128, 4, 2], kind=O)
    KT_d = dt("KT_d", [4, 128, 8192], BF16, kind="Internal")
    V_d = dt("V_d", [64, 128, 512], BF16, kind="Internal")


    es = contextlib.ExitStack()
    with es:
        S = Sched(nc, es)
        nest = []
        try:
            _body(nc, es, S, dict(locals()))
        except _Stop:
            for st in reversed(nest):
                st.__exit__(None, None, None)
        S.finish()
    return nc


def _body(nc, es, S, L):
    if True:
        globals_ = L
        (xf, xsm, ck, cvd, sconvT, shT, valid_d, gv, cvec_d, wrbd, wibd, w_in, w_a, w_b, w_o, w_up, w_dn, y_own, y_s, k_own, v_own,
         k_s, v_s, convT_p, hT_p, convT_s, hT_s, KT_d, V_d, dt) = [L[k] for k in (
            "xf", "xsm", "ck", "cvd", "sconvT", "shT", "valid_d", "gv", "cvec_d", "wrbd", "wibd", "w_in", "w_a", "w_b", "w_o", "w_up",
            "w_dn", "y_own", "y_s", "k_own", "v_own", "k_s", "v_s", "convT_p", "hT_p", "convT_s", "hT_s", "KT_d", "V_d", "dt")]
        pe, act, dve, pool, sp = S.pe, S.act, S.dve, S.pool, S.sp
        nest = L["nest"]

        def push():
            st = contextlib.ExitStack()
            st.__enter__()
            nest.append(st)
            return st

        def pop(st):
            assert nest[-1] is st
            nest.pop()
            st.__exit__(None, None, None)

        used_names = {}

        def sb(name, shape, dtype=F32, stack=es):
            k = used_names.get(name, 0)
            used_names[name] = k + 1
            nm = "sb_" + name + (f"_v{k}" if k else "")
            return stack.enter_context(nc.sbuf_tensor(nm, shape, dtype))

        bigs = [es.enter_context(nc.psum_tensor(f"pb{i}", [128, 1024], F32)) for i in range(3)]
        banks = Rot([bigs[i // 2][:, (i % 2) * 512:(i % 2 + 1) * 512] for i in range(6)])
        pts = Rot([es.enter_context(nc.psum_tensor(f"pt{i}", [128, 1024], BF16)) for i in range(2)])
        for _, r_ in banks.bufs + pts.bufs:
            r_.excl = True

        identf = sb("identf", [128, 128])
        ident = sb("ident", [128, 128], BF16)
        ntri = sb("ntri", [128, 128], BF16)
        nones = sb("nones", [128, 128], BF16)
        dmask = sb("dmask", [128, 128], BF16)
        tmpc = sb("tmpc", [128, 128])
        mneg = sb("mneg", [128, 128], BF16)
        c_one = sb("c_one", [128, 1])
        c_eps = sb("c_eps", [128, 1])
        validt = sb("validt", [128, 4])
        gts = sb("gts", [128, 4, D])
        cvec = sb("cvec", [128, 4, 8])
        coef = sb("coef", [128, 4])
        coef2 = sb("coef2", [128, 4])
        ctmp = sb("ctmp", [128, 4])
        wr = sb("wr", [128, 4, 128], BF16)
        wi = sb("wi", [128, 4, 128], BF16)
        ktn = sb("ktn", [128, 4, 128], BF16)
        vn = [sb(f"vn{s}", [64, 512], BF16) for s in range(2)]
        R = RD()

        def act_fn(out, in_, func, reads, writes, bias=None, scale=None):
            kw = {}
            if bias is not None:
                kw["bias"] = bias
            if scale is not None:
                kw["scale"] = scale
            S.op(act, lambda: nc.scalar.activation(out=out, in_=in_, func=func, **kw), reads, writes)

        def tt(q, out, in0, in1, op, reads, writes):
            S.op(q, lambda: q.eng.tensor_tensor(out=out, in0=in0, in1=in1, op=op), reads, writes)

        def ts(q, out, in0, s1, s2, op0, op1, reads, writes):
            if op1 is None:
                S.op(q, lambda: q.eng.tensor_scalar(out=out, in0=in0, scalar1=s1, scalar2=None, op0=op0), reads, writes)
            else:
                S.op(q, lambda: q.eng.tensor_scalar(out=out, in0=in0, scalar1=s1, scalar2=s2, op0=op0, op1=op1), reads, writes)

        def stt(out, in0, scalar, in1, op0, op1, reads, writes, accum_out=None):
            kw = {} if accum_out is None else {"accum_out": accum_out}
            S.op(dve, lambda: nc.vector.scalar_tensor_tensor(out=out, in0=in0, scalar=scalar, in1=in1, op0=op0, op1=op1, **kw),
                 reads, writes)

        def cp(q, out, in_, reads, writes):
            if q is act:
                S.op(act, lambda: nc.scalar.copy(out=out, in_=in_), reads, writes)
            else:
                S.op(q, lambda: q.eng.tensor_copy(out=out, in_=in_), reads, writes)

        def mm(out, lhsT, rhs, start, stop, reads, writes, skip=False):
            S.op(pe, lambda: nc.tensor.matmul(out, lhsT=lhsT, rhs=rhs, start=start, stop=stop, skip_group_check=skip), reads, writes)

        def tr(out, in_, reads, writes):
            S.op(pe, lambda: nc.tensor.transpose(out=out, in_=in_, identity=ident[:]), reads + [R["const"]], writes)

        C = R["const"]
        S.op(pool, lambda: nc.gpsimd.memset(identf[:], 0.0), [], [C])
        S.op(pool, lambda: nc.gpsimd.affine_select(out=identf[:], in_=identf[:], pattern=[[-1, 128]], compare_op=ALU.not_equal,
                                                   fill=1.0, base=0, channel_multiplier=1), [], [C])
        cp(pool, ident[:], identf[:], [], [C])
        S.op(pool, lambda: nc.gpsimd.memset(tmpc[:], -1.0), [], [C])
        S.op(pool, lambda: nc.gpsimd.affine_select(out=tmpc[:], in_=tmpc[:], pattern=[[-1, 128]], compare_op=ALU.is_ge,
                                                   fill=0.0, base=0, channel_multiplier=1), [], [C])
        cp(pool, ntri[:], tmpc[:], [], [C])
        S.op(pool, lambda: nc.gpsimd.memset(nones[:], -1.0), [], [C])
        S.op(pool, lambda: nc.gpsimd.memset(tmpc[:], 1.0), [], [C])
        S.op(pool, lambda: nc.gpsimd.affine_select(out=tmpc[:], in_=tmpc[:], pattern=[[1, 128]], compare_op=ALU.is_gt,
                                                   fill=0.0, base=0, channel_multiplier=-1), [], [C])
        cp(pool, dmask[:], tmpc[:], [], [C])
        S.op(pool, lambda: nc.gpsimd.memset(tmpc[:], 0.0), [], [C])
        S.op(pool, lambda: nc.gpsimd.affine_select(out=tmpc[:], in_=tmpc[:], pattern=[[1, 128]], compare_op=ALU.is_gt,
                                                   fill=-30000.0, base=0, channel_multiplier=-1), [], [C])
        cp(pool, mneg[:], tmpc[:], [], [C])
        S.op(pool, lambda: nc.gpsimd.memset(c_one[:], 1.0), [], [C])
        S.op(pool, lambda: nc.gpsimd.memset(c_eps[:], EPS), [], [C])
        S.dma(sp, validt[:], valid_d[:, :], [], [C])
        S.dma(sp, cvec[:], cvec_d[:, :, :], [], [C])
        for i in range(4):
            S.dma(sp, gts[:, i, :], gv[i, :, :], [], [C])
        S.dma(pool, wr[:], wrbd.rearrange("c p n -> p c n"), [], [C])
        S.dma(pool, wi[:], wibd.rearrange("c p n -> p c n"), [], [C])
        act_fn(ctmp[:], cvec[:, :, 7], AF.Exp, [C], [C], scale=-1.0)
        act_fn(ctmp[:], ctmp[:], AF.Ln, [C], [C], bias=c_one[:, 0:1])
        ts(dve, coef[:], ctmp[:], -8.0, None, ALU.mult, None, [C], [C])
        ts(dve, coef2[:], ctmp[:], -16.0, None, ALU.mult, None, [C], [C])
        GPRE, GPOST, GPREF, GPOSTF = 0, 1, 2, 3

        stop("setup")
        xts = Rot([sb(f"xt{i}", [128, D]) for i in range(2)])
        junk = sb("junk", [128, D], BF16)
        xsb = Rot([sb(f"xsb{i}", [128, D], BF16) for i in range(2)])
        stat = Rot([sb(f"stat{i}", [128, 4]) for i in range(4)])

        def norm_stats(src, src_res, g_idx, out_bf, out_res):
            st, st_r = stat.next()
            S.op(act, lambda: nc.scalar.activation(out=junk[:], in_=src, func=AF.Square, accum_out=st[:, 0:1]),
                 [src_res], [R["junk"], st_r])
            act_fn(st[:, 1:2], st[:, 0:1], AF.Ln, [st_r, C], [st_r], bias=c_eps[:, 0:1], scale=1.0 / D)
            act_fn(st[:, 2:3], st[:, 1:2], AF.Exp, [st_r], [st_r], scale=-0.5)
            if out_bf is not None:
                stt(out_bf, src, st[:, 2:3], gts[:, g_idx, :], ALU.mult, ALU.mult, [src_res, st_r, C], [out_res])
            return st, st_r

        def transpose_to(src_bf, src_res, dst3, dst_res):
            pt, pt_r = pts.next()
            for kc in range(8):
                tr(pt[:, kc * 128:(kc + 1) * 128], src_bf[:, kc * 128:(kc + 1) * 128], [src_res], [pt_r])
            cp(dve, dst3, pt[:, :].rearrange("p (k n) -> p k n", k=8), [pt_r], [dst_res])

        def barrier():
            tks = []
            for q in (pe, act, dve, pool):
                if q.cnt:
                    tks.append((q.sem, q.cnt, q.name, q.name))
            for q in (sp, pool):
                for i, sm in enumerate(q.dsems):
                    if q.dcnt[i]:
                        tks.append((sm, q.dcnt[i], f"{q.name}d{i}", "dma"))
            for q in (pe, act, dve, pool, sp):
                for tk in tks:
                    q.wait(tk)

        attn = {}

        def attn_alloc(stack):
            attn["e"] = Rot([sb(f"eb{i}", [128, 512], F32, stack) for i in range(2)])
            attn["sp"] = Rot([sb(f"spb{i}", [128, 512], BF16, stack) for i in range(3)])
            attn["w"] = Rot([sb(f"wb{i}", [128, 512], BF16, stack) for i in range(3)])
            attn["red"] = Rot([sb(f"red{i}", [128, 128], F32, stack) for i in range(2)])
            attn["zb"] = Rot.of(banks.bufs[0:2])
            attn["wbk"] = Rot.of(banks.bufs[2:4])
            attn["ctx"] = []
            for i in range(2):
                attn["ctx"].append(dict(S32=sb(f"S32_{i}", [128, 128], F32, stack), S32r=Res(),
                                        Sb=[sb(f"Sb{i}_{k}", [128, 128], BF16, stack) for k in range(2)], Sbr=[Res(), Res()],
                                        O=banks.bufs[4 + i][0], Or=banks.bufs[4 + i][1]))

        def make_items(ctx, qT, q_res, TQ, groups, out_ap, out_res):
            items = []
            ntile_total = sum(len(g) for g in groups)
            seen = 0
            for k, g in enumerate(groups):
                items.append(dict(ctx=ctx, qT=qT, q_res=q_res, TQ=TQ, tiles=g, k=k, first=(k == 0), last=(k == len(groups) - 1),
                                  o_first=seen, ntot=ntile_total, out_ap=out_ap, out_res=out_res))
                seen += len(g)
            return items

        def st1(it):
            ctx, TQ, tiles = it["ctx"], it["TQ"], it["tiles"]
            ks = tiles[0]["ks"]
            n = len(tiles)
            z, z_r = attn["zb"].next()
            for i, t in enumerate(tiles):
                mm(z[0:ks, i * TQ:(i + 1) * TQ], t["kT"], it["qT"], True, True, t["res"] + [it["q_res"]], [z_r])
            e, e_r = attn["e"].next()
            spt, sp_r = attn["sp"].next()
            it["sp"], it["sp_r"] = spt, sp_r
            act_fn(e[0:ks, 0:n * TQ], z[0:ks, 0:n * TQ], AF.Exp, [z_r], [e_r])
            act_fn(spt[0:ks, 0:n * TQ], e[0:ks, 0:n * TQ], AF.Ln, [e_r, C], [sp_r], bias=c_one[0:ks, 0:1])
            for i, t in enumerate(tiles):
                sl = spt[0:ks, i * TQ:(i + 1) * TQ]
                if t["mask"]:
                    tt(pool, sl, sl, dmask[0:ks, 0:TQ], ALU.mult, [sp_r, C], [sp_r])
                if t["valid"] is not None:
                    ts(pool, sl, sl, t["valid"], None, ALU.mult, None, [sp_r, C], [sp_r])
            if not it["last"]:
                k = it["k"]
                S32, S32r = ctx["S32"], ctx["S32r"]
                Sbn, Sbnr = ctx["Sb"][(k + 1) % 2], ctx["Sbr"][(k + 1) % 2]
                if n > 1:
                    rd, rd_r = attn["red"].next()
                    S.op(dve, lambda: nc.vector.tensor_reduce(out=rd[0:ks, 0:TQ],
                                                              in_=spt[0:ks, 0:n * TQ].rearrange("p (n t) -> p t n", n=n),
                                                              axis=AX.X, op=ALU.add), [sp_r], [rd_r])
                    src, src_r = rd[0:ks, 0:TQ], rd_r
                else:
                    src, src_r = spt[0:ks, 0:TQ], sp_r
                if it["first"] and ks == 128:
                    cp(dve, S32[0:ks, 0:TQ], src, [src_r], [S32r])
                else:
                    tt(dve, S32[0:ks, 0:TQ], S32[0:ks, 0:TQ], src, ALU.add, [src_r, S32r], [S32r])
                cp(dve, Sbn[:, 0:TQ], S32[:, 0:TQ], [S32r], [Sbnr])

        def st2(it):
            ctx, TQ, tiles = it["ctx"], it["TQ"], it["tiles"]
            ks = tiles[0]["ks"]
            n = len(tiles)
            k = it["k"]
            w, w_r = attn["wbk"].next()
            spt, sp_r = it["sp"], it["sp_r"]
            Sb, Sbr = ctx["Sb"][k % 2], ctx["Sbr"][k % 2]
            carry = not it["first"]
            for i, t in enumerate(tiles):
                o = w[0:ks, i * TQ:(i + 1) * TQ]
                nlater = n - 1 - i
                mm(o, t["kT"], it["qT"], True, False, t["res"] + [it["q_res"]], [w_r])
                if ks < 128 and ctx is attn["ctx"][1]:
                    pe.eng.wait_ge(pe.sem, pe.cnt)
                mm(o, ntri[0:ks, 0:ks], spt[0:ks, i * TQ:(i + 1) * TQ], False, (nlater == 0 and not carry), [sp_r, C], [w_r])
                for i2 in range(i + 1, n):
                    mm(o, nones[0:ks, 0:ks], spt[0:ks, i2 * TQ:(i2 + 1) * TQ], False, (i2 == n - 1 and not carry), [sp_r, C], [w_r])
                if carry:
                    mm(o, nones[0:128, 0:ks], Sb[:, 0:TQ], False, True, [Sbr, C], [w_r])
            wt, wt_r = attn["w"].next()
            it["w"], it["w_r"] = wt, wt_r
            act_fn(wt[0:ks, 0:n * TQ], w[0:ks, 0:n * TQ], AF.Exp, [w_r], [wt_r])
            for i, t in enumerate(tiles):
                sl = wt[0:ks, i * TQ:(i + 1) * TQ]
                if t["mask"]:
                    tt(pool, sl, sl, dmask[0:ks, 0:TQ], ALU.mult, [wt_r, C], [wt_r])
                if t["valid"] is not None:
                    ts(pool, sl, sl, t["valid"], None, ALU.mult, None, [wt_r, C], [wt_r])

        def st3(it):
            ctx, TQ, tiles = it["ctx"], it["TQ"], it["tiles"]
            ks = tiles[0]["ks"]
            Ob, Or = ctx["O"], ctx["Or"]
            wt, wt_r = it["w"], it["w_r"]
            for i, t in enumerate(tiles):
                idx = it["o_first"] + i
                mm(Ob[0:64, 0:TQ], t["v"], wt[0:ks, i * TQ:(i + 1) * TQ], idx == 0, idx == it["ntot"] - 1, t["res"] + [wt_r], [Or])
            if it["last"]:
                cp(dve, it["out_ap"], Ob[0:64, 0:TQ], [Or], [it["out_res"]])

        def run_items(items):
            n = len(items)
            for k in range(n + 2):
                if k < n:
                    st1(items[k])
                if 0 <= k - 1 < n:
                    st2(items[k - 1])
                if 0 <= k - 2 < n:
                    st3(items[k - 2])

        def interleave(a, b):
            out = []
            for i in range(max(len(a), len(b))):
                if i < len(a):
                    out.append(a[i])
                if i < len(b):
                    out.append(b[i])
            return out

        esq = push()
        QT_all = sb("QT_all", [128, 4, NTOK], BF16, esq)
        HG_all = sb("HG_all", [128, 4, NTOK], BF16, esq)
        es1 = push()
        W1 = sb("W1", [128, 8, 2560], BF16, es1)
        for kc in range(8):
            S.dma(pool, W1[:, kc, :], w_in[kc * 128:(kc + 1) * 128, 0:2560], [], [R["W1"]])
        W1r = R["W1"]
        QO, KO, VO, UO, GO = 0, 512, 1024, 1536, 2048
        xnT = Rot([sb(f"xnT{i}", [128, 8, 512], BF16, es1) for i in range(2)])
        ktc = Rot([sb(f"ktc{i}", [128, 4, 512], BF16, es1) for i in range(1)])
        vc = Rot([sb(f"vc{i}", [128, 4, 512], BF16, es1) for i in range(1)])
        f512 = Rot([sb(f"f512_{i}", [128, 512], F32, es1) for i in range(2)])
        ubA = [sb(f"ub{i}", [128, 4, 515], F32, es1) for i in range(3)]
        uc = sb("uc", [128, 2, 512], F32, es1)
        ucb = sb("ucb", [128, 2, 512], BF16, es1)
        rr = sb("rr", [128, 2, 512], F32, es1)
        ii = sb("ii", [128, 2, 512], F32, es1)
        aa = sb("aa", [128, 2, 512], F32, es1)
        tq = sb("tq", [128, 2, 512], F32, es1)
        hs = sb("hs", [128, 4, 128], F32, es1)
        hcar = sb("hcar", [128, 4], F32, es1)
        gx = sb("gx", [128, 4, 128], F32, es1)
        gyA = [sb(f"gy{i}", [128, 4, 128], F32, es1) for i in range(2)]
        ubs = sb("ubs", [128, 4, 2, 67], F32, es1)
        hin = sb("hin", [128, 4, 2], F32, es1)
        hout = sb("hout", [128, 4, 2], F32, es1)

        for i_ in range(3):
            S.op(pool, lambda i_=i_: nc.gpsimd.memset(ubA[i_][:], 0.0), [], [R["ub", i_]])
        S.op(pool, lambda: nc.gpsimd.memset(hcar[:], 0.0), [], [R["hcar"]])

        def proj_fm(col0, ncols_tiles, rhs3, rhs_res, n, evac):
            for j in range(ncols_tiles):
                bk, bk_r = banks.next()
                for kc in range(8):
                    mm(bk[:, 0:n], W1[:, kc, col0 + j * 128: col0 + (j + 1) * 128], rhs3[:, kc, 0:n], kc == 0, kc == 7,
                       [W1r, rhs_res], [bk_r])
                evac(j, bk, bk_r)

        def proj_fm_packed(col0, rhs3_own, rhs_res, evac):
            bk, bk_r = banks.next()
            for j in range(4):
                for kc in range(8):
                    mm(bk[:, j * 128:(j + 1) * 128], W1[:, kc, col0 + j * 128: col0 + (j + 1) * 128], rhs3_own(kc), kc == 0, kc == 7,
                       [W1r, rhs_res], [bk_r])
            evac(bk, bk_r)

        def gelu_gate(bk, bk_r, gi=0):
            G = R["gelu", gi]
            GX = R["gx"]
            g2 = gx[:].rearrange("p a b -> p (a b)")
            y2 = gyA[gi][:].rearrange("p a b -> p (a b)")
            cp(act, g2, bk[:, :], [bk_r], [GX])
            tt(dve, y2, g2, g2, ALU.mult, [GX], [G])
            ts(dve, y2, y2, 0.044715, 1.0, ALU.mult, ALU.add, [G], [G])
            tt(dve, y2, y2, g2, ALU.mult, [G, GX], [G])
            act_fn(y2, y2, AF.Exp, [G], [G], scale=-GELU_C)
            ts(dve, y2, y2, 1.0, None, ALU.add, None, [G], [G])
            S.op(dve, lambda: nc.vector.reciprocal(out=y2, in_=y2), [G], [G])
            tt(dve, y2, y2, g2, ALU.mult, [G, GX], [G])

        def rnn_pair(cp2, n, chunk0, conv_src, conv_out, scan_fn, ub_r):
            Ruc, Rucb, Rrr, Rii, Raa, Rtq = R["uc"], R["ucb"], R["rr"], R["ii"], R["aa"], R["tq"]
            for k2 in range(2):
                ct = 2 * cp2 + k2
                o = conv_out(k2)
                ts(dve, o, conv_src(ct, 0), cvec[:, ct, 0:1], cvec[:, ct, 4:5], ALU.mult, ALU.add, [ub_r, C], [Ruc])
                for j in range(1, 4):
                    stt(o, conv_src(ct, j), cvec[:, ct, j:j + 1], o, ALU.mult, ALU.add, [Ruc, ub_r, C], [Ruc])
            yield
            cp(act, ucb[:, :, 0:n], uc[:, :, 0:n], [Ruc], [Rucb])
            for k2 in range(2):
                ct = 2 * cp2 + k2
                bk, bk_r = banks.next()
                mm(bk[:, 0:n], wr[:, ct, :], ucb[:, k2, 0:n], True, True, [Rucb, C], [bk_r])
                act_fn(rr[:, k2, 0:n], bk[:, 0:n], AF.Sigmoid, [bk_r, C], [Rrr], bias=cvec[:, ct, 5:6])
                bk, bk_r = banks.next()
                mm(bk[:, 0:n], wi[:, ct, :], ucb[:, k2, 0:n], True, True, [Rucb, C], [bk_r])
                act_fn(ii[:, k2, 0:n], bk[:, 0:n], AF.Sigmoid, [bk_r, C], [Rii], bias=cvec[:, ct, 6:7])
            yield
            for k2 in range(2):
                ct = 2 * cp2 + k2
                act_fn(aa[:, k2, 0:n], rr[:, k2, 0:n], AF.Exp, [Rrr, C], [Raa], scale=coef[:, ct:ct + 1])
                act_fn(tq[:, k2, 0:n], rr[:, k2, 0:n], AF.Exp, [Rrr, C], [Rtq], scale=coef2[:, ct:ct + 1])
            act_fn(tq[:, :, 0:n], tq[:, :, 0:n], AF.Ln, [Rtq, C], [Rtq], bias=c_one[:, 0:1], scale=-1.0)
            act_fn(tq[:, :, 0:n], tq[:, :, 0:n], AF.Exp, [Rtq], [Rtq], scale=0.5)
            yield
            tt(dve, ii[:, :, 0:n], ii[:, :, 0:n], uc[:, :, 0:n], ALU.mult, [Rii, Ruc], [Rii])
            tt(dve, tq[:, :, 0:n], tq[:, :, 0:n], ii[:, :, 0:n], ALU.mult, [Rtq, Rii], [Rtq])
            if chunk0:
                for t in range(3):
                    ts(dve, tq[:, :, t * 128:(t + 1) * 128], tq[:, :, t * 128:(t + 1) * 128], validt[:, t:t + 1], None, ALU.mult, None,
                       [Rtq, C], [Rtq])
            yield
            scan_fn(cp2)
            yield

        OWN_S = 16
        xt, xt_r = xts.next()
        S.dma(sp, xt[:], xsm[:, :], [], [xt_r])
        xb, xb_r = xsb.next()
        norm_stats(xt[:], xt_r, GPRE, xb[:], xb_r)
        xn, xn_r = xnT.next()
        transpose_to(xb, xb_r, xn[:, :, 0:128], xn_r)
        stop("sp1")
        own_rhs = lambda kc, xn=xn: xn[:, kc, 0:128]
        proj_fm_packed(KO, own_rhs, xn_r, lambda bk, bk_r: cp(act, ktn[:].rearrange("p a b -> p (a b)"), bk[:, :], [bk_r], [R["ktn"]]))
        proj_fm_packed(QO, own_rhs, xn_r, lambda bk, bk_r: act_fn(
            QT_all[:, :, OWN_S * 128:(OWN_S + 1) * 128], bk[:, :].rearrange("p (a b) -> p a b", a=4), AF.Copy, [bk_r], [R["QT", OWN_S]],
            scale=0.125))
        proj_fm_packed(GO, own_rhs, xn_r, gelu_gate)

        stop("sp2")

        def tok_major(xn, xn_r, col0, lhs_cols, nrow, evacs):
            bk, bk_r = banks.next()
            for kc in range(8):
                mm(bk[0:nrow, :], xn[:, kc, lhs_cols[0]:lhs_cols[1]], W1[:, kc, col0:col0 + 512], kc == 0, kc == 7, [W1r, xn_r], [bk_r])
            evacs(bk, bk_r)

        def k_tok_s(bk, bk_r):
            f, f_r = f512.next()
            cp(act, f[:], bk[:, :], [bk_r], [f_r])
            S.dma(pool, k_s[:, :], f[:], [f_r], [], final=True)
        tok_major(xn, xn_r, KO, (0, 128), 128, k_tok_s)
        stop("sp2a")
        for s in range(2):
            def v_s_ev(bk, bk_r, s=s):
                f, f_r = f512.next()
                cp(act, f[0:64, :], bk[0:64, :], [bk_r], [f_r])
                cp(dve, vn[s][:], bk[0:64, :], [bk_r], [R["vn", s]])
                S.dma(pool, v_s[s * 64:(s + 1) * 64, :], f[0:64, :], [f_r], [], final=True)
            tok_major(xn, xn_r, VO, (s * 64, (s + 1) * 64), 64, v_s_ev)
            stop("sp2b")
        stop("sp3")
        S.dma(sp, ubs[:, :, :, 0:3], sconvT[:, :, :, :], [], [R["ubs"]])
        S.dma(sp, hin[:], shT[:, :, :], [], [R["hin"]])
        proj_fm(UO, 4, xn, xn_r, 128, lambda j, bk, bk_r: cp(
            act, ubs[:, j, :, 3:67], bk[:, 0:128].rearrange("p (s t) -> p s t", s=2), [bk_r], [R["ubs"]]))

        stop("sp4")

        def scan_s(cp2):
            for k2 in range(2):
                ct = 2 * cp2 + k2
                for s in range(2):
                    S.op(dve, lambda ct=ct, s=s, k2=k2: nc.vector.tensor_tensor_scan(
                        out=hs[:, ct, s * 64:(s + 1) * 64], data0=aa[:, k2, s * 64:(s + 1) * 64], data1=tq[:, k2, s * 64:(s + 1) * 64],
                        initial=hin[:, ct, s:s + 1], op0=ALU.mult, op1=ALU.add), [R["aa"], R["tq"], R["hin"]], [R["hs"]])
        for cp2 in range(2):
            for _ in rnn_pair(cp2, 128, False, lambda ct, j: ubs[:, ct, :, j:j + 64],
                              lambda k2: uc[:, k2, 0:128].rearrange("p (s t) -> p s t", s=2), scan_s, R["ubs"]):
                pass
        tt(dve, HG_all[:, :, OWN_S * 128:(OWN_S + 1) * 128], hs[:, :, :], gyA[0][:], ALU.mult, [R["hs"], R["gelu", 0]], [R["HG", OWN_S]])
        cp(dve, hout[:], hs[:, :, :].rearrange("p c (s t) -> p c s t", s=2)[:, :, :, 63], [R["hs"]], [R["hout"]])
        S.dma(pool, hT_s[:, :, :], hout[:], [R["hout"]], [], final=True)
        S.dma(pool, convT_s[:, :, :, :], ubs[:, :, :, 64:67], [R["ubs"]], [], final=True)

        stop("sample_pre")
        def front(c):
            xn, xn_r = xnT.next()
            for t in range(4):
                ft = 4 * c + t
                xt, xt_r = xts.next()
                S.dma(sp, xt[:], xf[ft * 128:(ft + 1) * 128, :], [], [xt_r])
                xb, xb_r = xsb.next()
                norm_stats(xt[:], xt_r, GPRE, xb[:], xb_r)
                transpose_to(xb, xb_r, xn[:, :, t * 128:(t + 1) * 128], xn_r)
                yield
            ub, ub_r = ubA[c % 3], R["ub", c % 3]
            proj_fm(UO, 4, xn, xn_r, 512, lambda j, bk, bk_r: cp(act, ub[:, j, 3:515], bk[:, :], [bk_r], [ub_r]))
            if c + 1 < 16:
                cp(pool, ubA[(c + 1) % 3][:, :, 0:3], ub[:, :, 512:515], [ub_r], [R["ub", (c + 1) % 3]])
            yield
            kt_, kt_r = ktc.next()
            proj_fm(KO, 4, xn, xn_r, 512, lambda j, bk, bk_r: cp(act, kt_[:, j, :], bk[:, :], [bk_r], [kt_r]))
            S.dma(pool, KT_d[:, :, c * 512:(c + 1) * 512].rearrange("h p n -> p h n"), kt_[:], [kt_r], [R["KT_d"]])
            yield
            v_, v_r = vc.next()
            for t in range(4):
                bk, bk_r = banks.next()
                for kc in range(8):
                    mm(bk[:, :], xn[:, kc, t * 128:(t + 1) * 128], W1[:, kc, VO:VO + 512], kc == 0, kc == 7, [W1r, xn_r], [bk_r])
                cp(dve, v_[:, t, :], bk[:, :], [bk_r], [v_r])
                if t == 1:
                    yield
                if t == 3:
                    f, f_r = f512.next()
                    cp(act, f[:], bk[:, :], [bk_r], [f_r])
                    S.dma(pool, v_own[c * 128:(c + 1) * 128, :], f[:], [f_r], [], final=True)
            S.dma(pool, V_d[4 * c:4 * c + 4, :, :].rearrange("t p n -> p t n"), v_[:], [v_r], [R["V_d"]])
            yield
            bk, bk_r = banks.next()
            for kc in range(8):
                mm(bk[:, :], xn[:, kc, 384:512], W1[:, kc, KO:KO + 512], kc == 0, kc == 7, [W1r, xn_r], [bk_r])
            f, f_r = f512.next()
            cp(act, f[:], bk[:, :], [bk_r], [f_r])
            S.dma(pool, k_own[c * 128:(c + 1) * 128, :], f[:], [f_r], [], final=True)
            own_rhs = lambda kc, xn=xn: xn[:, kc, 384:512]
            proj_fm_packed(QO, own_rhs, xn_r, lambda bk, bk_r, c=c: act_fn(
                QT_all[:, :, c * 128:(c + 1) * 128], bk[:, :].rearrange("p (a b) -> p a b", a=4), AF.Copy, [bk_r], [R["QT", c]], scale=0.125))
            yield
            proj_fm_packed(GO, own_rhs, xn_r, lambda bk, bk_r, c=c: gelu_gate(bk, bk_r, c % 2))
            yield

        def rnn(c):
            ub, ub_r = ubA[c % 3], R["ub", c % 3]

            def scan_p(cp2):
                for k2 in range(2):
                    ct = 2 * cp2 + k2
                    S.op(dve, lambda ct=ct, k2=k2: nc.vector.tensor_tensor_scan(
                        out=rr[:, k2, :], data0=aa[:, k2, :], data1=tq[:, k2, :], initial=hcar[:, ct:ct + 1], op0=ALU.mult, op1=ALU.add),
                        [R["aa"], R["tq"], R["hcar"]], [R["rr"]])
                    cp(dve, hcar[:, ct:ct + 1], rr[:, k2, 511:512], [R["rr"]], [R["hcar"]])
                    cp(dve, hs[:, ct, :], rr[:, k2, 384:512], [R["rr"]], [R["hs"]])
            for cp2 in range(2):
                yield from rnn_pair(cp2, 512, c == 0, lambda ct, j: ub[:, ct, j:j + 512], lambda k2: uc[:, k2, :], scan_p, ub_r)
            tt(dve, HG_all[:, :, c * 128:(c + 1) * 128], hs[:, :, :], gyA[c % 2][:], ALU.mult, [R["hs"], R["gelu", c % 2]], [R["HG", c]])

        def drive(g1, g2):
            gens = [g for g in (g1, g2) if g is not None]
            while gens:
                for g in list(gens):
                    try:
                        next(g)
                    except StopIteration:
                        gens.remove(g)

        drive(front(0), None)
        for c in range(16):
            if c == 1:
                stop("phase1_c0")
            drive(front(c + 1) if c + 1 < 16 else None, rnn(c))
        S.dma(pool, convT_p[:, :, :], ubA[15 % 3][:, :, 512:515], [R["ub", 15 % 3]], [], final=True)
        S.dma(pool, hT_p[:, :], hcar[:], [R["hcar"]], [], final=True)
        OT_d = dt("OT_d", [64, 8, NTOK], BF16, kind="Internal")
        HG_d = dt("HG_d", [128, 4, NTOK], BF16, kind="Internal")
        S.dma(sp, HG_d[:, :, :], HG_all[:], [R["HG", i] for i in range(NOWN)], [R["HG_d"]])
        stop("phase1")
        barrier()
        pop(es1)

        eso = push()
        OT_all = sb("OT_all", [64, 8, NTOK], BF16, eso)
        es2 = push()
        attn_alloc(es2)
        hctx = attn["ctx"]
        ess = push()
        ckb = sb("ckb", [128, 8, 512], BF16, ess)
        cvb = [sb(f"cvb{s}", [128, 8, 512], BF16, ess) for s in range(2)]
        KTs = [sb(f"KTs{s}", [128, 4, 1088], BF16, ess) for s in range(2)]
        items_all = []
        for s in range(2):
            S.dma(pool, ckb[:], ck[s].rearrange("(t p) n -> p t n", p=128), [], [R["ckb"]])
            S.dma(pool, cvb[s][:], cvd[s].rearrange("(t p) n -> p t n", p=128), [], [R["cvb", s]])
            for hp in range(4):
                pt, pt_r = pts.next()
                for t in range(8):
                    tr(pt[:, t * 128:(t + 1) * 128], ckb[:, t, hp * 128:(hp + 1) * 128], [R["ckb"]], [pt_r])
                cp(dve, KTs[s][:, hp, 0:1024], pt[:, :], [pt_r], [R["KTs", s]])
            cp(pool, KTs[s][:, :, 1024:1088], ktn[:, :, s * 64:(s + 1) * 64], [R["ktn"]], [R["KTs", s]])
        stop("sa1")
        for s in range(2):
            for hp in range(4):
                pair = []
                for h2 in range(2):
                    h = hp * 2 + h2
                    hb = h2 * 64
                    kres = [R["KTs", s], R["cvb", s], R["vn", s]]
                    tiles = [dict(kT=KTs[s][hb:hb + 64, hp, t * 128:(t + 1) * 128], v=cvb[s][:, t, h * 64:(h + 1) * 64], ks=128,
                                  mask=False, valid=None, res=kres) for t in range(8)]
                    newt = dict(kT=KTs[s][hb:hb + 64, hp, 1024:1088], v=vn[s][0:64, h * 64:(h + 1) * 64], ks=64, mask=True, valid=None,
                                res=kres)
                    ctx = hctx[h2]
                    c0 = OWN_S * 128 + s * 64
                    pair.append(make_items(ctx, QT_all[hb:hb + 64, hp, c0:c0 + 64], R["QT", OWN_S], 64,
                                           [[newt], tiles[4:8], tiles[0:4]], OT_all[:, h, c0:c0 + 64], R["OT", OWN_S]))
                for h2 in range(2):
                    S.op(pool, lambda h2=h2: nc.gpsimd.memset(hctx[h2]["S32"][:], 0.0), [], [hctx[h2]["S32r"]])
                run_items(interleave(pair[0], pair[1]))
                stop("sa2")
        stop("sample_attn")
        barrier()
        pop(ess)

        es3 = push()
        KTh = Rot([sb(f"KTh{i}", [128, 8192], BF16, es3) for i in range(1)])
        Vh = Rot([sb(f"Vh{i}", [128, 64, 128], BF16, es3) for i in range(1)])
        dmask2 = sb("dmask2", [128, 2, 128], BF16, es3)
        for k2 in range(2):
            cp(pool, dmask2[:, k2, :], dmask[:, :], [C], [R["dmask2"]])
        DM2 = R["dmask2"]
        S32p = sb("S32p", [128, 2, 512], F32, es3)
        S32r = Res()
        Sbp = [sb(f"Sbp{k}", [128, 2, 512], BF16, es3) for k in range(3)]
        Sbr = [Res() for _ in range(3)]
        ep = Rot([sb(f"ep{i}", [128, 2, 512], F32, es3) for i in range(2)])
        spp = Rot([sb(f"spp{i}", [128, 2, 512], BF16, es3) for i in range(4)])
        wtp = Rot([sb(f"wtp{i}", [128, 2, 512], BF16, es3) for i in range(3)])
        Zp, Zr = bigs[0], [banks.bufs[0][1], banks.bufs[1][1]]
        Wp, Wr = bigs[1], [banks.bufs[2][1], banks.bufs[3][1]]
        Ob = [banks.bufs[4][0], banks.bufs[5][0]]
        Or = [banks.bufs[4][1], banks.bufs[5][1]]
        Z3 = Zp[:, :].rearrange("p (a n) -> p a n", a=2)
        W3 = Wp[:, :].rearrange("p (a n) -> p a n", a=2)

        def make_units(kth, kth_r, vh, vh_r, hp, M):
            q_res = [R["QT", m] for m in range(4 * M, 4 * M + 4)]
            o_res = [R["OT", m] for m in range(4 * M, 4 * M + 4)]
            units = []
            top = 16 * M + 15
            for k, ft in enumerate(range(top, -1, -1)):
                m_min = max(4 * M, -(-(ft - 3) // 4))
                c_lo = (m_min - 4 * M) * 128
                diag = None
                if ft % 4 == 3 and (ft - 3) // 4 >= 4 * M:
                    d0 = ((ft - 3) // 4 - 4 * M) * 128
                    diag = (d0, d0 + 128)
                units.append(dict(kT=[kth[hb:hb + 64, ft * 128:(ft + 1) * 128] for hb in (0, 64)],
                                  v=[vh[:, ft, hb:hb + 64] for hb in (0, 64)],
                                  qT=[QT_all[hb:hb + 64, hp, M * 512:(M + 1) * 512] for hb in (0, 64)], q_res=q_res,
                                  res=[kth_r[ft // 16], vh_r[ft // 16]], c_lo=c_lo, diag=diag, valid=None, k=k,
                                  first=(k == 0), last=(ft == 0),
                                  out_ap=[OT_all[:, hp * 2 + h2, M * 512:(M + 1) * 512] for h2 in range(2)], out_res=o_res))
            return units

        def p1a(u):
            c0 = u["c_lo"]
            for h2 in range(2):
                mm(Zp[:, h2 * 512 + c0:(h2 + 1) * 512], u["kT"][h2], u["qT"][h2][:, c0:512], True, True, u["res"] + u["q_res"],
                   [Zr[h2]], skip=True)
                if u["diag"]:
                    d0, d1 = u["diag"]
                    mm(Zp[:, h2 * 512 + d0:h2 * 512 + d1], ident[:, :], mneg[:, :], False, True, [C], [Zr[h2]], skip=True)
            e, e_r = ep.next()
            u["e"], u["e_r"] = e, e_r
            act_fn(e[:, :, c0:512], Z3[:, :, c0:512], AF.Exp, Zr, [e_r])

        def p1b(u):
            c0 = u["c_lo"]
            e, e_r = u["e"], u["e_r"]
            spt, sp_r = spp.next()
            u["sp"], u["sp_r"] = spt, sp_r
            act_fn(spt[:, :, c0:512], e[:, :, c0:512], AF.Ln, [e_r, C], [sp_r], bias=c_one[:, 0:1])
            if not u["last"]:
                tt(dve, S32p[:, :, c0:512], S32p[:, :, c0:512], spt[:, :, c0:512], ALU.add, [sp_r, S32r], [S32r])

        def pcast(u):
            if not u["last"]:
                c0, k = u["c_lo"], u["k"]
                cp(dve, Sbp[k % 3][:, :, c0:512], S32p[:, :, c0:512], [S32r], [Sbr[k % 3]])

        def p2a(u):
            c0, k = u["c_lo"], u["k"]
            spt, sp_r = u["sp"], u["sp_r"]
            carry = not u["first"]
            for h2 in range(2):
                o = Wp[:, h2 * 512 + c0:(h2 + 1) * 512]
                mm(o, u["kT"][h2], u["qT"][h2][:, c0:512], True, False, u["res"] + u["q_res"], [Wr[h2]], skip=True)
                if u["diag"]:
                    d0, d1 = u["diag"]
                    mm(Wp[:, h2 * 512 + d0:h2 * 512 + d1], ident[:, :], mneg[:, :], False, False, [C], [Wr[h2]], skip=True)
                mm(o, ntri[:, :], spt[:, h2, c0:512], False, not carry, [sp_r, C], [Wr[h2]], skip=True)
                if carry:
                    mm(o, nones[:, :], Sbp[(k - 1) % 3][:, h2, c0:512], False, True, [Sbr[(k - 1) % 3], C], [Wr[h2]], skip=True)

        def p2b(u):
            c0 = u["c_lo"]
            wt, wt_r = wtp.next()
            u["w"], u["w_r"] = wt, wt_r
            act_fn(wt[:, :, c0:512], W3[:, :, c0:512], AF.Exp, Wr, [wt_r])

        def p3(u):
            c0 = u["c_lo"]
            for h2 in range(2):
                mm(Ob[h2][0:64, c0:512], u["v"][h2], u["w"][:, h2, c0:512], u["first"], u["last"], u["res"] + [u["w_r"]], [Or[h2]],
                   skip=True)
                if u["last"]:
                    cp(dve, u["out_ap"][h2], Ob[h2][0:64, 0:512], [Or[h2]], u["out_res"])

        def run_units(units):
            n = len(units)
            for t in range(n + 3):
                if t < n:
                    p1a(units[t])
                if 0 <= t - 2 < n:
                    p2a(units[t - 2])
                if 0 <= t - 1 < n:
                    pcast(units[t - 1])
                if 0 <= t - 2 < n:
                    p2b(units[t - 2])
                if t < n:
                    p1b(units[t])
                if 0 <= t - 3 < n:
                    p3(units[t - 3])

        for hp in range(4):
            kth, kth_r = KTh.next()
            vh, vh_r = Vh.next()
            kq = [R["kthq", q4] for q4 in range(4)]
            vq = [R["vhq", q4] for q4 in range(4)]
            for q4 in range(4):
                S.dma(sp, kth[:, q4 * 2048:(q4 + 1) * 2048], KT_d[hp, :, q4 * 2048:(q4 + 1) * 2048], [R["KT_d"]], [kq[q4]])
                S.dma(sp, vh[:, q4 * 16:(q4 + 1) * 16, :],
                      V_d[q4 * 16:(q4 + 1) * 16, :, hp * 128:(hp + 1) * 128].rearrange("t p n -> p t n"), [R["V_d"]], [vq[q4]])
            kth_r, vh_r = kq, vq
            for M in range(4):
                S.op(pool, lambda: nc.gpsimd.memset(S32p[:], 0.0), [], [S32r])
                for k3 in range(3):
                    S.op(pool, lambda k3=k3: nc.gpsimd.memset(Sbp[k3][:], 0.0), [], [Sbr[k3]])
                run_units(make_units(kth, kth_r, vh, vh_r, hp, M))
                if hp == 0 and M == 0:
                    stop("phase2_m1")
            S.dma(sp, OT_d[:, 2 * hp:2 * hp + 2, :], OT_all[:, 2 * hp:2 * hp + 2, :], [R["OT", i] for i in range(NOWN)],
                  [R["OT_d", hp]])
            if hp == 0:
                stop("phase2_hp0")
        barrier()
        pop(es3)
        pop(es2)
        pop(eso)
        pop(esq)

        stop("phase2")
        X1_d = dt("X1_d", [NOWN, 128, D], F32, kind="Internal")
        NTH = 9 * 128
        es5 = push()
        xnHs = {0: sb("xnH", [128, 8, NTH], BF16, es5), 9: sb("xnHB", [128, 8, 8 * 128], BF16, es5)}
        xnH_rs = {0: RD(), 9: RD()}

        def front_tile(t0, tl):
            xt, xt_r = xts.next()
            S.dma(sp, xt[:], own_rows(t0 + tl), [], [xt_r])
            xb, xb_r = xsb.next()
            norm_stats(xt[:], xt_r, GPRE, xb[:], xb_r)
            transpose_to(xb, xb_r, xnHs[t0][:, :, tl * 128:(tl + 1) * 128], xnH_rs[t0][tl])

        def own_rows(i):
            return xsm[:, :] if i == 16 else xf[(4 * i + 3) * 128:(4 * i + 4) * 128, :]

        wo = sb("wo", [128, 8, D], BF16, es5)
        wga = Rot([sb(f"wga{i}", [128, 8, 128], BF16, es5) for i in range(2)])
        wgb = Rot([sb(f"wgb{i}", [128, 8, 128], BF16, es5) for i in range(2)])
        wa = Rot([sb(f"wa{i}", [64, 8, 128], BF16, es5) for i in range(2)])
        wbt = Rot([sb(f"wbt{i}", [128, 4, 128], BF16, es5) for i in range(2)])
        wo_r = Res()
        S.dma(pool, wo[:], w_o.rearrange("(kc p) n -> p kc n", p=128), [], [wo_r])

        def load_mt(mt):
            a_, a_r = wga.next()
            b_, b_r = wgb.next()
            wa_, wa_r = wa.next()
            wb_2, wb_r = wbt.next()
            S.dma(pool, a_[:], w_in[:, 2560 + mt * 128:2560 + (mt + 1) * 128].rearrange("(kc p) n -> p kc n", p=128), [], [a_r])
            S.dma(pool, b_[:], w_in[:, 3584 + mt * 128:3584 + (mt + 1) * 128].rearrange("(kc p) n -> p kc n", p=128), [], [b_r])
            S.dma(pool, wa_[:], w_a[:, mt * 128:(mt + 1) * 128].rearrange("(h p) n -> p h n", p=64), [], [wa_r])
            S.dma(pool, wb_2[:], w_b[:, mt * 128:(mt + 1) * 128].rearrange("(c p) n -> p c n", p=128), [], [wb_r])
            return (a_, a_r, b_, b_r, wa_, wa_r, wb_2, wb_r)

        def prefetch_3a(t0, NT):
            return load_mt(0)
        pre3a = {}
        wu = Rot([sb(f"wu{i}", [128, 8, 512], BF16, es5) for i in range(2)])
        wd = Rot([sb(f"wd{i}", [128, 4, D], BF16, es5) for i in range(2)])

        def load_fg(fg):
            wu_, wu_r = wu.next()
            wd_, wd_r = wd.next()
            S.dma(pool, wu_[:], w_up[:, fg * 512:(fg + 1) * 512].rearrange("(kc p) n -> p kc n", p=128), [], [wu_r])
            S.dma(pool, wd_[:], w_dn[fg * 512:(fg + 1) * 512, :].rearrange("(f p) n -> p f n", p=128), [], [wd_r])
            return (wu_, wu_r, wd_, wd_r)

        for (t0, ntl) in ((0, 9), (9, 8)):
            NT = ntl * 128
            chunks = [(c0, min(512, NT - c0)) for c0 in range(0, NT, 512)]
            es6 = push()
            otc = sb("otc", [64, 8, NTH], BF16, es6)
            hgc = sb("hgc", [128, 4, NTH], BF16, es6)
            otc_r, hgc_r = Res(), Res()
            S.dma(sp, otc[:, :, 0:NT], OT_d[:, :, t0 * 128:t0 * 128 + NT], [R["OT_d", hp_] for hp_ in range(4)], [otc_r])
            S.dma(sp, hgc[:, :, 0:NT], HG_d[:, :, t0 * 128:t0 * 128 + NT], [R["HG_d"]], [hgc_r])
            mTh = sb("mTh", [128, 8, NTH], BF16, es6)
            sga = Rot([sb(f"sga{i}", [128, 512], F32, es6) for i in range(2)])
            sgb = Rot([sb(f"sgb{i}", [128, 512], F32, es6) for i in range(2)])
            mixs = Rot([sb(f"mixs{i}", [128, D], F32, es6) for i in range(2)])
            x1s = Rot([sb(f"x1s{i}", [128, D], F32, es6) for i in range(2)])
            mT_r = RD()
            nxt = pre3a.pop(t0) if t0 in pre3a else prefetch_3a(t0, NT)
            xnH, xnH_r = xnHs[t0], xnH_rs[t0]
            if t0 == 0:
                for tl in range(ntl):
                    front_tile(0, tl)
            for mt in range(8):
                (a_, a_r, b_, b_r, wa_, wa_r, wb_2, wb_r) = nxt
                if mt < 7:
                    nxt = load_mt(mt + 1)
                for (c0, n) in chunks:
                    xr = [xnH_r[tl] for tl in range(c0 // 128, (c0 + n) // 128)]
                    bkA, bkA_r = banks.next()
                    for kc in range(8):
                        mm(bkA[:, 0:n], a_[:, kc, :], xnH[:, kc, c0:c0 + n], kc == 0, kc == 7, [a_r] + xr, [bkA_r])
                    sa, sa_r = sga.next()
                    act_fn(sa[:, 0:n], bkA[:, 0:n], AF.Sigmoid, [bkA_r], [sa_r])
                    bkB, bkB_r = banks.next()
                    for kc in range(8):
                        mm(bkB[:, 0:n], b_[:, kc, :], xnH[:, kc, c0:c0 + n], kc == 0, kc == 7, [b_r] + xr, [bkB_r])
                    sb_, sb_r = sgb.next()
                    act_fn(sb_[:, 0:n], bkB[:, 0:n], AF.Sigmoid, [bkB_r], [sb_r])
                    bkY, bkY_r = banks.next()
                    for h in range(8):
                        mm(bkY[:, 0:n], wa_[0:64, h, :], otc[0:64, h, c0:c0 + n], h == 0, h == 7, [wa_r, otc_r], [bkY_r])
                    tt(dve, sa[:, 0:n], sa[:, 0:n], bkY[:, 0:n], ALU.mult, [sa_r, bkY_r], [sa_r])
                    bkZ, bkZ_r = banks.next()
                    for ct in range(4):
                        mm(bkZ[:, 0:n], wb_2[:, ct, :], hgc[:, ct, c0:c0 + n], ct == 0, ct == 3, [wb_r, hgc_r], [bkZ_r])
                    tt(dve, sb_[:, 0:n], sb_[:, 0:n], bkZ[:, 0:n], ALU.mult, [sb_r, bkZ_r], [sb_r])
                    tt(dve, mTh[:, mt, c0:c0 + n], sa[:, 0:n], sb_[:, 0:n], ALU.add, [sa_r, sb_r], [mT_r[c0]])
            allmT = [mT_r[c0] for (c0, n) in chunks]
            nxt_fg = load_fg(0)
            for tl in range(ntl):
                i = t0 + tl
                mx, mx_r = mixs.next()
                for half in range(2):
                    bk, bk_r = banks.next()
                    for kc in range(8):
                        mm(bk[:, :], mTh[:, kc, tl * 128:(tl + 1) * 128], wo[:, kc, half * 512:(half + 1) * 512], kc == 0, kc == 7,
                           allmT + [wo_r], [bk_r])
                    cp(act, mx[:, half * 512:(half + 1) * 512], bk[:, :], [bk_r], [mx_r])
                st, st_r = norm_stats(mx[:], mx_r, GPOST, None, None)
                stt(mx[:], mx[:], st[:, 2:3], gts[:, GPOST, :], ALU.mult, ALU.mult, [mx_r, st_r, C], [mx_r])
                xt, xt_r = xts.next()
                S.dma(sp, xt[:], own_rows(i), [], [xt_r])
                x1, x1_r = x1s.next()
                tt(dve, x1[:], mx[:], xt[:], ALU.add, [mx_r, xt_r], [x1_r])
                S.dma(sp, X1_d[i, :, :], x1[:], [x1_r], [R["X1_d", i]])
                xb, xb_r = xsb.next()
                norm_stats(x1[:], x1_r, GPREF, xb[:], xb_r)
                transpose_to(xb, xb_r, xnH[:, :, tl * 128:(tl + 1) * 128], xnH_r[tl])
            barrier()
            pop(es6)
            es7 = push()
            facc = sb("facc", [128, 9, D], F32, es7)
            rl = Rot([sb(f"rl{i}", [128, 512], F32, es7) for i in range(2)])
            hT = Rot([sb(f"hT{i}", [128, 4, 512], BF16, es7) for i in range(2)])
            facc_r = RD()

            nxt = nxt_fg
            if t0 == 0:
                pre3a[9] = prefetch_3a(9, 8 * 128)
            for fg in range(8):
                (wu_, wu_r, wd_, wd_r) = nxt
                if fg < 7:
                    nxt = load_fg(fg + 1)
                for (c0, n) in chunks:
                    xr = [xnH_r[tl] for tl in range(c0 // 128, (c0 + n) // 128)]
                    h_, h_r = hT.next()
                    for f4 in range(4):
                        bk, bk_r = banks.next()
                        for kc in range(8):
                            mm(bk[:, 0:n], wu_[:, kc, f4 * 128:(f4 + 1) * 128], xnH[:, kc, c0:c0 + n], kc == 0, kc == 7, [wu_r] + xr, [bk_r])
                        r_, r_r = rl.next()
                        act_fn(r_[:, 0:n], bk[:, 0:n], AF.Relu, [bk_r], [r_r])
                        tt(pool, h_[:, f4, 0:n], r_[:, 0:n], r_[:, 0:n], ALU.mult, [r_r], [h_r])
                    for tq_ in range(n // 128):
                        tl = c0 // 128 + tq_
                        for half in range(2):
                            bk, bk_r = banks.next()
                            for f4 in range(4):
                                mm(bk[:, :], h_[:, f4, tq_ * 128:(tq_ + 1) * 128], wd_[:, f4, half * 512:(half + 1) * 512], f4 == 0, f4 == 3,
                                   [h_r, wd_r], [bk_r])
                            dst = facc[:, tl, half * 512:(half + 1) * 512]
                            if fg == 0:
                                cp(act, dst, bk[:, :], [bk_r], [facc_r[tl]])
                            else:
                                tt(dve, dst, dst, bk[:, :], ALU.add, [bk_r, facc_r[tl]], [facc_r[tl]])
                if t0 == 0:
                    front_tile(9, fg)
            for tl in range(ntl):
                i = t0 + tl
                st, st_r = norm_stats(facc[:, tl, :], facc_r[tl], GPOSTF, None, None)
                stt(facc[:, tl, :], facc[:, tl, :], st[:, 2:3], gts[:, GPOSTF, :], ALU.mult, ALU.mult, [facc_r[tl], st_r, C], [facc_r[tl]])
                xt, xt_r = xts.next()
                S.dma(sp, xt[:], X1_d[i, :, :], [R["X1_d", i]], [xt_r])
                tt(dve, xt[:], xt[:], facc[:, tl, :], ALU.add, [xt_r, facc_r[tl]], [xt_r])
                dst = y_s[:, :] if i == 16 else y_own[i * 128:(i + 1) * 128, :]
                S.dma(sp, dst, xt[:], [xt_r], [xt_r], final=True)
            if t0 == 0:
                stop("phase3_c0")
            barrier()
            pop(es7)
        pop(es5)


_NC_CACHE = {}


def kernel(x_prompt, x_sample, cache_k, cache_v, state_conv, state_h, w_in, g_pre_mix, w_conv, b_conv, w_r, b_r, w_i, b_i,
           lam, w_a_out, w_b_out, w_o, g_post_mix, g_pre_ffn, w_up, w_down, g_post_ffn):
    if "nc" not in _NC_CACHE:
        _NC_CACHE["nc"] = build()
    nc = _NC_CACHE["nc"]
    in_maps = prep_inputs(x_prompt, x_sample, cache_k, cache_v, state_conv, state_h, w_in, g_pre_mix, w_conv, b_conv, w_r, b_r,
                          w_i, b_i, lam, w_a_out, w_b_out, w_o, g_post_mix, g_pre_ffn, w_up, w_down, g_post_ffn)
    res = run_bass_kernel_spmd(nc, in_maps, core_ids=list(range(8)))
    return assemble(res.results)


def prep_inputs(x_prompt, x_sample, cache_k, cache_v, state_conv, state_h, w_in, g_pre_mix, w_conv, b_conv, w_r, b_r, w_i, b_i,
                lam, w_a_out, w_b_out, w_o, g_post_mix, g_pre_ffn, w_up, w_down, g_post_ffn):
    f = lambda a: np.ascontiguousarray(np.asarray(a, dtype=np.float32))
    x_prompt, x_sample = f(x_prompt), f(x_sample)
    cache_k, cache_v, state_conv, state_h = f(cache_k), f(cache_v), f(state_conv), f(state_h)

    def chT(v):
        return np.ascontiguousarray(f(v).reshape(4, 128).T)

    gvv = np.stack([np.broadcast_to(f(g)[0][None, :], (128, D)) for g in (g_pre_mix, g_post_mix, g_pre_ffn, g_post_ffn)])
    gvv = np.ascontiguousarray(gvv)
    wc = f(w_conv)[0]
    cvec = np.stack([chT(wc[0]), chT(wc[1]), chT(wc[2]), chT(wc[3]), chT(f(b_conv)[0]), chT(f(b_r)[0]), chT(f(b_i)[0]),
                     chT(f(lam)[0])], axis=-1)
    cvec = np.ascontiguousarray(cvec)

    def bd(w):
        w = f(w)[0]
        out = np.zeros((4, 128, 128), np.float32)
        for ct in range(4):
            out[ct, 0:64, 0:64] = w[2 * ct]
            out[ct, 64:128, 64:128] = w[2 * ct + 1]
        return out

    common = dict(gv=gvv, cvec=cvec, wrbd=bd(w_r), wibd=bd(w_i), w_in=f(w_in)[0], w_a=f(w_a_out)[0], w_b=f(w_b_out)[0],
                  w_o=f(w_o)[0], w_up=f(w_up)[0], w_dn=f(w_down)[0])
    in_maps = []
    for c in range(8):
        b, j = c // 4, c % 4
        pad = 384 - 128 * j
        xfr = np.zeros((8192, D), np.float32)
        xfr[pad:] = x_prompt[b, 0:8192 - pad]
        s0 = 2 * c
        valid = np.zeros((128, 4), np.float32)
        for i in range(4):
            valid[:, i] = 1.0 if i >= 3 - j else 0.0
        sc = state_conv[0, s0:s0 + 2]
        scT = np.ascontiguousarray(sc.reshape(2, 3, 4, 128).transpose(3, 2, 0, 1))
        shT = np.ascontiguousarray(state_h[0, s0:s0 + 2].reshape(2, 4, 128).transpose(2, 1, 0))
        m = dict(common)
        m.update(xf=xfr, xsm=np.ascontiguousarray(x_sample[s0:s0 + 2].reshape(128, D)),
                 ck=np.ascontiguousarray(cache_k[0, s0:s0 + 2].reshape(2, 1024, 512)),
                 cv=np.ascontiguousarray(cache_v[0, s0:s0 + 2].reshape(2, 1024, 512)),
                 sconvT=scT, shT=shT, valid=valid)
        in_maps.append(m)
    return in_maps


def assemble(rs):
    y_prompt = np.zeros((2, 8192, D), np.float32)
    k_prompt = np.zeros((1, 2, 8192, 8, 64), np.float32)
    v_prompt = np.zeros((1, 2, 8192, 8, 64), np.float32)
    conv_prompt = np.zeros((1, 2, 3, 512), np.float32)
    h_prompt = np.zeros((1, 2, 512), np.float32)
    y_sample = np.zeros((16, 64, D), np.float32)
    k_sample = np.zeros((1, 16, 64, 8, 64), np.float32)
    v_sample = np.zeros((1, 16, 64, 8, 64), np.float32)
    conv_sample = np.zeros((1, 16, 3, 512), np.float32)
    h_sample = np.zeros((1, 16, 512), np.float32)
    for c in range(8):
        b, j = c // 4, c % 4
        r = rs[c]
        for m in range(16):
            g0 = (4 * m + j) * 128
            y_prompt[b, g0:g0 + 128] = r["y_own"][m * 128:(m + 1) * 128]
            k_prompt[0, b, g0:g0 + 128] = r["k_own"][m * 128:(m + 1) * 128].reshape(128, 8, 64)
            v_prompt[0, b, g0:g0 + 128] = r["v_own"][m * 128:(m + 1) * 128].reshape(128, 8, 64)
        if j == 3:
            conv_prompt[0, b] = r["convT_p"].transpose(2, 1, 0).reshape(3, 512)
            h_prompt[0, b] = r["hT_p"].T.reshape(512)
        for s in range(2):
            sid = 2 * c + s
            y_sample[sid] = r["y_s"][s * 64:(s + 1) * 64]
            k_sample[0, sid] = r["k_s"][s * 64:(s + 1) * 64].reshape(64, 8, 64)
            v_sample[0, sid] = r["v_s"][s * 64:(s + 1) * 64].reshape(64, 8, 64)
            conv_sample[0, sid] = r["convT_s"][:, :, s, :].transpose(2, 1, 0).reshape(3, 512)
            h_sample[0, sid] = r["hT_s"][:, :, s].T.reshape(512)
    return (y_prompt, y_sample, k_prompt, v_prompt, conv_prompt, h_prompt, k_sample, v_sample, conv_sample, h_sample)
```

```python
import contextlib
import os
import numpy as np
import concourse.bass as bass
import concourse.mybir as mybir
from concourse.bass_utils import run_bass_kernel_spmd

F32 = mybir.dt.float32
BF16 = mybir.dt.bfloat16
AF = mybir.ActivationFunctionType
ALU = mybir.AluOpType
AX = mybir.AxisListType

D = 1024
NOWN = 17
NTOK = NOWN * 128
EPS = 1e-6
GELU_C = 1.5957691216057308


class _Stop(Exception):
    pass


STOP = os.environ.get("MK_STOP", "")


def stop(tag):
    if STOP == tag:
        raise _Stop()


class Res:
    __slots__ = ("w", "r", "excl")

    def __init__(self):
        self.w = None
        self.r = {}
        self.excl = False


class RD(dict):
    def __missing__(self, k):
        v = Res()
        self[k] = v
        return v


class Q:
    def __init__(self, name, eng, sem):
        self.name = name
        self.eng = eng
        self.sem = sem
        self.cnt = 0
        self.seen = {}
        self.dsems = []
        self.dcnt = []
        self.di = 0

    def wait(self, tk):
        sem, val, key, qn = tk
        if qn == "pe" and self.name == "pe":
            return
        if self.seen.get(key, 0) >= val:
            return
        self.eng.wait_ge(sem, val)
        self.seen[key] = val


class Sched:
    def __init__(self, nc, es):
        self.nc = nc
        mk = lambda n: es.enter_context(nc.semaphore(n))
        self.pe = Q("pe", nc.tensor, mk("s_pe"))
        self.act = Q("act", nc.scalar, mk("s_act"))
        self.dve = Q("dve", nc.vector, mk("s_dve"))
        self.pool = Q("pool", nc.gpsimd, mk("s_pool"))
        self.sp = Q("sp", nc.sync, None)
        for q, pre, n in ((self.sp, "dsp", 32), (self.pool, "dpl", 32)):
            q.dsems = [mk(f"{pre}{i}") for i in range(n)]
            q.dcnt = [0] * n
        self.out_tk = []

    def _deps(self, q, reads, writes):
        for r in reads:
            if r.w is not None:
                q.wait(r.w)
        for w in writes:
            if w.w is not None:
                q.wait(w.w)
            for tk in w.r.values():
                q.wait(tk)

    def _mark(self, tk, reads, writes):
        for r in reads:
            r.r[tk[2]] = tk
        for w in writes:
            w.w = tk
            w.r = {}

    def op(self, q, fn, reads=(), writes=()):
        if any(r.excl for r in reads):
            writes = list(writes) + [r for r in reads if r.excl]
            reads = [r for r in reads if not r.excl]
        self._deps(q, reads, writes)
        ins = fn()
        q.cnt += 1
        ins.then_inc(q.sem, 1)
        self._mark((q.sem, q.cnt, q.name, q.name), reads, writes)

    def dma(self, q, out, in_, reads=(), writes=(), final=False):
        self._deps(q, reads, writes)
        i = q.di
        q.di = (i + 1) % len(q.dsems)
        key = f"{q.name}d{i}"
        if q.dcnt[i] > 0:
            q.wait((q.dsems[i], q.dcnt[i], key, "dma"))
        q.eng.dma_start(out=out, in_=in_).then_inc(q.dsems[i], 16)
        q.dcnt[i] += 16
        tk = (q.dsems[i], q.dcnt[i], key, "dma")
        self._mark(tk, reads, writes)
        if final:
            self.out_tk.append(tk)

    def finish(self):
        for tk in self.out_tk:
            self.sp.wait(tk)


class Rot:
    def __init__(self, bufs):
        self.bufs = [(b, Res()) for b in bufs]
        self.i = 0

    @classmethod
    def of(cls, pairs):
        r = cls([])
        r.bufs = list(pairs)
        return r

    def next(self):
        b = self.bufs[self.i]
        self.i = (self.i + 1) % len(self.bufs)
        return b


def build():
    nc = bass.Bass("TRN2", target_bir_lowering=False)

    def dt(name, shape, dtype=F32, kind="ExternalInput"):
        return nc.dram_tensor(name, shape, dtype, kind=kind).ap()

    xf = dt("xf", [8192, D])
    xsm = dt("xsm", [128, D])
    ck = dt("ck", [2, 1024, 512])
    cvd = dt("cv", [2, 1024, 512])
    sconvT = dt("sconvT", [128, 4, 2, 3])
    shT = dt("shT", [128, 4, 2])
    valid_d = dt("valid", [128, 4])
    gv = dt("gv", [4, 128, D])
    cvec_d = dt("cvec", [128, 4, 8])
    wrbd = dt("wrbd", [4, 128, 128])
    wibd = dt("wibd", [4, 128, 128])
    w_in = dt("w_in", [D, 4608])
    w_a = dt("w_a", [512, D])
    w_b = dt("w_b", [512, D])
    w_o = dt("w_o", [D, D])
    w_up = dt("w_up", [D, 4096])
    w_dn = dt("w_dn", [4096, D])
    O = "ExternalOutput"
    y_own = dt("y_own", [2048, D], kind=O)
    y_s = dt("y_s", [128, D], kind=O)
    k_own = dt("k_own", [2048, 512], kind=O)
    v_own = dt("v_own", [2048, 512], kind=O)
    k_s = dt("k_s", [128, 512], kind=O)
    v_s = dt("v_s", [128, 512], kind=O)
    convT_p = dt("convT_p", [128, 4, 3], kind=O)
    hT_p = dt("hT_p", [128, 4], kind=O)
    convT_s = dt("convT_s", [128, 4, 2, 3], kind=O)
    hT_s = dt("hT_s", [128, 4, 2], kind=O)
    KT_d = dt("KT_d", [4, 128, 8192], BF16, kind="Internal")
    V_d = dt("V_d", [64, 128, 512], BF16, kind="Internal")


    es = contextlib.ExitStack()
    with es:
        S = Sched(nc, es)
        nest = []
        try:
            _body(nc, es, S, dict(locals()))
        except _Stop:
            for st in reversed(nest):
                st.__exit__(None, None, None)
        S.finish()
    return nc


def _body(nc, es, S, L):
    if True:
        globals_ = L
        (xf, xsm, ck, cvd, sconvT, shT, valid_d, gv, cvec_d, wrbd, wibd, w_in, w_a, w_b, w_o, w_up, w_dn, y_own, y_s, k_own, v_own,
         k_s, v_s, convT_p, hT_p, convT_s, hT_s, KT_d, V_d, dt) = [L[k] for k in (
            "xf", "xsm", "ck", "cvd", "sconvT", "shT", "valid_d", "gv", "cvec_d", "wrbd", "wibd", "w_in", "w_a", "w_b", "w_o", "w_up",
            "w_dn", "y_own", "y_s", "k_own", "v_own", "k_s", "v_s", "convT_p", "hT_p", "convT_s", "hT_s", "KT_d", "V_d", "dt")]
        pe, act, dve, pool, sp = S.pe, S.act, S.dve, S.pool, S.sp
        nest = L["nest"]

        def push():
            st = contextlib.ExitStack()
            st.__enter__()
            nest.append(st)
            return st

        def pop(st):
            assert nest[-1] is st
            nest.pop()
            st.__exit__(None, None, None)

        used_names = {}

        def sb(name, shape, dtype=F32, stack=es):
            k = used_names.get(name, 0)
            used_names[name] = k + 1
            nm = "sb_" + name + (f"_v{k}" if k else "")
            return stack.enter_context(nc.sbuf_tensor(nm, shape, dtype))

        bigs = [es.enter_context(nc.psum_tensor(f"pb{i}", [128, 1024], F32)) for i in range(3)]
        banks = Rot([bigs[i // 2][:, (i % 2) * 512:(i % 2 + 1) * 512] for i in range(6)])
        pts = Rot([es.enter_context(nc.psum_tensor(f"pt{i}", [128, 1024], BF16)) for i in range(2)])
        for _, r_ in banks.bufs + pts.bufs:
            r_.excl = True

        identf = sb("identf", [128, 128])
        ident = sb("ident", [128, 128], BF16)
        ntri = sb("ntri", [128, 128], BF16)
        nones = sb("nones", [128, 128], BF16)
        dmask = sb("dmask", [128, 128], BF16)
        tmpc = sb("tmpc", [128, 128])
        mneg = sb("mneg", [128, 128], BF16)
        c_one = sb("c_one", [128, 1])
        c_eps = sb("c_eps", [128, 1])
        validt = sb("validt", [128, 4])
        gts = sb("gts", [128, 4, D])
        cvec = sb("cvec", [128, 4, 8])
        coef = sb("coef", [128, 4])
        coef2 = sb("coef2", [128, 4])
        ctmp = sb("ctmp", [128, 4])
        wr = sb("wr", [128, 4, 128], BF16)
        wi = sb("wi", [128, 4, 128], BF16)
        ktn = sb("ktn", [128, 4, 128], BF16)
        vn = [sb(f"vn{s}", [64, 512], BF16) for s in range(2)]
        R = RD()

        def act_fn(out, in_, func, reads, writes, bias=None, scale=None):
            kw = {}
            if bias is not None:
                kw["bias"] = bias
            if scale is not None:
                kw["scale"] = scale
            S.op(act, lambda: nc.scalar.activation(out=out, in_=in_, func=func, **kw), reads, writes)

        def tt(q, out, in0, in1, op, reads, writes):
            S.op(q, lambda: q.eng.tensor_tensor(out=out, in0=in0, in1=in1, op=op), reads, writes)

        def ts(q, out, in0, s1, s2, op0, op1, reads, writes):
            if op1 is None:
                S.op(q, lambda: q.eng.tensor_scalar(out=out, in0=in0, scalar1=s1, scalar2=None, op0=op0), reads, writes)
            else:
                S.op(q, lambda: q.eng.tensor_scalar(out=out, in0=in0, scalar1=s1, scalar2=s2, op0=op0, op1=op1), reads, writes)

        def stt(out, in0, scalar, in1, op0, op1, reads, writes, accum_out=None):
            kw = {} if accum_out is None else {"accum_out": accum_out}
            S.op(dve, lambda: nc.vector.scalar_tensor_tensor(out=out, in0=in0, scalar=scalar, in1=in1, op0=op0, op1=op1, **kw),
                 reads, writes)

        def cp(q, out, in_, reads, writes):
            if q is act:
                S.op(act, lambda: nc.scalar.copy(out=out, in_=in_), reads, writes)
            else:
                S.op(q, lambda: q.eng.tensor_copy(out=out, in_=in_), reads, writes)

        def mm(out, lhsT, rhs, start, stop, reads, writes, skip=False):
            S.op(pe, lambda: nc.tensor.matmul(out, lhsT=lhsT, rhs=rhs, start=start, stop=stop, skip_group_check=skip), reads, writes)

        def tr(out, in_, reads, writes):
            S.op(pe, lambda: nc.tensor.transpose(out=out, in_=in_, identity=ident[:]), reads + [R["const"]], writes)

        C = R["const"]
        S.op(pool, lambda: nc.gpsimd.memset(identf[:], 0.0), [], [C])
        S.op(pool, lambda: nc.gpsimd.affine_select(out=identf[:], in_=identf[:], pattern=[[-1, 128]], compare_op=ALU.not_equal,
                                                   fill=1.0, base=0, channel_multiplier=1), [], [C])
        cp(pool, ident[:], identf[:], [], [C])
        S.op(pool, lambda: nc.gpsimd.memset(tmpc[:], -1.0), [], [C])
        S.op(pool, lambda: nc.gpsimd.affine_select(out=tmpc[:], in_=tmpc[:], pattern=[[-1, 128]], compare_op=ALU.is_ge,
                                                   fill=0.0, base=0, channel_multiplier=1), [], [C])
        cp(pool, ntri[:], tmpc[:], [], [C])
        S.op(pool, lambda: nc.gpsimd.memset(nones[:], -1.0), [], [C])
        S.op(pool, lambda: nc.gpsimd.memset(tmpc[:], 1.0), [], [C])
        S.op(pool, lambda: nc.gpsimd.affine_select(out=tmpc[:], in_=tmpc[:], pattern=[[1, 128]], compare_op=ALU.is_gt,
                                                   fill=0.0, base=0, channel_multiplier=-1), [], [C])
        cp(pool, dmask[:], tmpc[:], [], [C])
        S.op(pool, lambda: nc.gpsimd.memset(tmpc[:], 0.0), [], [C])
        S.op(pool, lambda: nc.gpsimd.affine_select(out=tmpc[:], in_=tmpc[:], pattern=[[1, 128]], compare_op=ALU.is_gt,
                                                   fill=-30000.0, base=0, channel_multiplier=-1), [], [C])
        cp(pool, mneg[:], tmpc[:], [], [C])
        S.op(pool, lambda: nc.gpsimd.memset(c_one[:], 1.0), [], [C])
        S.op(pool, lambda: nc.gpsimd.memset(c_eps[:], EPS), [], [C])
        S.dma(sp, validt[:], valid_d[:, :], [], [C])
        S.dma(sp, cvec[:], cvec_d[:, :, :], [], [C])
        for i in range(4):
            S.dma(sp, gts[:, i, :], gv[i, :, :], [], [C])
        S.dma(pool, wr[:], wrbd.rearrange("c p n -> p c n"), [], [C])
        S.dma(pool, wi[:], wibd.rearrange("c p n -> p c n"), [], [C])
        act_fn(ctmp[:], cvec[:, :, 7], AF.Exp, [C], [C], scale=-1.0)
        act_fn(ctmp[:], ctmp[:], AF.Ln, [C], [C], bias=c_one[:, 0:1])
        ts(dve, coef[:], ctmp[:], -8.0, None, ALU.mult, None, [C], [C])
        ts(dve, coef2[:], ctmp[:], -16.0, None, ALU.mult, None, [C], [C])
        GPRE, GPOST, GPREF, GPOSTF = 0, 1, 2, 3

        stop("setup")
        xts = Rot([sb(f"xt{i}", [128, D]) for i in range(2)])
        junk = sb("junk", [128, D], BF16)
        xsb = Rot([sb(f"xsb{i}", [128, D], BF16) for i in range(2)])
        stat = Rot([sb(f"stat{i}", [128, 4]) for i in range(4)])

        def norm_stats(src, src_res, g_idx, out_bf, out_res):
            st, st_r = stat.next()
            S.op(act, lambda: nc.scalar.activation(out=junk[:], in_=src, func=AF.Square, accum_out=st[:, 0:1]),
                 [src_res], [R["junk"], st_r])
            act_fn(st[:, 1:2], st[:, 0:1], AF.Ln, [st_r, C], [st_r], bias=c_eps[:, 0:1], scale=1.0 / D)
            act_fn(st[:, 2:3], st[:, 1:2], AF.Exp, [st_r], [st_r], scale=-0.5)
            if out_bf is not None:
                stt(out_bf, src, st[:, 2:3], gts[:, g_idx, :], ALU.mult, ALU.mult, [src_res, st_r, C], [out_res])
            return st, st_r

        def transpose_to(src_bf, src_res, dst3, dst_res):
            pt, pt_r = pts.next()
            for kc in range(8):
                tr(pt[:, kc * 128:(kc + 1) * 128], src_bf[:, kc * 128:(kc + 1) * 128], [src_res], [pt_r])
            cp(dve, dst3, pt[:, :].rearrange("p (k n) -> p k n", k=8), [pt_r], [dst_res])

        def barrier():
            tks = []
            for q in (pe, act, dve, pool):
                if q.cnt:
                    tks.append((q.sem, q.cnt, q.name, q.name))
            for q in (sp, pool):
                for i, sm in enumerate(q.dsems):
                    if q.dcnt[i]:
                        tks.append((sm, q.dcnt[i], f"{q.name}d{i}", "dma"))
            for q in (pe, act, dve, pool, sp):
                for tk in tks:
                    q.wait(tk)

        attn = {}

        def attn_alloc(stack):
            attn["e"] = Rot([sb(f"eb{i}", [128, 512], F32, stack) for i in range(2)])
            attn["sp"] = Rot([sb(f"spb{i}", [128, 512], BF16, stack) for i in range(3)])
            attn["w"] = Rot([sb(f"wb{i}", [128, 512], BF16, stack) for i in range(3)])
            attn["red"] = Rot([sb(f"red{i}", [128, 128], F32, stack) for i in range(2)])
            attn["zb"] = Rot.of(banks.bufs[0:2])
            attn["wbk"] = Rot.of(banks.bufs[2:4])
            attn["ctx"] = []
            for i in range(2):
                attn["ctx"].append(dict(S32=sb(f"S32_{i}", [128, 128], F32, stack), S32r=Res(),
                                        Sb=[sb(f"Sb{i}_{k}", [128, 128], BF16, stack) for k in range(2)], Sbr=[Res(), Res()],
                                        O=banks.bufs[4 + i][0], Or=banks.bufs[4 + i][1]))

        def make_items(ctx, qT, q_res, TQ, groups, out_ap, out_res):
            items = []
            ntile_total = sum(len(g) for g in groups)
            seen = 0
            for k, g in enumerate(groups):
                items.append(dict(ctx=ctx, qT=qT, q_res=q_res, TQ=TQ, tiles=g, k=k, first=(k == 0), last=(k == len(groups) - 1),
                                  o_first=seen, ntot=ntile_total, out_ap=out_ap, out_res=out_res))
                seen += len(g)
            return items

        def st1(it):
            ctx, TQ, tiles = it["ctx"], it["TQ"], it["tiles"]
            ks = tiles[0]["ks"]
            n = len(tiles)
            z, z_r = attn["zb"].next()
            for i, t in enumerate(tiles):
                mm(z[0:ks, i * TQ:(i + 1) * TQ], t["kT"], it["qT"], True, True, t["res"] + [it["q_res"]], [z_r])
            e, e_r = attn["e"].next()
            spt, sp_r = attn["sp"].next()
            it["sp"], it["sp_r"] = spt, sp_r
            act_fn(e[0:ks, 0:n * TQ], z[0:ks, 0:n * TQ], AF.Exp, [z_r], [e_r])
            act_fn(spt[0:ks, 0:n * TQ], e[0:ks, 0:n * TQ], AF.Ln, [e_r, C], [sp_r], bias=c_one[0:ks, 0:1])
            for i, t in enumerate(tiles):
                sl = spt[0:ks, i * TQ:(i + 1) * TQ]
                if t["mask"]:
                    tt(pool, sl, sl, dmask[0:ks, 0:TQ], ALU.mult, [sp_r, C], [sp_r])
                if t["valid"] is not None:
                    ts(pool, sl, sl, t["valid"], None, ALU.mult, None, [sp_r, C], [sp_r])
            if not it["last"]:
                k = it["k"]
                S32, S32r = ctx["S32"], ctx["S32r"]
                Sbn, Sbnr = ctx["Sb"][(k + 1) % 2], ctx["Sbr"][(k + 1) % 2]
                if n > 1:
                    rd, rd_r = attn["red"].next()
                    S.op(dve, lambda: nc.vector.tensor_reduce(out=rd[0:ks, 0:TQ],
                                                              in_=spt[0:ks, 0:n * TQ].rearrange("p (n t) -> p t n", n=n),
                                                              axis=AX.X, op=ALU.add), [sp_r], [rd_r])
                    src, src_r = rd[0:ks, 0:TQ], rd_r
                else:
                    src, src_r = spt[0:ks, 0:TQ], sp_r
                if it["first"] and ks == 128:
                    cp(dve, S32[0:ks, 0:TQ], src, [src_r], [S32r])
                else:
                    tt(dve, S32[0:ks, 0:TQ], S32[0:ks, 0:TQ], src, ALU.add, [src_r, S32r], [S32r])
                cp(dve, Sbn[:, 0:TQ], S32[:, 0:TQ], [S32r], [Sbnr])

        def st2(it):
            ctx, TQ, tiles = it["ctx"], it["TQ"], it["tiles"]
            ks = tiles[0]["ks"]
            n = len(tiles)
            k = it["k"]
            w, w_r = attn["wbk"].next()
            spt, sp_r = it["sp"], it["sp_r"]
            Sb, Sbr = ctx["Sb"][k % 2], ctx["Sbr"][k % 2]
            carry = not it["first"]
            for i, t in enumerate(tiles):
                o = w[0:ks, i * TQ:(i + 1) * TQ]
                nlater = n - 1 - i
                mm(o, t["kT"], it["qT"], True, False, t["res"] + [it["q_res"]], [w_r])
                if ks < 128 and ctx is attn["ctx"][1]:
                    pe.eng.wait_ge(pe.sem, pe.cnt)
                mm(o, ntri[0:ks, 0:ks], spt[0:ks, i * TQ:(i + 1) * TQ], False, (nlater == 0 and not carry), [sp_r, C], [w_r])
                for i2 in range(i + 1, n):
                    mm(o, nones[0:ks, 0:ks], spt[0:ks, i2 * TQ:(i2 + 1) * TQ], False, (i2 == n - 1 and not carry), [sp_r, C], [w_r])
                if carry:
                    mm(o, nones[0:128, 0:ks], Sb[:, 0:TQ], False, True, [Sbr, C], [w_r])
            wt, wt_r = attn["w"].next()
            it["w"], it["w_r"] = wt, wt_r
            act_fn(wt[0:ks, 0:n * TQ], w[0:ks, 0:n * TQ], AF.Exp, [w_r], [wt_r])
            for i, t in enumerate(tiles):
                sl = wt[0:ks, i * TQ:(i + 1) * TQ]
                if t["mask"]:
                    tt(pool, sl, sl, dmask[0:ks, 0:TQ], ALU.mult, [wt_r, C], [wt_r])
                if t["valid"] is not None:
                    ts(pool, sl, sl, t["valid"], None, ALU.mult, None, [wt_r, C], [wt_r])

        def st3(it):
            ctx, TQ, tiles = it["ctx"], it["TQ"], it["tiles"]
            ks = tiles[0]["ks"]
            Ob, Or = ctx["O"], ctx["Or"]
            wt, wt_r = it["w"], it["w_r"]
            for i, t in enumerate(tiles):
                idx = it["o_first"] + i
                mm(Ob[0:64, 0:TQ], t["v"], wt[0:ks, i * TQ:(i + 1) * TQ], idx == 0, idx == it["ntot"] - 1, t["res"] + [wt_r], [Or])
            if it["last"]:
                cp(dve, it["out_ap"], Ob[0:64, 0:TQ], [Or], [it["out_res"]])

        def run_items(items):
            n = len(items)
            for k in range(n + 2):
                if k < n:
                    st1(items[k])
                if 0 <= k - 1 < n:
                    st2(items[k - 1])
                if 0 <= k - 2 < n:
                    st3(items[k - 2])

        def interleave(a, b):
            out = []
            for i in range(max(len(a), len(b))):
                if i < len(a):
                    out.append(a[i])
                if i < len(b):
                    out.append(b[i])
            return out

        esq = push()
        QT_all = sb("QT_all", [128, 4, NTOK], BF16, esq)
        HG_all = sb("HG_all", [128, 4, NTOK], BF16, esq)
        es1 = push()
        W1 = sb("W1", [128, 8, 2560], BF16, es1)
        for kc in range(8):
            S.dma(pool, W1[:, kc, :], w_in[kc * 128:(kc + 1) * 128, 0:2560], [], [R["W1"]])
        W1r = R["W1"]
        QO, KO, VO, UO, GO = 0, 512, 1024, 1536, 2048
        xnT = Rot([sb(f"xnT{i}", [128, 8, 512], BF16, es1) for i in range(2)])
        ktc = Rot([sb(f"ktc{i}", [128, 4, 512], BF16, es1) for i in range(1)])
        vc = Rot([sb(f"vc{i}", [128, 4, 512], BF16, es1) for i in range(1)])
        f512 = Rot([sb(f"f512_{i}", [128, 512], F32, es1) for i in range(2)])
        ubA = [sb(f"ub{i}", [128, 4, 515], F32, es1) for i in range(3)]
        uc = sb("uc", [128, 2, 512], F32, es1)
        ucb = sb("ucb", [128, 2, 512], BF16, es1)
        rr = sb("rr", [128, 2, 512], F32, es1)
        ii = sb("ii", [128, 2, 512], F32, es1)
        aa = sb("aa", [128, 2, 512], F32, es1)
        tq = sb("tq", [128, 2, 512], F32, es1)
        hs = sb("hs", [128, 4, 128], F32, es1)
        hcar = sb("hcar", [128, 4], F32, es1)
        gx = sb("gx", [128, 4, 128], F32, es1)
        gyA = [sb(f"gy{i}", [128, 4, 128], F32, es1) for i in range(2)]
        ubs = sb("ubs", [128, 4, 2, 67], F32, es1)
        hin = sb("hin", [128, 4, 2], F32, es1)
        hout = sb("hout", [128, 4, 2], F32, es1)

        for i_ in range(3):
            S.op(pool, lambda i_=i_: nc.gpsimd.memset(ubA[i_][:], 0.0), [], [R["ub", i_]])
        S.op(pool, lambda: nc.gpsimd.memset(hcar[:], 0.0), [], [R["hcar"]])

        def proj_fm(col0, ncols_tiles, rhs3, rhs_res, n, evac):
            for j in range(ncols_tiles):
                bk, bk_r = banks.next()
                for kc in range(8):
                    mm(bk[:, 0:n], W1[:, kc, col0 + j * 128: col0 + (j + 1) * 128], rhs3[:, kc, 0:n], kc == 0, kc == 7,
                       [W1r, rhs_res], [bk_r])
                evac(j, bk, bk_r)

        def proj_fm_packed(col0, rhs3_own, rhs_res, evac):
            bk, bk_r = banks.next()
            for j in range(4):
                for kc in range(8):
                    mm(bk[:, j * 128:(j + 1) * 128], W1[:, kc, col0 + j * 128: col0 + (j + 1) * 128], rhs3_own(kc), kc == 0, kc == 7,
                       [W1r, rhs_res], [bk_r])
            evac(bk, bk_r)

        def gelu_a(bk, bk_r, gi=0):
            G = R["gelu", gi]
            GX = R["gx"]
            g2 = gx[:].rearrange("p a b -> p (a b)")
            y2 = gyA[gi][:].rearrange("p a b -> p (a b)")
            cp(act, g2, bk[:, :], [bk_r], [GX])
            tt(dve, y2, g2, g2, ALU.mult, [GX], [G])
            ts(dve, y2, y2, 0.044715, 1.0, ALU.mult, ALU.add, [G], [G])
            tt(dve, y2, y2, g2, ALU.mult, [G, GX], [G])

        def gelu_b(gi=0):
            G = R["gelu", gi]
            GX = R["gx"]
            g2 = gx[:].rearrange("p a b -> p (a b)")
            y2 = gyA[gi][:].rearrange("p a b -> p (a b)")
            act_fn(y2, y2, AF.Sigmoid, [G], [G], scale=GELU_C)
            tt(dve, y2, y2, g2, ALU.mult, [G, GX], [G])

        def gelu_gate(bk, bk_r, gi=0):
            gelu_a(bk, bk_r, gi)
            gelu_b(gi)

        def rnn_pair(cp2, n, chunk0, conv_src, conv_out, scan_fn, ub_r, after_sig=None):
            Ruc, Rucb, Rrr, Rii, Raa, Rtq = R["uc"], R["ucb"], R["rr"], R["ii"], R["aa"], R["tq"]
            for k2 in range(2):
                ct = 2 * cp2 + k2
                o = conv_out(k2)
                ts(dve, o, conv_src(ct, 0), cvec[:, ct, 0:1], cvec[:, ct, 4:5], ALU.mult, ALU.add, [ub_r, C], [Ruc])
                for j in range(1, 4):
                    stt(o, conv_src(ct, j), cvec[:, ct, j:j + 1], o, ALU.mult, ALU.add, [Ruc, ub_r, C], [Ruc])
            yield
            cp(act, ucb[:, :, 0:n], uc[:, :, 0:n], [Ruc], [Rucb])
            for k2 in range(2):
                ct = 2 * cp2 + k2
                bk, bk_r = banks.next()
                mm(bk[:, 0:n], wr[:, ct, :], ucb[:, k2, 0:n], True, True, [Rucb, C], [bk_r])
                act_fn(rr[:, k2, 0:n], bk[:, 0:n], AF.Sigmoid, [bk_r, C], [Rrr], bias=cvec[:, ct, 5:6])
                bk, bk_r = banks.next()
                mm(bk[:, 0:n], wi[:, ct, :], ucb[:, k2, 0:n], True, True, [Rucb, C], [bk_r])
                act_fn(ii[:, k2, 0:n], bk[:, 0:n], AF.Sigmoid, [bk_r, C], [Rii], bias=cvec[:, ct, 6:7])
            if after_sig is not None:
                after_sig()
            yield
            for k2 in range(2):
                ct = 2 * cp2 + k2
                act_fn(aa[:, k2, 0:n], rr[:, k2, 0:n], AF.Exp, [Rrr, C], [Raa], scale=coef[:, ct:ct + 1])
                act_fn(tq[:, k2, 0:n], rr[:, k2, 0:n], AF.Exp, [Rrr, C], [Rtq], scale=coef2[:, ct:ct + 1])
            act_fn(tq[:, :, 0:n], tq[:, :, 0:n], AF.Ln, [Rtq, C], [Rtq], bias=c_one[:, 0:1], scale=-1.0)
            act_fn(tq[:, :, 0:n], tq[:, :, 0:n], AF.Exp, [Rtq], [Rtq], scale=0.5)
            yield
            tt(dve, ii[:, :, 0:n], ii[:, :, 0:n], uc[:, :, 0:n], ALU.mult, [Rii, Ruc], [Rii])
            tt(dve, tq[:, :, 0:n], tq[:, :, 0:n], ii[:, :, 0:n], ALU.mult, [Rtq, Rii], [Rtq])
            if chunk0:
                for t in range(3):
                    ts(dve, tq[:, :, t * 128:(t + 1) * 128], tq[:, :, t * 128:(t + 1) * 128], validt[:, t:t + 1], None, ALU.mult, None,
                       [Rtq, C], [Rtq])
            yield
            scan_fn(cp2)
            yield

        OWN_S = 16
        xt, xt_r = xts.next()
        S.dma(sp, xt[:], xsm[:, :], [], [xt_r])
        xb, xb_r = xsb.next()
        norm_stats(xt[:], xt_r, GPRE, xb[:], xb_r)
        xn, xn_r = xnT.next()
        transpose_to(xb, xb_r, xn[:, :, 0:128], xn_r)
        stop("sp1")
        own_rhs = lambda kc, xn=xn: xn[:, kc, 0:128]
        proj_fm_packed(KO, own_rhs, xn_r, lambda bk, bk_r: cp(act, ktn[:].rearrange("p a b -> p (a b)"), bk[:, :], [bk_r], [R["ktn"]]))
        proj_fm_packed(QO, own_rhs, xn_r, lambda bk, bk_r: act_fn(
            QT_all[:, :, OWN_S * 128:(OWN_S + 1) * 128], bk[:, :].rearrange("p (a b) -> p a b", a=4), AF.Copy, [bk_r], [R["QT", OWN_S]],
            scale=0.125))
        proj_fm_packed(GO, own_rhs, xn_r, gelu_gate)

        stop("sp2")

        def tok_major(xn, xn_r, col0, lhs_cols, nrow, evacs):
            bk, bk_r = banks.next()
            for kc in range(8):
                mm(bk[0:nrow, :], xn[:, kc, lhs_cols[0]:lhs_cols[1]], W1[:, kc, col0:col0 + 512], kc == 0, kc == 7, [W1r, xn_r], [bk_r])
            evacs(bk, bk_r)

        def k_tok_s(bk, bk_r):
            f, f_r = f512.next()
            cp(act, f[:], bk[:, :], [bk_r], [f_r])
            S.dma(pool, k_s[:, :], f[:], [f_r], [], final=True)
        tok_major(xn, xn_r, KO, (0, 128), 128, k_tok_s)
        stop("sp2a")
        for s in range(2):
            def v_s_ev(bk, bk_r, s=s):
                f, f_r = f512.next()
                cp(act, f[0:64, :], bk[0:64, :], [bk_r], [f_r])
                cp(dve, vn[s][:], bk[0:64, :], [bk_r], [R["vn", s]])
                S.dma(pool, v_s[s * 64:(s + 1) * 64, :], f[0:64, :], [f_r], [], final=True)
            tok_major(xn, xn_r, VO, (s * 64, (s + 1) * 64), 64, v_s_ev)
            stop("sp2b")
        stop("sp3")
        S.dma(sp, ubs[:, :, :, 0:3], sconvT[:, :, :, :], [], [R["ubs"]])
        S.dma(sp, hin[:], shT[:, :, :], [], [R["hin"]])
        proj_fm(UO, 4, xn, xn_r, 128, lambda j, bk, bk_r: cp(
            act, ubs[:, j, :, 3:67], bk[:, 0:128].rearrange("p (s t) -> p s t", s=2), [bk_r], [R["ubs"]]))

        stop("sp4")

        def scan_s(cp2):
            for k2 in range(2):
                ct = 2 * cp2 + k2
                for s in range(2):
                    S.op(dve, lambda ct=ct, s=s, k2=k2: nc.vector.tensor_tensor_scan(
                        out=hs[:, ct, s * 64:(s + 1) * 64], data0=aa[:, k2, s * 64:(s + 1) * 64], data1=tq[:, k2, s * 64:(s + 1) * 64],
                        initial=hin[:, ct, s:s + 1], op0=ALU.mult, op1=ALU.add), [R["aa"], R["tq"], R["hin"]], [R["hs"]])
        for cp2 in range(2):
            for _ in rnn_pair(cp2, 128, False, lambda ct, j: ubs[:, ct, :, j:j + 64],
                              lambda k2: uc[:, k2, 0:128].rearrange("p (s t) -> p s t", s=2), scan_s, R["ubs"]):
                pass
        tt(dve, HG_all[:, :, OWN_S * 128:(OWN_S + 1) * 128], hs[:, :, :], gyA[0][:], ALU.mult, [R["hs"], R["gelu", 0]], [R["HG", OWN_S]])
        cp(dve, hout[:], hs[:, :, :].rearrange("p c (s t) -> p c s t", s=2)[:, :, :, 63], [R["hs"]], [R["hout"]])
        S.dma(pool, hT_s[:, :, :], hout[:], [R["hout"]], [], final=True)
        S.dma(pool, convT_s[:, :, :, :], ubs[:, :, :, 64:67], [R["ubs"]], [], final=True)

        stop("sample_pre")
        def front(c):
            xn, xn_r = xnT.next()
            for t in range(4):
                ft = 4 * c + t
                xt, xt_r = xts.next()
                S.dma(sp, xt[:], xf[ft * 128:(ft + 1) * 128, :], [], [xt_r])
                xb, xb_r = xsb.next()
                norm_stats(xt[:], xt_r, GPRE, xb[:], xb_r)
                transpose_to(xb, xb_r, xn[:, :, t * 128:(t + 1) * 128], xn_r)
                yield
            ub, ub_r = ubA[c % 3], R["ub", c % 3]
            proj_fm(UO, 4, xn, xn_r, 512, lambda j, bk, bk_r: cp(act, ub[:, j, 3:515], bk[:, :], [bk_r], [ub_r]))
            if c + 1 < 16:
                cp(pool, ubA[(c + 1) % 3][:, :, 0:3], ub[:, :, 512:515], [ub_r], [R["ub", (c + 1) % 3]])
            yield
            kt_, kt_r = ktc.next()
            proj_fm(KO, 4, xn, xn_r, 512, lambda j, bk, bk_r: cp(act, kt_[:, j, :], bk[:, :], [bk_r], [kt_r]))
            S.dma(pool, KT_d[:, :, c * 512:(c + 1) * 512].rearrange("h p n -> p h n"), kt_[:], [kt_r], [R["KT_d"]])
            yield
            v_, v_r = vc.next()
            for t in range(4):
                bk, bk_r = banks.next()
                for kc in range(8):
                    mm(bk[:, :], xn[:, kc, t * 128:(t + 1) * 128], W1[:, kc, VO:VO + 512], kc == 0, kc == 7, [W1r, xn_r], [bk_r])
                cp(dve, v_[:, t, :], bk[:, :], [bk_r], [v_r])
                if t == 1:
                    yield
                if t == 3:
                    f, f_r = f512.next()
                    cp(act, f[:], bk[:, :], [bk_r], [f_r])
                    S.dma(pool, v_own[c * 128:(c + 1) * 128, :], f[:], [f_r], [], final=True)
            S.dma(pool, V_d[4 * c:4 * c + 4, :, :].rearrange("t p n -> p t n"), v_[:], [v_r], [R["V_d"]])
            yield
            bk, bk_r = banks.next()
            for kc in range(8):
                mm(bk[:, :], xn[:, kc, 384:512], W1[:, kc, KO:KO + 512], kc == 0, kc == 7, [W1r, xn_r], [bk_r])
            f, f_r = f512.next()
            cp(act, f[:], bk[:, :], [bk_r], [f_r])
            S.dma(pool, k_own[c * 128:(c + 1) * 128, :], f[:], [f_r], [], final=True)
            own_rhs = lambda kc, xn=xn: xn[:, kc, 384:512]
            proj_fm_packed(QO, own_rhs, xn_r, lambda bk, bk_r, c=c: act_fn(
                QT_all[:, :, c * 128:(c + 1) * 128], bk[:, :].rearrange("p (a b) -> p a b", a=4), AF.Copy, [bk_r], [R["QT", c]], scale=0.125))
            yield
            proj_fm_packed(GO, own_rhs, xn_r, lambda bk, bk_r, c=c: gelu_a(bk, bk_r, c % 2))
            yield

        def rnn(c):
            ub, ub_r = ubA[c % 3], R["ub", c % 3]

            def scan_p(cp2):
                for k2 in range(2):
                    ct = 2 * cp2 + k2
                    S.op(dve, lambda ct=ct, k2=k2: nc.vector.tensor_tensor_scan(
                        out=rr[:, k2, :], data0=aa[:, k2, :], data1=tq[:, k2, :], initial=hcar[:, ct:ct + 1], op0=ALU.mult, op1=ALU.add),
                        [R["aa"], R["tq"], R["hcar"]], [R["rr"]])
                    cp(dve, hcar[:, ct:ct + 1], rr[:, k2, 511:512], [R["rr"]], [R["hcar"]])
                    cp(dve, hs[:, ct, :], rr[:, k2, 384:512], [R["rr"]], [R["hs"]])
            for cp2 in range(2):
                yield from rnn_pair(cp2, 512, c == 0, lambda ct, j: ub[:, ct, j:j + 512], lambda k2: uc[:, k2, :], scan_p, ub_r,
                                    after_sig=((lambda c=c: gelu_b(c % 2)) if cp2 == 0 else None))
            tt(dve, HG_all[:, :, c * 128:(c + 1) * 128], hs[:, :, :], gyA[c % 2][:], ALU.mult, [R["hs"], R["gelu", c % 2]], [R["HG", c]])

        def drive(g1, g2):
            gens = [g for g in (g1, g2) if g is not None]
            while gens:
                for g in list(gens):
                    try:
                        next(g)
                    except StopIteration:
                        gens.remove(g)

        drive(front(0), None)
        for c in range(16):
            if c == 1:
                stop("phase1_c0")
            drive(front(c + 1) if c + 1 < 16 else None, rnn(c))
        S.dma(pool, convT_p[:, :, :], ubA[15 % 3][:, :, 512:515], [R["ub", 15 % 3]], [], final=True)
        S.dma(pool, hT_p[:, :], hcar[:], [R["hcar"]], [], final=True)
        OT_d = dt("OT_d", [64, 8, NTOK], BF16, kind="Internal")
        HG_d = dt("HG_d", [128, 4, NTOK], BF16, kind="Internal")
        S.dma(sp, HG_d[:, :, :], HG_all[:], [R["HG", i] for i in range(NOWN)], [R["HG_d"]])
        stop("phase1")
        barrier()
        pop(es1)

        eso = push()
        OT_all = sb("OT_all", [64, 8, NTOK], BF16, eso)
        es2 = push()
        attn_alloc(es2)
        hctx = attn["ctx"]
        ess = push()
        ckb = sb("ckb", [128, 8, 512], BF16, ess)
        cvb = [sb(f"cvb{s}", [128, 8, 512], BF16, ess) for s in range(2)]
        KTs = [sb(f"KTs{s}", [128, 4, 1088], BF16, ess) for s in range(2)]
        items_all = []
        for s in range(2):
            S.dma(pool, ckb[:], ck[s].rearrange("(t p) n -> p t n", p=128), [], [R["ckb"]])
            S.dma(pool, cvb[s][:], cvd[s].rearrange("(t p) n -> p t n", p=128), [], [R["cvb", s]])
            for hp in range(4):
                pt, pt_r = pts.next()
                for t in range(8):
                    tr(pt[:, t * 128:(t + 1) * 128], ckb[:, t, hp * 128:(hp + 1) * 128], [R["ckb"]], [pt_r])
                cp(dve, KTs[s][:, hp, 0:1024], pt[:, :], [pt_r], [R["KTs", s]])
            cp(pool, KTs[s][:, :, 1024:1088], ktn[:, :, s * 64:(s + 1) * 64], [R["ktn"]], [R["KTs", s]])
        stop("sa1")
        for s in range(2):
            for hp in range(4):
                pair = []
                for h2 in range(2):
                    h = hp * 2 + h2
                    hb = h2 * 64
                    kres = [R["KTs", s], R["cvb", s], R["vn", s]]
                    tiles = [dict(kT=KTs[s][hb:hb + 64, hp, t * 128:(t + 1) * 128], v=cvb[s][:, t, h * 64:(h + 1) * 64], ks=128,
                                  mask=False, valid=None, res=kres) for t in range(8)]
                    newt = dict(kT=KTs[s][hb:hb + 64, hp, 1024:1088], v=vn[s][0:64, h * 64:(h + 1) * 64], ks=64, mask=True, valid=None,
                                res=kres)
                    ctx = hctx[h2]
                    c0 = OWN_S * 128 + s * 64
                    pair.append(make_items(ctx, QT_all[hb:hb + 64, hp, c0:c0 + 64], R["QT", OWN_S], 64,
                                           [[newt], tiles[4:8], tiles[0:4]], OT_all[:, h, c0:c0 + 64], R["OT", OWN_S]))
                for h2 in range(2):
                    S.op(pool, lambda h2=h2: nc.gpsimd.memset(hctx[h2]["S32"][:], 0.0), [], [hctx[h2]["S32r"]])
                run_items(interleave(pair[0], pair[1]))
                stop("sa2")
        stop("sample_attn")
        barrier()
        pop(ess)

        es3 = push()
        KTh = Rot([sb(f"KTh{i}", [128, 8192], BF16, es3) for i in range(1)])
        Vh = Rot([sb(f"Vh{i}", [128, 64, 128], BF16, es3) for i in range(1)])
        dmask2 = sb("dmask2", [128, 2, 128], BF16, es3)
        for k2 in range(2):
            cp(pool, dmask2[:, k2, :], dmask[:, :], [C], [R["dmask2"]])
        DM2 = R["dmask2"]
        S32p = sb("S32p", [128, 2, 512], F32, es3)
        S32r = Res()
        Sbp = [sb(f"Sbp{k}", [128, 2, 512], BF16, es3) for k in range(3)]
        Sbr = [Res() for _ in range(3)]
        ep = Rot([sb(f"ep{i}", [128, 2, 512], F32, es3) for i in range(2)])
        spp = Rot([sb(f"spp{i}", [128, 2, 512], BF16, es3) for i in range(4)])
        wtp = Rot([sb(f"wtp{i}", [128, 2, 512], BF16, es3) for i in range(3)])
        Zp, Zr = bigs[0], [banks.bufs[0][1], banks.bufs[1][1]]
        Wp, Wr = bigs[1], [banks.bufs[2][1], banks.bufs[3][1]]
        Ob = [banks.bufs[4][0], banks.bufs[5][0]]
        Or = [banks.bufs[4][1], banks.bufs[5][1]]
        Z3 = Zp[:, :].rearrange("p (a n) -> p a n", a=2)
        W3 = Wp[:, :].rearrange("p (a n) -> p a n", a=2)

        def make_units(kth, kth_r, vh, vh_r, hp, M):
            q_res = [R["QT", m] for m in range(4 * M, 4 * M + 4)]
            o_res = [R["OT", m] for m in range(4 * M, 4 * M + 4)]
            units = []
            top = 16 * M + 15
            for k, ft in enumerate(range(top, -1, -1)):
                m_min = max(4 * M, -(-(ft - 3) // 4))
                c_lo = (m_min - 4 * M) * 128
                diag = None
                if ft % 4 == 3 and (ft - 3) // 4 >= 4 * M:
                    d0 = ((ft - 3) // 4 - 4 * M) * 128
                    diag = (d0, d0 + 128)
                units.append(dict(kT=[kth[hb:hb + 64, ft * 128:(ft + 1) * 128] for hb in (0, 64)],
                                  v=[vh[:, ft, hb:hb + 64] for hb in (0, 64)],
                                  qT=[QT_all[hb:hb + 64, hp, M * 512:(M + 1) * 512] for hb in (0, 64)], q_res=q_res,
                                  res=[kth_r[ft // 16], vh_r[ft // 16]], c_lo=c_lo, diag=diag, valid=None, k=k,
                                  first=(k == 0), last=(ft == 0),
                                  out_ap=[OT_all[:, hp * 2 + h2, M * 512:(M + 1) * 512] for h2 in range(2)], out_res=o_res))
            return units

        def p1a(u):
            c0 = u["c_lo"]
            for h2 in range(2):
                mm(Zp[:, h2 * 512 + c0:(h2 + 1) * 512], u["kT"][h2], u["qT"][h2][:, c0:512], True, True, u["res"] + u["q_res"],
                   [Zr[h2]], skip=True)
                if u["diag"]:
                    d0, d1 = u["diag"]
                    mm(Zp[:, h2 * 512 + d0:h2 * 512 + d1], ident[:, :], mneg[:, :], False, True, [C], [Zr[h2]], skip=True)
            e, e_r = ep.next()
            u["e"], u["e_r"] = e, e_r
            act_fn(e[:, :, c0:512], Z3[:, :, c0:512], AF.Exp, Zr, [e_r])

        def p1b(u):
            c0 = u["c_lo"]
            e, e_r = u["e"], u["e_r"]
            spt, sp_r = spp.next()
            u["sp"], u["sp_r"] = spt, sp_r
            act_fn(spt[:, :, c0:512], e[:, :, c0:512], AF.Ln, [e_r, C], [sp_r], bias=c_one[:, 0:1])
            if not u["last"]:
                tt(dve, S32p[:, :, c0:512], S32p[:, :, c0:512], spt[:, :, c0:512], ALU.add, [sp_r, S32r], [S32r])

        def pcast(u):
            if not u["last"]:
                c0, k = u["c_lo"], u["k"]
                cp(dve, Sbp[k % 3][:, :, c0:512], S32p[:, :, c0:512], [S32r], [Sbr[k % 3]])

        def p2a(u):
            c0, k = u["c_lo"], u["k"]
            spt, sp_r = u["sp"], u["sp_r"]
            carry = not u["first"]
            for h2 in range(2):
                o = Wp[:, h2 * 512 + c0:(h2 + 1) * 512]
                mm(o, u["kT"][h2], u["qT"][h2][:, c0:512], True, False, u["res"] + u["q_res"], [Wr[h2]], skip=True)
                if u["diag"]:
                    d0, d1 = u["diag"]
                    mm(Wp[:, h2 * 512 + d0:h2 * 512 + d1], ident[:, :], mneg[:, :], False, False, [C], [Wr[h2]], skip=True)
                mm(o, ntri[:, :], spt[:, h2, c0:512], False, not carry, [sp_r, C], [Wr[h2]], skip=True)
                if carry:
                    mm(o, nones[:, :], Sbp[(k - 1) % 3][:, h2, c0:512], False, True, [Sbr[(k - 1) % 3], C], [Wr[h2]], skip=True)

        def p2b(u):
            c0 = u["c_lo"]
            wt, wt_r = wtp.next()
            u["w"], u["w_r"] = wt, wt_r
            act_fn(wt[:, :, c0:512], W3[:, :, c0:512], AF.Exp, Wr, [wt_r])

        def p3(u):
            c0 = u["c_lo"]
            for h2 in range(2):
                mm(Ob[h2][0:64, c0:512], u["v"][h2], u["w"][:, h2, c0:512], u["first"], u["last"], u["res"] + [u["w_r"]], [Or[h2]],
                   skip=True)
                if u["last"]:
                    cp(dve, u["out_ap"][h2], Ob[h2][0:64, 0:512], [Or[h2]], u["out_res"])

        def run_units(units):
            n = len(units)
            for t in range(n + 3):
                if t < n:
                    p1a(units[t])
                if 0 <= t - 2 < n:
                    p2a(units[t - 2])
                if 0 <= t - 1 < n:
                    pcast(units[t - 1])
                if 0 <= t - 2 < n:
                    p2b(units[t - 2])
                if t < n:
                    p1b(units[t])
                if 0 <= t - 3 < n:
                    p3(units[t - 3])

        for hp in range(4):
            kth, kth_r = KTh.next()
            vh, vh_r = Vh.next()
            kq = [R["kthq", q4] for q4 in range(4)]
            vq = [R["vhq", q4] for q4 in range(4)]
            for q4 in range(4):
                S.dma(sp, kth[:, q4 * 2048:(q4 + 1) * 2048], KT_d[hp, :, q4 * 2048:(q4 + 1) * 2048], [R["KT_d"]], [kq[q4]])
                S.dma(sp, vh[:, q4 * 16:(q4 + 1) * 16, :],
                      V_d[q4 * 16:(q4 + 1) * 16, :, hp * 128:(hp + 1) * 128].rearrange("t p n -> p t n"), [R["V_d"]], [vq[q4]])
            kth_r, vh_r = kq, vq
            for M in range(4):
                S.op(pool, lambda: nc.gpsimd.memset(S32p[:], 0.0), [], [S32r])
                for k3 in range(3):
                    S.op(pool, lambda k3=k3: nc.gpsimd.memset(Sbp[k3][:], 0.0), [], [Sbr[k3]])
                run_units(make_units(kth, kth_r, vh, vh_r, hp, M))
                if hp == 0 and M == 0:
                    stop("phase2_m1")
            S.dma(sp, OT_d[:, 2 * hp:2 * hp + 2, :], OT_all[:, 2 * hp:2 * hp + 2, :], [R["OT", i] for i in range(NOWN)],
                  [R["OT_d", hp]])
            if hp == 0:
                stop("phase2_hp0")
        barrier()
        pop(es3)
        pop(es2)
        pop(eso)
        pop(esq)

        stop("phase2")
        X1_d = dt("X1_d", [NOWN, 128, D], F32, kind="Internal")
        NTH = 9 * 128
        es5 = push()
        xnHs = {0: sb("xnH", [128, 8, NTH], BF16, es5), 9: sb("xnHB", [128, 8, 8 * 128], BF16, es5)}
        xnH_rs = {0: RD(), 9: RD()}

        def front_tile(t0, tl):
            xt, xt_r = xts.next()
            S.dma(sp, xt[:], own_rows(t0 + tl), [], [xt_r])
            xb, xb_r = xsb.next()
            norm_stats(xt[:], xt_r, GPRE, xb[:], xb_r)
            transpose_to(xb, xb_r, xnHs[t0][:, :, tl * 128:(tl + 1) * 128], xnH_rs[t0][tl])

        def own_rows(i):
            return xsm[:, :] if i == 16 else xf[(4 * i + 3) * 128:(4 * i + 4) * 128, :]

        wo = sb("wo", [128, 8, D], BF16, es5)
        wga = Rot([sb(f"wga{i}", [128, 8, 128], BF16, es5) for i in range(2)])
        wgb = Rot([sb(f"wgb{i}", [128, 8, 128], BF16, es5) for i in range(2)])
        wa = Rot([sb(f"wa{i}", [64, 8, 128], BF16, es5) for i in range(2)])
        wbt = Rot([sb(f"wbt{i}", [128, 4, 128], BF16, es5) for i in range(2)])
        wo_r = Res()
        S.dma(pool, wo[:], w_o.rearrange("(kc p) n -> p kc n", p=128), [], [wo_r])

        def load_mt(mt):
            a_, a_r = wga.next()
            b_, b_r = wgb.next()
            wa_, wa_r = wa.next()
            wb_2, wb_r = wbt.next()
            S.dma(pool, a_[:], w_in[:, 2560 + mt * 128:2560 + (mt + 1) * 128].rearrange("(kc p) n -> p kc n", p=128), [], [a_r])
            S.dma(pool, b_[:], w_in[:, 3584 + mt * 128:3584 + (mt + 1) * 128].rearrange("(kc p) n -> p kc n", p=128), [], [b_r])
            S.dma(pool, wa_[:], w_a[:, mt * 128:(mt + 1) * 128].rearrange("(h p) n -> p h n", p=64), [], [wa_r])
            S.dma(pool, wb_2[:], w_b[:, mt * 128:(mt + 1) * 128].rearrange("(c p) n -> p c n", p=128), [], [wb_r])
            return (a_, a_r, b_, b_r, wa_, wa_r, wb_2, wb_r)

        def prefetch_3a(t0, NT):
            return load_mt(0)
        pre3a = {}
        wu = Rot([sb(f"wu{i}", [128, 8, 512], BF16, es5) for i in range(2)])
        wd = Rot([sb(f"wd{i}", [128, 4, D], BF16, es5) for i in range(2)])

        def load_fg(fg):
            wu_, wu_r = wu.next()
            wd_, wd_r = wd.next()
            S.dma(pool, wu_[:], w_up[:, fg * 512:(fg + 1) * 512].rearrange("(kc p) n -> p kc n", p=128), [], [wu_r])
            S.dma(pool, wd_[:], w_dn[fg * 512:(fg + 1) * 512, :].rearrange("(f p) n -> p f n", p=128), [], [wd_r])
            return (wu_, wu_r, wd_, wd_r)

        for (t0, ntl) in ((0, 9), (9, 8)):
            NT = ntl * 128
            chunks = [(c0, min(512, NT - c0)) for c0 in range(0, NT, 512)]
            es6 = push()
            otc = sb("otc", [64, 8, NTH], BF16, es6)
            hgc = sb("hgc", [128, 4, NTH], BF16, es6)
            otc_r, hgc_r = Res(), Res()
            S.dma(sp, otc[:, :, 0:NT], OT_d[:, :, t0 * 128:t0 * 128 + NT], [R["OT_d", hp_] for hp_ in range(4)], [otc_r])
            S.dma(sp, hgc[:, :, 0:NT], HG_d[:, :, t0 * 128:t0 * 128 + NT], [R["HG_d"]], [hgc_r])
            mTh = sb("mTh", [128, 8, NTH], BF16, es6)
            sga = Rot([sb(f"sga{i}", [128, 512], F32, es6) for i in range(2)])
            sgb = Rot([sb(f"sgb{i}", [128, 512], F32, es6) for i in range(2)])
            mixs = Rot([sb(f"mixs{i}", [128, D], F32, es6) for i in range(2)])
            x1s = Rot([sb(f"x1s{i}", [128, D], F32, es6) for i in range(2)])
            mT_r = RD()
            nxt = pre3a.pop(t0) if t0 in pre3a else prefetch_3a(t0, NT)
            xnH, xnH_r = xnHs[t0], xnH_rs[t0]
            if t0 == 0:
                for tl in range(ntl):
                    front_tile(0, tl)
            for mt in range(8):
                (a_, a_r, b_, b_r, wa_, wa_r, wb_2, wb_r) = nxt
                if mt < 7:
                    nxt = load_mt(mt + 1)
                for (c0, n) in chunks:
                    xr = [xnH_r[tl] for tl in range(c0 // 128, (c0 + n) // 128)]
                    bkA, bkA_r = banks.next()
                    for kc in range(8):
                        mm(bkA[:, 0:n], a_[:, kc, :], xnH[:, kc, c0:c0 + n], kc == 0, kc == 7, [a_r] + xr, [bkA_r])
                    sa, sa_r = sga.next()
                    act_fn(sa[:, 0:n], bkA[:, 0:n], AF.Sigmoid, [bkA_r], [sa_r])
                    bkB, bkB_r = banks.next()
                    for kc in range(8):
                        mm(bkB[:, 0:n], b_[:, kc, :], xnH[:, kc, c0:c0 + n], kc == 0, kc == 7, [b_r] + xr, [bkB_r])
                    sb_, sb_r = sgb.next()
                    act_fn(sb_[:, 0:n], bkB[:, 0:n], AF.Sigmoid, [bkB_r], [sb_r])
                    bkY, bkY_r = banks.next()
                    for h in range(8):
                        mm(bkY[:, 0:n], wa_[0:64, h, :], otc[0:64, h, c0:c0 + n], h == 0, h == 7, [wa_r, otc_r], [bkY_r])
                    tt(dve, sa[:, 0:n], sa[:, 0:n], bkY[:, 0:n], ALU.mult, [sa_r, bkY_r], [sa_r])
                    bkZ, bkZ_r = banks.next()
                    for ct in range(4):
                        mm(bkZ[:, 0:n], wb_2[:, ct, :], hgc[:, ct, c0:c0 + n], ct == 0, ct == 3, [wb_r, hgc_r], [bkZ_r])
                    tt(dve, sb_[:, 0:n], sb_[:, 0:n], bkZ[:, 0:n], ALU.mult, [sb_r, bkZ_r], [sb_r])
                    tt(dve, mTh[:, mt, c0:c0 + n], sa[:, 0:n], sb_[:, 0:n], ALU.add, [sa_r, sb_r], [mT_r[c0]])
            allmT = [mT_r[c0] for (c0, n) in chunks]
            nxt_fg = load_fg(0)
            for tl in range(ntl):
                i = t0 + tl
                mx, mx_r = mixs.next()
                for half in range(2):
                    bk, bk_r = banks.next()
                    for kc in range(8):
                        mm(bk[:, :], mTh[:, kc, tl * 128:(tl + 1) * 128], wo[:, kc, half * 512:(half + 1) * 512], kc == 0, kc == 7,
                           allmT + [wo_r], [bk_r])
                    cp(act, mx[:, half * 512:(half + 1) * 512], bk[:, :], [bk_r], [mx_r])
                st, st_r = norm_stats(mx[:], mx_r, GPOST, None, None)
                stt(mx[:], mx[:], st[:, 2:3], gts[:, GPOST, :], ALU.mult, ALU.mult, [mx_r, st_r, C], [mx_r])
                xt, xt_r = xts.next()
                S.dma(sp, xt[:], own_rows(i), [], [xt_r])
                x1, x1_r = x1s.next()
                tt(dve, x1[:], mx[:], xt[:], ALU.add, [mx_r, xt_r], [x1_r])
                S.dma(sp, X1_d[i, :, :], x1[:], [x1_r], [R["X1_d", i]])
                xb, xb_r = xsb.next()
                norm_stats(x1[:], x1_r, GPREF, xb[:], xb_r)
                transpose_to(xb, xb_r, xnH[:, :, tl * 128:(tl + 1) * 128], xnH_r[tl])
            barrier()
            pop(es6)
            es7 = push()
            facc = sb("facc", [128, 9, D], F32, es7)
            rl = Rot([sb(f"rl{i}", [128, 512], F32, es7) for i in range(2)])
            hT = Rot([sb(f"hT{i}", [128, 4, 512], BF16, es7) for i in range(2)])
            facc_r = RD()

            nxt = nxt_fg
            if t0 == 0:
                pre3a[9] = prefetch_3a(9, 8 * 128)
            for fg in range(8):
                (wu_, wu_r, wd_, wd_r) = nxt
                if fg < 7:
                    nxt = load_fg(fg + 1)
                for (c0, n) in chunks:
                    xr = [xnH_r[tl] for tl in range(c0 // 128, (c0 + n) // 128)]
                    h_, h_r = hT.next()
                    for f4 in range(4):
                        bk, bk_r = banks.next()
                        for kc in range(8):
                            mm(bk[:, 0:n], wu_[:, kc, f4 * 128:(f4 + 1) * 128], xnH[:, kc, c0:c0 + n], kc == 0, kc == 7, [wu_r] + xr, [bk_r])
                        r_, r_r = rl.next()
                        act_fn(r_[:, 0:n], bk[:, 0:n], AF.Relu, [bk_r], [r_r])
                        tt(pool, h_[:, f4, 0:n], r_[:, 0:n], r_[:, 0:n], ALU.mult, [r_r], [h_r])
                    for tq_ in range(n // 128):
                        tl = c0 // 128 + tq_
                        for half in range(2):
                            bk, bk_r = banks.next()
                            for f4 in range(4):
                                mm(bk[:, :], h_[:, f4, tq_ * 128:(tq_ + 1) * 128], wd_[:, f4, half * 512:(half + 1) * 512], f4 == 0, f4 == 3,
                                   [h_r, wd_r], [bk_r])
                            dst = facc[:, tl, half * 512:(half + 1) * 512]
                            if fg == 0:
                                cp(act, dst, bk[:, :], [bk_r], [facc_r[tl]])
                            else:
                                tt(dve, dst, dst, bk[:, :], ALU.add, [bk_r, facc_r[tl]], [facc_r[tl]])
                if t0 == 0:
                    front_tile(9, fg)
            for tl in range(ntl):
                i = t0 + tl
                st, st_r = norm_stats(facc[:, tl, :], facc_r[tl], GPOSTF, None, None)
                stt(facc[:, tl, :], facc[:, tl, :], st[:, 2:3], gts[:, GPOSTF, :], ALU.mult, ALU.mult, [facc_r[tl], st_r, C], [facc_r[tl]])
                xt, xt_r = xts.next()
                S.dma(sp, xt[:], X1_d[i, :, :], [R["X1_d", i]], [xt_r])
                tt(dve, xt[:], xt[:], facc[:, tl, :], ALU.add, [xt_r, facc_r[tl]], [xt_r])
                dst = y_s[:, :] if i == 16 else y_own[i * 128:(i + 1) * 128, :]
                S.dma(sp, dst, xt[:], [xt_r], [xt_r], final=True)
            if t0 == 0:
                stop("phase3_c0")
            barrier()
            pop(es7)
        pop(es5)


_NC_CACHE = {}


def kernel(x_prompt, x_sample, cache_k, cache_v, state_conv, state_h, w_in, g_pre_mix, w_conv, b_conv, w_r, b_r, w_i, b_i,
           lam, w_a_out, w_b_out, w_o, g_post_mix, g_pre_ffn, w_up, w_down, g_post_ffn):
    if "nc" not in _NC_CACHE:
        _NC_CACHE["nc"] = build()
    nc = _NC_CACHE["nc"]
    in_maps = prep_inputs(x_prompt, x_sample, cache_k, cache_v, state_conv, state_h, w_in, g_pre_mix, w_conv, b_conv, w_r, b_r,
                          w_i, b_i, lam, w_a_out, w_b_out, w_o, g_post_mix, g_pre_ffn, w_up, w_down, g_post_ffn)
    res = run_bass_kernel_spmd(nc, in_maps, core_ids=list(range(8)))
    return assemble(res.results)


def prep_inputs(x_prompt, x_sample, cache_k, cache_v, state_conv, state_h, w_in, g_pre_mix, w_conv, b_conv, w_r, b_r, w_i, b_i,
                lam, w_a_out, w_b_out, w_o, g_post_mix, g_pre_ffn, w_up, w_down, g_post_ffn):
    f = lambda a: np.ascontiguousarray(np.asarray(a, dtype=np.float32))
    x_prompt, x_sample = f(x_prompt), f(x_sample)
    cache_k, cache_v, state_conv, state_h = f(cache_k), f(cache_v), f(state_conv), f(state_h)

    def chT(v):
        return np.ascontiguousarray(f(v).reshape(4, 128).T)

    gvv = np.stack([np.broadcast_to(f(g)[0][None, :], (128, D)) for g in (g_pre_mix, g_post_mix, g_pre_ffn, g_post_ffn)])
    gvv = np.ascontiguousarray(gvv)
    wc = f(w_conv)[0]
    cvec = np.stack([chT(wc[0]), chT(wc[1]), chT(wc[2]), chT(wc[3]), chT(f(b_conv)[0]), chT(f(b_r)[0]), chT(f(b_i)[0]),
                     chT(f(lam)[0])], axis=-1)
    cvec = np.ascontiguousarray(cvec)

    def bd(w):
        w = f(w)[0]
        out = np.zeros((4, 128, 128), np.float32)
        for ct in range(4):
            out[ct, 0:64, 0:64] = w[2 * ct]
            out[ct, 64:128, 64:128] = w[2 * ct + 1]
        return out

    common = dict(gv=gvv, cvec=cvec, wrbd=bd(w_r), wibd=bd(w_i), w_in=f(w_in)[0], w_a=f(w_a_out)[0], w_b=f(w_b_out)[0],
                  w_o=f(w_o)[0], w_up=f(w_up)[0], w_dn=f(w_down)[0])
    in_maps = []
    for c in range(8):
        b, j = c // 4, c % 4
        pad = 384 - 128 * j
        xfr = np.zeros((8192, D), np.float32)
        xfr[pad:] = x_prompt[b, 0:8192 - pad]
        s0 = 2 * c
        valid = np.zeros((128, 4), np.float32)
        for i in range(4):
            valid[:, i] = 1.0 if i >= 3 - j else 0.0
        sc = state_conv[0, s0:s0 + 2]
        scT = np.ascontiguousarray(sc.reshape(2, 3, 4, 128).transpose(3, 2, 0, 1))
        shT = np.ascontiguousarray(state_h[0, s0:s0 + 2].reshape(2, 4, 128).transpose(2, 1, 0))
        m = dict(common)
        m.update(xf=xfr, xsm=np.ascontiguousarray(x_sample[s0:s0 + 2].reshape(128, D)),
                 ck=np.ascontiguousarray(cache_k[0, s0:s0 + 2].reshape(2, 1024, 512)),
                 cv=np.ascontiguousarray(cache_v[0, s0:s0 + 2].reshape(2, 1024, 512)),
                 sconvT=scT, shT=shT, valid=valid)
        in_maps.append(m)
    return in_maps


def assemble(rs):
    y_prompt = np.zeros((2, 8192, D), np.float32)
    k_prompt = np.zeros((1, 2, 8192, 8, 64), np.float32)
    v_prompt = np.zeros((1, 2, 8192, 8, 64), np.float32)
    conv_prompt = np.zeros((1, 2, 3, 512), np.float32)
    h_prompt = np.zeros((1, 2, 512), np.float32)
    y_sample = np.zeros((16, 64, D), np.float32)
    k_sample = np.zeros((1, 16, 64, 8, 64), np.float32)
    v_sample = np.zeros((1, 16, 64, 8, 64), np.float32)
    conv_sample = np.zeros((1, 16, 3, 512), np.float32)
    h_sample = np.zeros((1, 16, 512), np.float32)
    for c in range(8):
        b, j = c // 4, c % 4
        r = rs[c]
        for m in range(16):
            g0 = (4 * m + j) * 128
            y_prompt[b, g0:g0 + 128] = r["y_own"][m * 128:(m + 1) * 128]
            k_prompt[0, b, g0:g0 + 128] = r["k_own"][m * 128:(m + 1) * 128].reshape(128, 8, 64)
            v_prompt[0, b, g0:g0 + 128] = r["v_own"][m * 128:(m + 1) * 128].reshape(128, 8, 64)
        if j == 3:
            conv_prompt[0, b] = r["convT_p"].transpose(2, 1, 0).reshape(3, 512)
            h_prompt[0, b] = r["hT_p"].T.reshape(512)
        for s in range(2):
            sid = 2 * c + s
            y_sample[sid] = r["y_s"][s * 64:(s + 1) * 64]
            k_sample[0, sid] = r["k_s"][s * 64:(s + 1) * 64].reshape(64, 8, 64)
            v_sample[0, sid] = r["v_s"][s * 64:(s + 1) * 64].reshape(64, 8, 64)
            conv_sample[0, sid] = r["convT_s"][:, :, s, :].transpose(2, 1, 0).reshape(3, 512)
            h_sample[0, sid] = r["hT_s"][:, :, s].T.reshape(512)
    return (y_prompt, y_sample, k_prompt, v_prompt, conv_prompt, h_prompt, k_sample, v_sample, conv_sample, h_sample)
```

```python
import contextlib
import os
import numpy as np
import concourse.bass as bass
import concourse.mybir as mybir
from concourse.bass_utils import run_bass_kernel_spmd

F32 = mybir.dt.float32
BF16 = mybir.dt.bfloat16
AF = mybir.ActivationFunctionType
ALU = mybir.AluOpType
AX = mybir.AxisListType

D = 1024
NOWN = 17
NTOK = NOWN * 128
EPS = 1e-6
GELU_C = 1.5957691216057308


class _Stop(Exception):
    pass


STOP = os.environ.get("MK_STOP", "")


def stop(tag):
    if STOP == tag:
        raise _Stop()


class Res:
    __slots__ = ("w", "r", "excl")

    def __init__(self):
        self.w = None
        self.r = {}
        self.excl = False


class RD(dict):
    def __missing__(self, k):
        v = Res()
        self[k] = v
        return v


class Q:
    def __init__(self, name, eng, sem):
        self.name = name
        self.eng = eng
        self.sem = sem
        self.cnt = 0
        self.seen = {}
        self.dsems = []
        self.dcnt = []
        self.di = 0

    def wait(self, tk):
        sem, val, key, qn = tk
        if qn == "pe" and self.name == "pe":
            return
        if self.seen.get(key, 0) >= val:
            return
        self.eng.wait_ge(sem, val)
        self.seen[key] = val


class Sched:
    def __init__(self, nc, es):
        self.nc = nc
        mk = lambda n: es.enter_context(nc.semaphore(n))
        self.pe = Q("pe", nc.tensor, mk("s_pe"))
        self.act = Q("act", nc.scalar, mk("s_act"))
        self.dve = Q("dve", nc.vector, mk("s_dve"))
        self.pool = Q("pool", nc.gpsimd, mk("s_pool"))
        self.sp = Q("sp", nc.sync, None)
        for q, pre, n in ((self.sp, "dsp", 32), (self.pool, "dpl", 32)):
            q.dsems = [mk(f"{pre}{i}") for i in range(n)]
            q.dcnt = [0] * n
        self.out_tk = []

    def _deps(self, q, reads, writes):
        for r in reads:
            if r.w is not None:
                q.wait(r.w)
        for w in writes:
            if w.w is not None:
                q.wait(w.w)
            for tk in w.r.values():
                q.wait(tk)

    def _mark(self, tk, reads, writes):
        for r in reads:
            r.r[tk[2]] = tk
        for w in writes:
            w.w = tk
            w.r = {}

    def op(self, q, fn, reads=(), writes=()):
        if any(r.excl for r in reads):
            writes = list(writes) + [r for r in reads if r.excl]
            reads = [r for r in reads if not r.excl]
        self._deps(q, reads, writes)
        ins = fn()
        q.cnt += 1
        ins.then_inc(q.sem, 1)
        self._mark((q.sem, q.cnt, q.name, q.name), reads, writes)

    def dma(self, q, out, in_, reads=(), writes=(), final=False):
        self._deps(q, reads, writes)
        i = q.di
        q.di = (i + 1) % len(q.dsems)
        key = f"{q.name}d{i}"
        if q.dcnt[i] > 0:
            q.wait((q.dsems[i], q.dcnt[i], key, "dma"))
        q.eng.dma_start(out=out, in_=in_).then_inc(q.dsems[i], 16)
        q.dcnt[i] += 16
        tk = (q.dsems[i], q.dcnt[i], key, "dma")
        self._mark(tk, reads, writes)
        if final:
            self.out_tk.append(tk)

    def finish(self):
        for tk in self.out_tk:
            self.sp.wait(tk)


class Rot:
    def __init__(self, bufs):
        self.bufs = [(b, Res()) for b in bufs]
        self.i = 0

    @classmethod
    def of(cls, pairs):
        r = cls([])
        r.bufs = list(pairs)
        return r

    def next(self):
        b = self.bufs[self.i]
        self.i = (self.i + 1) % len(self.bufs)
        return b


def build():
    nc = bass.Bass("TRN2", target_bir_lowering=False)

    def dt(name, shape, dtype=F32, kind="ExternalInput"):
        return nc.dram_tensor(name, shape, dtype, kind=kind).ap()

    xf = dt("xf", [8192, D])
    xsm = dt("xsm", [128, D])
    ck = dt("ck", [2, 1024, 512])
    cvd = dt("cv", [2, 1024, 512])
    sconvT = dt("sconvT", [128, 4, 2, 3])
    shT = dt("shT", [128, 4, 2])
    valid_d = dt("valid", [128, 4])
    gv = dt("gv", [4, 128, D])
    cvec_d = dt("cvec", [128, 4, 8])
    wrbd = dt("wrbd", [4, 128, 128])
    wibd = dt("wibd", [4, 128, 128])
    w_in = dt("w_in", [D, 4608])
    w_a = dt("w_a", [512, D])
    w_b = dt("w_b", [512, D])
    w_o = dt("w_o", [D, D])
    w_up = dt("w_up", [D, 4096])
    w_dn = dt("w_dn", [4096, D])
    O = "ExternalOutput"
    y_own = dt("y_own", [2048, D], kind=O)
    y_s = dt("y_s", [128, D], kind=O)
    k_own = dt("k_own", [2048, 512], kind=O)
    v_own = dt("v_own", [2048, 512], kind=O)
    k_s = dt("k_s", [128, 512], kind=O)
    v_s = dt("v_s", [128, 512], kind=O)
    convT_p = dt("convT_p", [128, 4, 3], kind=O)
    hT_p = dt("hT_p", [128, 4], kind=O)
    convT_s = dt("convT_s", [128, 4, 2, 3], kind=O)
    hT_s = dt("hT_s", [128, 4, 2], kind=O)
    KT_d = dt("KT_d", [4, 128, 8192], BF16, kind="Internal")
    V_d = dt("V_d", [64, 128, 512], BF16, kind="Internal")


    es = contextlib.ExitStack()
    with es:
        S = Sched(nc, es)
        nest = []
        try:
            _body(nc, es, S, dict(locals()))
        except _Stop:
            for st in reversed(nest):
                st.__exit__(None, None, None)
        S.finish()
    return nc


def _body(nc, es, S, L):
    if True:
        globals_ = L
        (xf, xsm, ck, cvd, sconvT, shT, valid_d, gv, cvec_d, wrbd, wibd, w_in, w_a, w_b, w_o, w_up, w_dn, y_own, y_s, k_own, v_own,
         k_s, v_s, convT_p, hT_p, convT_s, hT_s, KT_d, V_d, dt) = [L[k] for k in (
            "xf", "xsm", "ck", "cvd", "sconvT", "shT", "valid_d", "gv", "cvec_d", "wrbd", "wibd", "w_in", "w_a", "w_b", "w_o", "w_up",
            "w_dn", "y_own", "y_s", "k_own", "v_own", "k_s", "v_s", "convT_p", "hT_p", "convT_s", "hT_s", "KT_d", "V_d", "dt")]
        pe, act, dve, pool, sp = S.pe, S.act, S.dve, S.pool, S.sp
        nest = L["nest"]

        def push():
            st = contextlib.ExitStack()
            st.__enter__()
            nest.append(st)
            return st

        def pop(st):
            assert nest[-1] is st
            nest.pop()
            st.__exit__(None, None, None)

        used_names = {}

        def sb(name, shape, dtype=F32, stack=es):
            k = used_names.get(name, 0)
            used_names[name] = k + 1
            nm = "sb_" + name + (f"_v{k}" if k else "")
            return stack.enter_context(nc.sbuf_tensor(nm, shape, dtype))

        bigs = [es.enter_context(nc.psum_tensor(f"pb{i}", [128, 1024], F32)) for i in range(3)]
        banks = Rot([bigs[i // 2][:, (i % 2) * 512:(i % 2 + 1) * 512] for i in range(6)])
        pts = Rot([es.enter_context(nc.psum_tensor(f"pt{i}", [128, 1024], BF16)) for i in range(2)])
        for _, r_ in banks.bufs + pts.bufs:
            r_.excl = True

        identf = sb("identf", [128, 128])
        ident = sb("ident", [128, 128], BF16)
        ntri = sb("ntri", [128, 128], BF16)
        nones = sb("nones", [128, 128], BF16)
        dmask = sb("dmask", [128, 128], BF16)
        tmpc = sb("tmpc", [128, 128])
        mneg = sb("mneg", [128, 128], BF16)
        c_one = sb("c_one", [128, 1])
        c_eps = sb("c_eps", [128, 1])
        validt = sb("validt", [128, 4])
        gts = sb("gts", [128, 4, D])
        cvec = sb("cvec", [128, 4, 8])
        coef = sb("coef", [128, 4])
        coef2 = sb("coef2", [128, 4])
        ctmp = sb("ctmp", [128, 4])
        wr = sb("wr", [128, 4, 128], BF16)
        wi = sb("wi", [128, 4, 128], BF16)
        ktn = sb("ktn", [128, 4, 128], BF16)
        vn = [sb(f"vn{s}", [64, 512], BF16) for s in range(2)]
        R = RD()

        def act_fn(out, in_, func, reads, writes, bias=None, scale=None):
            kw = {}
            if bias is not None:
                kw["bias"] = bias
            if scale is not None:
                kw["scale"] = scale
            S.op(act, lambda: nc.scalar.activation(out=out, in_=in_, func=func, **kw), reads, writes)

        def tt(q, out, in0, in1, op, reads, writes):
            S.op(q, lambda: q.eng.tensor_tensor(out=out, in0=in0, in1=in1, op=op), reads, writes)

        def ts(q, out, in0, s1, s2, op0, op1, reads, writes):
            if op1 is None:
                S.op(q, lambda: q.eng.tensor_scalar(out=out, in0=in0, scalar1=s1, scalar2=None, op0=op0), reads, writes)
            else:
                S.op(q, lambda: q.eng.tensor_scalar(out=out, in0=in0, scalar1=s1, scalar2=s2, op0=op0, op1=op1), reads, writes)

        def stt(out, in0, scalar, in1, op0, op1, reads, writes, accum_out=None):
            kw = {} if accum_out is None else {"accum_out": accum_out}
            S.op(dve, lambda: nc.vector.scalar_tensor_tensor(out=out, in0=in0, scalar=scalar, in1=in1, op0=op0, op1=op1, **kw),
                 reads, writes)

        def cp(q, out, in_, reads, writes):
            if q is act:
                S.op(act, lambda: nc.scalar.copy(out=out, in_=in_), reads, writes)
            else:
                S.op(q, lambda: q.eng.tensor_copy(out=out, in_=in_), reads, writes)

        def mm(out, lhsT, rhs, start, stop, reads, writes, skip=False):
            S.op(pe, lambda: nc.tensor.matmul(out, lhsT=lhsT, rhs=rhs, start=start, stop=stop, skip_group_check=skip), reads, writes)

        def tr(out, in_, reads, writes):
            S.op(pe, lambda: nc.tensor.transpose(out=out, in_=in_, identity=ident[:]), reads + [R["const"]], writes)

        C = R["const"]
        S.op(pool, lambda: nc.gpsimd.memset(identf[:], 0.0), [], [C])
        S.op(pool, lambda: nc.gpsimd.affine_select(out=identf[:], in_=identf[:], pattern=[[-1, 128]], compare_op=ALU.not_equal,
                                                   fill=1.0, base=0, channel_multiplier=1), [], [C])
        cp(pool, ident[:], identf[:], [], [C])
        S.op(pool, lambda: nc.gpsimd.memset(tmpc[:], -1.0), [], [C])
        S.op(pool, lambda: nc.gpsimd.affine_select(out=tmpc[:], in_=tmpc[:], pattern=[[-1, 128]], compare_op=ALU.is_ge,
                                                   fill=0.0, base=0, channel_multiplier=1), [], [C])
        cp(pool, ntri[:], tmpc[:], [], [C])
        S.op(pool, lambda: nc.gpsimd.memset(nones[:], -1.0), [], [C])
        S.op(pool, lambda: nc.gpsimd.memset(tmpc[:], 1.0), [], [C])
        S.op(pool, lambda: nc.gpsimd.affine_select(out=tmpc[:], in_=tmpc[:], pattern=[[1, 128]], compare_op=ALU.is_gt,
                                                   fill=0.0, base=0, channel_multiplier=-1), [], [C])
        cp(pool, dmask[:], tmpc[:], [], [C])
        S.op(pool, lambda: nc.gpsimd.memset(tmpc[:], 0.0), [], [C])
        S.op(pool, lambda: nc.gpsimd.affine_select(out=tmpc[:], in_=tmpc[:], pattern=[[1, 128]], compare_op=ALU.is_gt,
                                                   fill=-30000.0, base=0, channel_multiplier=-1), [], [C])
        cp(pool, mneg[:], tmpc[:], [], [C])
        S.op(pool, lambda: nc.gpsimd.memset(c_one[:], 1.0), [], [C])
        S.op(pool, lambda: nc.gpsimd.memset(c_eps[:], EPS), [], [C])
        S.dma(sp, validt[:], valid_d[:, :], [], [C])
        S.dma(sp, cvec[:], cvec_d[:, :, :], [], [C])
        for i in range(4):
            S.dma(sp, gts[:, i, :], gv[i, :, :], [], [C])
        S.dma(pool, wr[:], wrbd.rearrange("c p n -> p c n"), [], [C])
        S.dma(pool, wi[:], wibd.rearrange("c p n -> p c n"), [], [C])
        act_fn(ctmp[:], cvec[:, :, 7], AF.Exp, [C], [C], scale=-1.0)
        act_fn(ctmp[:], ctmp[:], AF.Ln, [C], [C], bias=c_one[:, 0:1])
        ts(dve, coef[:], ctmp[:], -8.0, None, ALU.mult, None, [C], [C])
        ts(dve, coef2[:], ctmp[:], -16.0, None, ALU.mult, None, [C], [C])
        GPRE, GPOST, GPREF, GPOSTF = 0, 1, 2, 3

        stop("setup")
        xts = Rot([sb(f"xt{i}", [128, D]) for i in range(2)])
        junk = sb("junk", [128, D], BF16)
        xsb = Rot([sb(f"xsb{i}", [128, D], BF16) for i in range(2)])
        stat = Rot([sb(f"stat{i}", [128, 4]) for i in range(4)])

        def norm_stats(src, src_res, g_idx, out_bf, out_res):
            st, st_r = stat.next()
            S.op(act, lambda: nc.scalar.activation(out=junk[:], in_=src, func=AF.Square, accum_out=st[:, 0:1]),
                 [src_res], [R["junk"], st_r])
            act_fn(st[:, 1:2], st[:, 0:1], AF.Ln, [st_r, C], [st_r], bias=c_eps[:, 0:1], scale=1.0 / D)
            act_fn(st[:, 2:3], st[:, 1:2], AF.Exp, [st_r], [st_r], scale=-0.5)
            if out_bf is not None:
                stt(out_bf, src, st[:, 2:3], gts[:, g_idx, :], ALU.mult, ALU.mult, [src_res, st_r, C], [out_res])
            return st, st_r

        def transpose_to(src_bf, src_res, dst3, dst_res):
            pt, pt_r = pts.next()
            for kc in range(8):
                tr(pt[:, kc * 128:(kc + 1) * 128], src_bf[:, kc * 128:(kc + 1) * 128], [src_res], [pt_r])
            cp(dve, dst3, pt[:, :].rearrange("p (k n) -> p k n", k=8), [pt_r], [dst_res])

        def barrier():
            tks = []
            for q in (pe, act, dve, pool):
                if q.cnt:
                    tks.append((q.sem, q.cnt, q.name, q.name))
            for q in (sp, pool):
                for i, sm in enumerate(q.dsems):
                    if q.dcnt[i]:
                        tks.append((sm, q.dcnt[i], f"{q.name}d{i}", "dma"))
            for q in (pe, act, dve, pool, sp):
                for tk in tks:
                    q.wait(tk)

        attn = {}

        def attn_alloc(stack):
            attn["e"] = Rot([sb(f"eb{i}", [128, 512], F32, stack) for i in range(2)])
            attn["sp"] = Rot([sb(f"spb{i}", [128, 512], BF16, stack) for i in range(3)])
            attn["w"] = Rot([sb(f"wb{i}", [128, 512], BF16, stack) for i in range(3)])
            attn["red"] = Rot([sb(f"red{i}", [128, 128], F32, stack) for i in range(2)])
            attn["zb"] = Rot.of(banks.bufs[0:2])
            attn["wbk"] = Rot.of(banks.bufs[2:4])
            attn["ctx"] = []
            for i in range(2):
                attn["ctx"].append(dict(S32=sb(f"S32_{i}", [128, 128], F32, stack), S32r=Res(),
                                        Sb=[sb(f"Sb{i}_{k}", [128, 128], BF16, stack) for k in range(2)], Sbr=[Res(), Res()],
                                        O=banks.bufs[4 + i][0], Or=banks.bufs[4 + i][1]))

        def make_items(ctx, qT, q_res, TQ, groups, out_ap, out_res):
            items = []
            ntile_total = sum(len(g) for g in groups)
            seen = 0
            for k, g in enumerate(groups):
                items.append(dict(ctx=ctx, qT=qT, q_res=q_res, TQ=TQ, tiles=g, k=k, first=(k == 0), last=(k == len(groups) - 1),
                                  o_first=seen, ntot=ntile_total, out_ap=out_ap, out_res=out_res))
                seen += len(g)
            return items

        def st1(it):
            ctx, TQ, tiles = it["ctx"], it["TQ"], it["tiles"]
            ks = tiles[0]["ks"]
            n = len(tiles)
            z, z_r = attn["zb"].next()
            for i, t in enumerate(tiles):
                mm(z[0:ks, i * TQ:(i + 1) * TQ], t["kT"], it["qT"], True, True, t["res"] + [it["q_res"]], [z_r])
            e, e_r = attn["e"].next()
            spt, sp_r = attn["sp"].next()
            it["sp"], it["sp_r"] = spt, sp_r
            act_fn(e[0:ks, 0:n * TQ], z[0:ks, 0:n * TQ], AF.Exp, [z_r], [e_r])
            act_fn(spt[0:ks, 0:n * TQ], e[0:ks, 0:n * TQ], AF.Ln, [e_r, C], [sp_r], bias=c_one[0:ks, 0:1])
            for i, t in enumerate(tiles):
                sl = spt[0:ks, i * TQ:(i + 1) * TQ]
                if t["mask"]:
                    tt(pool, sl, sl, dmask[0:ks, 0:TQ], ALU.mult, [sp_r, C], [sp_r])
                if t["valid"] is not None:
                    ts(pool, sl, sl, t["valid"], None, ALU.mult, None, [sp_r, C], [sp_r])
            if not it["last"]:
                k = it["k"]
                S32, S32r = ctx["S32"], ctx["S32r"]
                Sbn, Sbnr = ctx["Sb"][(k + 1) % 2], ctx["Sbr"][(k + 1) % 2]
                if n > 1:
                    rd, rd_r = attn["red"].next()
                    S.op(dve, lambda: nc.vector.tensor_reduce(out=rd[0:ks, 0:TQ],
                                                              in_=spt[0:ks, 0:n * TQ].rearrange("p (n t) -> p t n", n=n),
                                                              axis=AX.X, op=ALU.add), [sp_r], [rd_r])
                    src, src_r = rd[0:ks, 0:TQ], rd_r
                else:
                    src, src_r = spt[0:ks, 0:TQ], sp_r
                if it["first"] and ks == 128:
                    cp(dve, S32[0:ks, 0:TQ], src, [src_r], [S32r])
                else:
                    tt(dve, S32[0:ks, 0:TQ], S32[0:ks, 0:TQ], src, ALU.add, [src_r, S32r], [S32r])
                cp(dve, Sbn[:, 0:TQ], S32[:, 0:TQ], [S32r], [Sbnr])

        def st2(it):
            ctx, TQ, tiles = it["ctx"], it["TQ"], it["tiles"]
            ks = tiles[0]["ks"]
            n = len(tiles)
            k = it["k"]
            w, w_r = attn["wbk"].next()
            spt, sp_r = it["sp"], it["sp_r"]
            Sb, Sbr = ctx["Sb"][k % 2], ctx["Sbr"][k % 2]
            carry = not it["first"]
            for i, t in enumerate(tiles):
                o = w[0:ks, i * TQ:(i + 1) * TQ]
                nlater = n - 1 - i
                mm(o, t["kT"], it["qT"], True, False, t["res"] + [it["q_res"]], [w_r])
                if ks < 128 and ctx is attn["ctx"][1]:
                    pe.eng.wait_ge(pe.sem, pe.cnt)
                mm(o, ntri[0:ks, 0:ks], spt[0:ks, i * TQ:(i + 1) * TQ], False, (nlater == 0 and not carry), [sp_r, C], [w_r])
                for i2 in range(i + 1, n):
                    mm(o, nones[0:ks, 0:ks], spt[0:ks, i2 * TQ:(i2 + 1) * TQ], False, (i2 == n - 1 and not carry), [sp_r, C], [w_r])
                if carry:
                    mm(o, nones[0:128, 0:ks], Sb[:, 0:TQ], False, True, [Sbr, C], [w_r])
            wt, wt_r = attn["w"].next()
            it["w"], it["w_r"] = wt, wt_r
            act_fn(wt[0:ks, 0:n * TQ], w[0:ks, 0:n * TQ], AF.Exp, [w_r], [wt_r])
            for i, t in enumerate(tiles):
                sl = wt[0:ks, i * TQ:(i + 1) * TQ]
                if t["mask"]:
                    tt(pool, sl, sl, dmask[0:ks, 0:TQ], ALU.mult, [wt_r, C], [wt_r])
                if t["valid"] is not None:
                    ts(pool, sl, sl, t["valid"], None, ALU.mult, None, [wt_r, C], [wt_r])

        def st3(it):
            ctx, TQ, tiles = it["ctx"], it["TQ"], it["tiles"]
            ks = tiles[0]["ks"]
            Ob, Or = ctx["O"], ctx["Or"]
            wt, wt_r = it["w"], it["w_r"]
            for i, t in enumerate(tiles):
                idx = it["o_first"] + i
                mm(Ob[0:64, 0:TQ], t["v"], wt[0:ks, i * TQ:(i + 1) * TQ], idx == 0, idx == it["ntot"] - 1, t["res"] + [wt_r], [Or])
            if it["last"]:
                cp(dve, it["out_ap"], Ob[0:64, 0:TQ], [Or], [it["out_res"]])

        def run_items(items):
            n = len(items)
            for k in range(n + 2):
                if k < n:
                    st1(items[k])
                if 0 <= k - 1 < n:
                    st2(items[k - 1])
                if 0 <= k - 2 < n:
                    st3(items[k - 2])

        def interleave(a, b):
            out = []
            for i in range(max(len(a), len(b))):
                if i < len(a):
                    out.append(a[i])
                if i < len(b):
                    out.append(b[i])
            return out

        esq = push()
        QT_all = sb("QT_all", [128, 4, NTOK], BF16, esq)
        HG_all = sb("HG_all", [128, 4, NTOK], BF16, esq)
        es1 = push()
        W1 = sb("W1", [128, 8, 2560], BF16, es1)
        for kc in range(8):
            S.dma(pool, W1[:, kc, :], w_in[kc * 128:(kc + 1) * 128, 0:2560], [], [R["W1"]])
        W1r = R["W1"]
        QO, KO, VO, UO, GO = 0, 512, 1024, 1536, 2048
        xnT = Rot([sb(f"xnT{i}", [128, 8, 512], BF16, es1) for i in range(2)])
        ktc = Rot([sb(f"ktc{i}", [128, 4, 512], BF16, es1) for i in range(1)])
        vc = Rot([sb(f"vc{i}", [128, 4, 512], BF16, es1) for i in range(1)])
        f512 = Rot([sb(f"f512_{i}", [128, 512], F32, es1) for i in range(2)])
        ubA = [sb(f"ub{i}", [128, 4, 515], F32, es1) for i in range(3)]
        uc = sb("uc", [128, 2, 512], F32, es1)
        ucb = sb("ucb", [128, 2, 512], BF16, es1)
        rr = sb("rr", [128, 2, 512], F32, es1)
        ii = sb("ii", [128, 2, 512], F32, es1)
        aa = sb("aa", [128, 2, 512], F32, es1)
        tq = sb("tq", [128, 2, 512], F32, es1)
        hs = sb("hs", [128, 4, 128], F32, es1)
        hcar = sb("hcar", [128, 4], F32, es1)
        gx = sb("gx", [128, 4, 128], F32, es1)
        gyA = [sb(f"gy{i}", [128, 4, 128], F32, es1) for i in range(2)]
        ubs = sb("ubs", [128, 4, 2, 67], F32, es1)
        hin = sb("hin", [128, 4, 2], F32, es1)
        hout = sb("hout", [128, 4, 2], F32, es1)

        for i_ in range(3):
            S.op(pool, lambda i_=i_: nc.gpsimd.memset(ubA[i_][:], 0.0), [], [R["ub", i_]])
        S.op(pool, lambda: nc.gpsimd.memset(hcar[:], 0.0), [], [R["hcar"]])

        def proj_fm(col0, ncols_tiles, rhs3, rhs_res, n, evac):
            for j in range(ncols_tiles):
                bk, bk_r = banks.next()
                for kc in range(8):
                    mm(bk[:, 0:n], W1[:, kc, col0 + j * 128: col0 + (j + 1) * 128], rhs3[:, kc, 0:n], kc == 0, kc == 7,
                       [W1r, rhs_res], [bk_r])
                evac(j, bk, bk_r)

        def proj_fm_packed(col0, rhs3_own, rhs_res, evac):
            bk, bk_r = banks.next()
            for j in range(4):
                for kc in range(8):
                    mm(bk[:, j * 128:(j + 1) * 128], W1[:, kc, col0 + j * 128: col0 + (j + 1) * 128], rhs3_own(kc), kc == 0, kc == 7,
                       [W1r, rhs_res], [bk_r])
            evac(bk, bk_r)

        def gelu_a(bk, bk_r, gi=0):
            G = R["gelu", gi]
            GX = R["gx"]
            g2 = gx[:].rearrange("p a b -> p (a b)")
            y2 = gyA[gi][:].rearrange("p a b -> p (a b)")
            cp(act, g2, bk[:, :], [bk_r], [GX])
            tt(dve, y2, g2, g2, ALU.mult, [GX], [G])
            ts(dve, y2, y2, 0.044715, 1.0, ALU.mult, ALU.add, [G], [G])
            tt(dve, y2, y2, g2, ALU.mult, [G, GX], [G])

        def gelu_b(gi=0):
            G = R["gelu", gi]
            GX = R["gx"]
            g2 = gx[:].rearrange("p a b -> p (a b)")
            y2 = gyA[gi][:].rearrange("p a b -> p (a b)")
            act_fn(y2, y2, AF.Sigmoid, [G], [G], scale=GELU_C)
            tt(dve, y2, y2, g2, ALU.mult, [G, GX], [G])

        def gelu_gate(bk, bk_r, gi=0):
            gelu_a(bk, bk_r, gi)
            gelu_b(gi)

        def rnn_pair(cp2, n, chunk0, conv_src, conv_out, scan_fn, ub_r, after_sig=None):
            Ruc, Rucb, Rrr, Rii, Raa, Rtq = R["uc"], R["ucb"], R["rr"], R["ii"], R["aa"], R["tq"]
            for k2 in range(2):
                ct = 2 * cp2 + k2
                o = conv_out(k2)
                ts(dve, o, conv_src(ct, 0), cvec[:, ct, 0:1], cvec[:, ct, 4:5], ALU.mult, ALU.add, [ub_r, C], [Ruc])
                for j in range(1, 4):
                    stt(o, conv_src(ct, j), cvec[:, ct, j:j + 1], o, ALU.mult, ALU.add, [Ruc, ub_r, C], [Ruc])
            yield
            cp(act, ucb[:, :, 0:n], uc[:, :, 0:n], [Ruc], [Rucb])
            if after_sig is not None:
                after_sig()
            for k2 in range(2):
                ct = 2 * cp2 + k2
                bk, bk_r = banks.next()
                mm(bk[:, 0:n], wr[:, ct, :], ucb[:, k2, 0:n], True, True, [Rucb, C], [bk_r])
                act_fn(rr[:, k2, 0:n], bk[:, 0:n], AF.Sigmoid, [bk_r, C], [Rrr], bias=cvec[:, ct, 5:6])
                bk, bk_r = banks.next()
                mm(bk[:, 0:n], wi[:, ct, :], ucb[:, k2, 0:n], True, True, [Rucb, C], [bk_r])
                act_fn(ii[:, k2, 0:n], bk[:, 0:n], AF.Sigmoid, [bk_r, C], [Rii], bias=cvec[:, ct, 6:7])
            yield
            for k2 in range(2):
                ct = 2 * cp2 + k2
                act_fn(aa[:, k2, 0:n], rr[:, k2, 0:n], AF.Exp, [Rrr, C], [Raa], scale=coef[:, ct:ct + 1])
                act_fn(tq[:, k2, 0:n], rr[:, k2, 0:n], AF.Exp, [Rrr, C], [Rtq], scale=coef2[:, ct:ct + 1])
            act_fn(tq[:, :, 0:n], tq[:, :, 0:n], AF.Ln, [Rtq, C], [Rtq], bias=c_one[:, 0:1], scale=-1.0)
            act_fn(tq[:, :, 0:n], tq[:, :, 0:n], AF.Exp, [Rtq], [Rtq], scale=0.5)
            yield
            tt(dve, ii[:, :, 0:n], ii[:, :, 0:n], uc[:, :, 0:n], ALU.mult, [Rii, Ruc], [Rii])
            tt(dve, tq[:, :, 0:n], tq[:, :, 0:n], ii[:, :, 0:n], ALU.mult, [Rtq, Rii], [Rtq])
            if chunk0:
                for t in range(3):
                    ts(dve, tq[:, :, t * 128:(t + 1) * 128], tq[:, :, t * 128:(t + 1) * 128], validt[:, t:t + 1], None, ALU.mult, None,
                       [Rtq, C], [Rtq])
            yield
            scan_fn(cp2)
            yield

        OWN_S = 16
        xt, xt_r = xts.next()
        S.dma(sp, xt[:], xsm[:, :], [], [xt_r])
        xb, xb_r = xsb.next()
        norm_stats(xt[:], xt_r, GPRE, xb[:], xb_r)
        xn, xn_r = xnT.next()
        transpose_to(xb, xb_r, xn[:, :, 0:128], xn_r)
        stop("sp1")
        own_rhs = lambda kc, xn=xn: xn[:, kc, 0:128]
        proj_fm_packed(KO, own_rhs, xn_r, lambda bk, bk_r: cp(act, ktn[:].rearrange("p a b -> p (a b)"), bk[:, :], [bk_r], [R["ktn"]]))
        proj_fm_packed(QO, own_rhs, xn_r, lambda bk, bk_r: act_fn(
            QT_all[:, :, OWN_S * 128:(OWN_S + 1) * 128], bk[:, :].rearrange("p (a b) -> p a b", a=4), AF.Copy, [bk_r], [R["QT", OWN_S]],
            scale=0.125))
        proj_fm_packed(GO, own_rhs, xn_r, gelu_gate)

        stop("sp2")

        def tok_major(xn, xn_r, col0, lhs_cols, nrow, evacs):
            bk, bk_r = banks.next()
            for kc in range(8):
                mm(bk[0:nrow, :], xn[:, kc, lhs_cols[0]:lhs_cols[1]], W1[:, kc, col0:col0 + 512], kc == 0, kc == 7, [W1r, xn_r], [bk_r])
            evacs(bk, bk_r)

        def k_tok_s(bk, bk_r):
            f, f_r = f512.next()
            cp(act, f[:], bk[:, :], [bk_r], [f_r])
            S.dma(pool, k_s[:, :], f[:], [f_r], [], final=True)
        tok_major(xn, xn_r, KO, (0, 128), 128, k_tok_s)
        stop("sp2a")
        for s in range(2):
            def v_s_ev(bk, bk_r, s=s):
                f, f_r = f512.next()
                cp(act, f[0:64, :], bk[0:64, :], [bk_r], [f_r])
                cp(dve, vn[s][:], bk[0:64, :], [bk_r], [R["vn", s]])
                S.dma(pool, v_s[s * 64:(s + 1) * 64, :], f[0:64, :], [f_r], [], final=True)
            tok_major(xn, xn_r, VO, (s * 64, (s + 1) * 64), 64, v_s_ev)
            stop("sp2b")
        stop("sp3")
        S.dma(sp, ubs[:, :, :, 0:3], sconvT[:, :, :, :], [], [R["ubs"]])
        S.dma(sp, hin[:], shT[:, :, :], [], [R["hin"]])
        proj_fm(UO, 4, xn, xn_r, 128, lambda j, bk, bk_r: cp(
            act, ubs[:, j, :, 3:67], bk[:, 0:128].rearrange("p (s t) -> p s t", s=2), [bk_r], [R["ubs"]]))

        stop("sp4")

        def scan_s(cp2):
            for k2 in range(2):
                ct = 2 * cp2 + k2
                for s in range(2):
                    S.op(dve, lambda ct=ct, s=s, k2=k2: nc.vector.tensor_tensor_scan(
                        out=hs[:, ct, s * 64:(s + 1) * 64], data0=aa[:, k2, s * 64:(s + 1) * 64], data1=tq[:, k2, s * 64:(s + 1) * 64],
                        initial=hin[:, ct, s:s + 1], op0=ALU.mult, op1=ALU.add), [R["aa"], R["tq"], R["hin"]], [R["hs"]])
        for cp2 in range(2):
            for _ in rnn_pair(cp2, 128, False, lambda ct, j: ubs[:, ct, :, j:j + 64],
                              lambda k2: uc[:, k2, 0:128].rearrange("p (s t) -> p s t", s=2), scan_s, R["ubs"]):
                pass
        tt(dve, HG_all[:, :, OWN_S * 128:(OWN_S + 1) * 128], hs[:, :, :], gyA[0][:], ALU.mult, [R["hs"], R["gelu", 0]], [R["HG", OWN_S]])
        cp(dve, hout[:], hs[:, :, :].rearrange("p c (s t) -> p c s t", s=2)[:, :, :, 63], [R["hs"]], [R["hout"]])
        S.dma(pool, hT_s[:, :, :], hout[:], [R["hout"]], [], final=True)
        S.dma(pool, convT_s[:, :, :, :], ubs[:, :, :, 64:67], [R["ubs"]], [], final=True)

        stop("sample_pre")
        def front(c):
            xn, xn_r = xnT.next()
            for t in range(4):
                ft = 4 * c + t
                xt, xt_r = xts.next()
                S.dma(sp, xt[:], xf[ft * 128:(ft + 1) * 128, :], [], [xt_r])
                xb, xb_r = xsb.next()
                norm_stats(xt[:], xt_r, GPRE, xb[:], xb_r)
                transpose_to(xb, xb_r, xn[:, :, t * 128:(t + 1) * 128], xn_r)
                yield
            ub, ub_r = ubA[c % 3], R["ub", c % 3]
            proj_fm(UO, 4, xn, xn_r, 512, lambda j, bk, bk_r: cp(act, ub[:, j, 3:515], bk[:, :], [bk_r], [ub_r]))
            if c + 1 < 16:
                cp(pool, ubA[(c + 1) % 3][:, :, 0:3], ub[:, :, 512:515], [ub_r], [R["ub", (c + 1) % 3]])
            yield
            kt_, kt_r = ktc.next()
            proj_fm(KO, 4, xn, xn_r, 512, lambda j, bk, bk_r: cp(act, kt_[:, j, :], bk[:, :], [bk_r], [kt_r]))
            S.dma(pool, KT_d[:, :, c * 512:(c + 1) * 512].rearrange("h p n -> p h n"), kt_[:], [kt_r], [R["KT_d"]])
            yield
            v_, v_r = vc.next()
            for t in range(4):
                bk, bk_r = banks.next()
                for kc in range(8):
                    mm(bk[:, :], xn[:, kc, t * 128:(t + 1) * 128], W1[:, kc, VO:VO + 512], kc == 0, kc == 7, [W1r, xn_r], [bk_r])
                cp(dve, v_[:, t, :], bk[:, :], [bk_r], [v_r])
                if t == 1:
                    yield
                if t == 3:
                    f, f_r = f512.next()
                    cp(act, f[:], bk[:, :], [bk_r], [f_r])
                    S.dma(pool, v_own[c * 128:(c + 1) * 128, :], f[:], [f_r], [], final=True)
            S.dma(pool, V_d[4 * c:4 * c + 4, :, :].rearrange("t p n -> p t n"), v_[:], [v_r], [R["V_d"]])
            yield
            bk, bk_r = banks.next()
            for kc in range(8):
                mm(bk[:, :], xn[:, kc, 384:512], W1[:, kc, KO:KO + 512], kc == 0, kc == 7, [W1r, xn_r], [bk_r])
            f, f_r = f512.next()
            cp(act, f[:], bk[:, :], [bk_r], [f_r])
            S.dma(pool, k_own[c * 128:(c + 1) * 128, :], f[:], [f_r], [], final=True)
            own_rhs = lambda kc, xn=xn: xn[:, kc, 384:512]
            proj_fm_packed(QO, own_rhs, xn_r, lambda bk, bk_r, c=c: act_fn(
                QT_all[:, :, c * 128:(c + 1) * 128], bk[:, :].rearrange("p (a b) -> p a b", a=4), AF.Copy, [bk_r], [R["QT", c]], scale=0.125))
            yield
            proj_fm_packed(GO, own_rhs, xn_r, lambda bk, bk_r, c=c: gelu_a(bk, bk_r, c % 2))
            yield

        def rnn(c):
            ub, ub_r = ubA[c % 3], R["ub", c % 3]

            def scan_p(cp2):
                for k2 in range(2):
                    ct = 2 * cp2 + k2
                    S.op(dve, lambda ct=ct, k2=k2: nc.vector.tensor_tensor_scan(
                        out=rr[:, k2, :], data0=aa[:, k2, :], data1=tq[:, k2, :], initial=hcar[:, ct:ct + 1], op0=ALU.mult, op1=ALU.add),
                        [R["aa"], R["tq"], R["hcar"]], [R["rr"]])
                    cp(dve, hcar[:, ct:ct + 1], rr[:, k2, 511:512], [R["rr"]], [R["hcar"]])
                    cp(dve, hs[:, ct, :], rr[:, k2, 384:512], [R["rr"]], [R["hs"]])
            for cp2 in range(2):
                yield from rnn_pair(cp2, 512, c == 0, lambda ct, j: ub[:, ct, j:j + 512], lambda k2: uc[:, k2, :], scan_p, ub_r,
                                    after_sig=((lambda c=c: gelu_b(c % 2)) if cp2 == 0 else None))
            tt(dve, HG_all[:, :, c * 128:(c + 1) * 128], hs[:, :, :], gyA[c % 2][:], ALU.mult, [R["hs"], R["gelu", c % 2]], [R["HG", c]])

        def drive(g1, g2):
            gens = [g for g in (g1, g2) if g is not None]
            while gens:
                for g in list(gens):
                    try:
                        next(g)
                    except StopIteration:
                        gens.remove(g)

        drive(front(0), None)
        for c in range(16):
            if c == 1:
                stop("phase1_c0")
            drive(front(c + 1) if c + 1 < 16 else None, rnn(c))
        S.dma(pool, convT_p[:, :, :], ubA[15 % 3][:, :, 512:515], [R["ub", 15 % 3]], [], final=True)
        S.dma(pool, hT_p[:, :], hcar[:], [R["hcar"]], [], final=True)
        OT_d = dt("OT_d", [64, 8, NTOK], BF16, kind="Internal")
        HG_d = dt("HG_d", [128, 4, NTOK], BF16, kind="Internal")
        S.dma(sp, HG_d[:, :, :], HG_all[:], [R["HG", i] for i in range(NOWN)], [R["HG_d"]])
        stop("phase1")
        barrier()
        pop(es1)

        eso = push()
        OT_all = sb("OT_all", [64, 8, NTOK], BF16, eso)
        es2 = push()
        attn_alloc(es2)
        hctx = attn["ctx"]
        ess = push()
        ckb = sb("ckb", [128, 8, 512], BF16, ess)
        cvb = [sb(f"cvb{s}", [128, 8, 512], BF16, ess) for s in range(2)]
        KTs = [sb(f"KTs{s}", [128, 4, 1088], BF16, ess) for s in range(2)]
        items_all = []
        for s in range(2):
            S.dma(pool, ckb[:], ck[s].rearrange("(t p) n -> p t n", p=128), [], [R["ckb"]])
            S.dma(pool, cvb[s][:], cvd[s].rearrange("(t p) n -> p t n", p=128), [], [R["cvb", s]])
            for hp in range(4):
                pt, pt_r = pts.next()
                for t in range(8):
                    tr(pt[:, t * 128:(t + 1) * 128], ckb[:, t, hp * 128:(hp + 1) * 128], [R["ckb"]], [pt_r])
                cp(dve, KTs[s][:, hp, 0:1024], pt[:, :], [pt_r], [R["KTs", s]])
            cp(pool, KTs[s][:, :, 1024:1088], ktn[:, :, s * 64:(s + 1) * 64], [R["ktn"]], [R["KTs", s]])
        stop("sa1")
        for s in range(2):
            for hp in range(4):
                pair = []
                for h2 in range(2):
                    h = hp * 2 + h2
                    hb = h2 * 64
                    kres = [R["KTs", s], R["cvb", s], R["vn", s]]
                    tiles = [dict(kT=KTs[s][hb:hb + 64, hp, t * 128:(t + 1) * 128], v=cvb[s][:, t, h * 64:(h + 1) * 64], ks=128,
                                  mask=False, valid=None, res=kres) for t in range(8)]
                    newt = dict(kT=KTs[s][hb:hb + 64, hp, 1024:1088], v=vn[s][0:64, h * 64:(h + 1) * 64], ks=64, mask=True, valid=None,
                                res=kres)
                    ctx = hctx[h2]
                    c0 = OWN_S * 128 + s * 64
                    pair.append(make_items(ctx, QT_all[hb:hb + 64, hp, c0:c0 + 64], R["QT", OWN_S], 64,
                                           [[newt], tiles[4:8], tiles[0:4]], OT_all[:, h, c0:c0 + 64], R["OT", OWN_S]))
                for h2 in range(2):
                    S.op(pool, lambda h2=h2: nc.gpsimd.memset(hctx[h2]["S32"][:], 0.0), [], [hctx[h2]["S32r"]])
                run_items(interleave(pair[0], pair[1]))
                stop("sa2")
        stop("sample_attn")
        barrier()
        pop(ess)

        es3 = push()
        KTh = Rot([sb(f"KTh{i}", [128, 8192], BF16, es3) for i in range(1)])
        Vh = Rot([sb(f"Vh{i}", [128, 64, 128], BF16, es3) for i in range(1)])
        dmask2 = sb("dmask2", [128, 2, 128], BF16, es3)
        for k2 in range(2):
            cp(pool, dmask2[:, k2, :], dmask[:, :], [C], [R["dmask2"]])
        DM2 = R["dmask2"]
        S32p = sb("S32p", [128, 2, 512], F32, es3)
        S32r = Res()
        Sbp = [sb(f"Sbp{k}", [128, 2, 512], BF16, es3) for k in range(3)]
        Sbr = [Res() for _ in range(3)]
        ep = Rot([sb(f"ep{i}", [128, 2, 512], F32, es3) for i in range(2)])
        spp = Rot([sb(f"spp{i}", [128, 2, 512], BF16, es3) for i in range(4)])
        wtp = Rot([sb(f"wtp{i}", [128, 2, 512], BF16, es3) for i in range(3)])
        Zp, Zr = bigs[0], [banks.bufs[0][1], banks.bufs[1][1]]
        Wp, Wr = bigs[1], [banks.bufs[2][1], banks.bufs[3][1]]
        Ob = [banks.bufs[4][0], banks.bufs[5][0]]
        Or = [banks.bufs[4][1], banks.bufs[5][1]]
        Z3 = Zp[:, :].rearrange("p (a n) -> p a n", a=2)
        W3 = Wp[:, :].rearrange("p (a n) -> p a n", a=2)

        def make_units(kth, kth_r, vh, vh_r, hp, M):
            q_res = [R["QT", m] for m in range(4 * M, 4 * M + 4)]
            o_res = [R["OT", m] for m in range(4 * M, 4 * M + 4)]
            units = []
            top = 16 * M + 15
            for k, ft in enumerate(range(top, -1, -1)):
                m_min = max(4 * M, -(-(ft - 3) // 4))
                c_lo = (m_min - 4 * M) * 128
                diag = None
                if ft % 4 == 3 and (ft - 3) // 4 >= 4 * M:
                    d0 = ((ft - 3) // 4 - 4 * M) * 128
                    diag = (d0, d0 + 128)
                units.append(dict(kT=[kth[hb:hb + 64, ft * 128:(ft + 1) * 128] for hb in (0, 64)],
                                  v=[vh[:, ft, hb:hb + 64] for hb in (0, 64)],
                                  qT=[QT_all[hb:hb + 64, hp, M * 512:(M + 1) * 512] for hb in (0, 64)], q_res=q_res,
                                  res=[kth_r[ft // 16], vh_r[ft // 16]], c_lo=c_lo, diag=diag, valid=None, k=k,
                                  first=(k == 0), last=(ft == 0),
                                  out_ap=[OT_all[:, hp * 2 + h2, M * 512:(M + 1) * 512] for h2 in range(2)], out_res=o_res))
            return units

        def p1a(u):
            c0 = u["c_lo"]
            for h2 in range(2):
                mm(Zp[:, h2 * 512 + c0:(h2 + 1) * 512], u["kT"][h2], u["qT"][h2][:, c0:512], True, True, u["res"] + u["q_res"],
                   [Zr[h2]], skip=True)
                if u["diag"]:
                    d0, d1 = u["diag"]
                    mm(Zp[:, h2 * 512 + d0:h2 * 512 + d1], ident[:, :], mneg[:, :], False, True, [C], [Zr[h2]], skip=True)
            e, e_r = ep.next()
            u["e"], u["e_r"] = e, e_r
            act_fn(e[:, :, c0:512], Z3[:, :, c0:512], AF.Exp, Zr, [e_r])

        def p1b(u):
            c0 = u["c_lo"]
            e, e_r = u["e"], u["e_r"]
            spt, sp_r = spp.next()
            u["sp"], u["sp_r"] = spt, sp_r
            act_fn(spt[:, :, c0:512], e[:, :, c0:512], AF.Ln, [e_r, C], [sp_r], bias=c_one[:, 0:1])
            if not u["last"]:
                tt(dve, S32p[:, :, c0:512], S32p[:, :, c0:512], spt[:, :, c0:512], ALU.add, [sp_r, S32r], [S32r])

        def pcast(u):
            if not u["last"]:
                c0, k = u["c_lo"], u["k"]
                cp(dve, Sbp[k % 3][:, :, c0:512], S32p[:, :, c0:512], [S32r], [Sbr[k % 3]])

        def p2a(u):
            c0, k = u["c_lo"], u["k"]
            spt, sp_r = u["sp"], u["sp_r"]
            carry = not u["first"]
            for h2 in range(2):
                o = Wp[:, h2 * 512 + c0:(h2 + 1) * 512]
                mm(o, u["kT"][h2], u["qT"][h2][:, c0:512], True, False, u["res"] + u["q_res"], [Wr[h2]], skip=True)
                if u["diag"]:
                    d0, d1 = u["diag"]
                    mm(Wp[:, h2 * 512 + d0:h2 * 512 + d1], ident[:, :], mneg[:, :], False, False, [C], [Wr[h2]], skip=True)
                mm(o, ntri[:, :], spt[:, h2, c0:512], False, not carry, [sp_r, C], [Wr[h2]], skip=True)
                if carry:
                    mm(o, nones[:, :], Sbp[(k - 1) % 3][:, h2, c0:512], False, True, [Sbr[(k - 1) % 3], C], [Wr[h2]], skip=True)

        def p2b(u):
            c0 = u["c_lo"]
            wt, wt_r = wtp.next()
            u["w"], u["w_r"] = wt, wt_r
            act_fn(wt[:, :, c0:512], W3[:, :, c0:512], AF.Exp, Wr, [wt_r])

        def p3(u):
            c0 = u["c_lo"]
            for h2 in range(2):
                mm(Ob[h2][0:64, c0:512], u["v"][h2], u["w"][:, h2, c0:512], u["first"], u["last"], u["res"] + [u["w_r"]], [Or[h2]],
                   skip=True)
                if u["last"]:
                    cp(dve, u["out_ap"][h2], Ob[h2][0:64, 0:512], [Or[h2]], u["out_res"])

        def run_units(units):
            n = len(units)
            for t in range(n + 3):
                if t < n:
                    p1a(units[t])
                if 0 <= t - 2 < n:
                    p2a(units[t - 2])
                if 0 <= t - 1 < n:
                    pcast(units[t - 1])
                if 0 <= t - 2 < n:
                    p2b(units[t - 2])
                if t < n:
                    p1b(units[t])
                if 0 <= t - 3 < n:
                    p3(units[t - 3])

        for hp in range(4):
            kth, kth_r = KTh.next()
            vh, vh_r = Vh.next()
            kq = [R["kthq", q4] for q4 in range(4)]
            vq = [R["vhq", q4] for q4 in range(4)]
            for q4 in range(4):
                S.dma(sp, kth[:, q4 * 2048:(q4 + 1) * 2048], KT_d[hp, :, q4 * 2048:(q4 + 1) * 2048], [R["KT_d"]], [kq[q4]])
                S.dma(sp, vh[:, q4 * 16:(q4 + 1) * 16, :],
                      V_d[q4 * 16:(q4 + 1) * 16, :, hp * 128:(hp + 1) * 128].rearrange("t p n -> p t n"), [R["V_d"]], [vq[q4]])
            kth_r, vh_r = kq, vq
            for M in range(4):
                S.op(pool, lambda: nc.gpsimd.memset(S32p[:], 0.0), [], [S32r])
                for k3 in range(3):
                    S.op(pool, lambda k3=k3: nc.gpsimd.memset(Sbp[k3][:], 0.0), [], [Sbr[k3]])
                run_units(make_units(kth, kth_r, vh, vh_r, hp, M))
                if hp == 0 and M == 0:
                    stop("phase2_m1")
            S.dma(sp, OT_d[:, 2 * hp:2 * hp + 2, :], OT_all[:, 2 * hp:2 * hp + 2, :], [R["OT", i] for i in range(NOWN)],
                  [R["OT_d", hp]])
            if hp == 0:
                stop("phase2_hp0")
        barrier()
        pop(es3)
        pop(es2)
        pop(eso)
        pop(esq)

        stop("phase2")
        X1_d = dt("X1_d", [NOWN, 128, D], F32, kind="Internal")
        NTH = 9 * 128
        es5 = push()
        xnHs = {0: sb("xnH", [128, 8, NTH], BF16, es5), 9: sb("xnHB", [128, 8, 8 * 128], BF16, es5)}
        xnH_rs = {0: RD(), 9: RD()}

        def front_tile(t0, tl):
            xt, xt_r = xts.next()
            S.dma(sp, xt[:], own_rows(t0 + tl), [], [xt_r])
            xb, xb_r = xsb.next()
            norm_stats(xt[:], xt_r, GPRE, xb[:], xb_r)
            transpose_to(xb, xb_r, xnHs[t0][:, :, tl * 128:(tl + 1) * 128], xnH_rs[t0][tl])

        def own_rows(i):
            return xsm[:, :] if i == 16 else xf[(4 * i + 3) * 128:(4 * i + 4) * 128, :]

        wo = sb("wo", [128, 8, D], BF16, es5)
        wga = Rot([sb(f"wga{i}", [128, 8, 128], BF16, es5) for i in range(2)])
        wgb = Rot([sb(f"wgb{i}", [128, 8, 128], BF16, es5) for i in range(2)])
        wa = Rot([sb(f"wa{i}", [64, 8, 128], BF16, es5) for i in range(2)])
        wbt = Rot([sb(f"wbt{i}", [128, 4, 128], BF16, es5) for i in range(2)])
        wo_r = Res()
        S.dma(pool, wo[:], w_o.rearrange("(kc p) n -> p kc n", p=128), [], [wo_r])

        def load_mt(mt):
            a_, a_r = wga.next()
            b_, b_r = wgb.next()
            wa_, wa_r = wa.next()
            wb_2, wb_r = wbt.next()
            S.dma(pool, a_[:], w_in[:, 2560 + mt * 128:2560 + (mt + 1) * 128].rearrange("(kc p) n -> p kc n", p=128), [], [a_r])
            S.dma(pool, b_[:], w_in[:, 3584 + mt * 128:3584 + (mt + 1) * 128].rearrange("(kc p) n -> p kc n", p=128), [], [b_r])
            S.dma(pool, wa_[:], w_a[:, mt * 128:(mt + 1) * 128].rearrange("(h p) n -> p h n", p=64), [], [wa_r])
            S.dma(pool, wb_2[:], w_b[:, mt * 128:(mt + 1) * 128].rearrange("(c p) n -> p c n", p=128), [], [wb_r])
            return (a_, a_r, b_, b_r, wa_, wa_r, wb_2, wb_r)

        def prefetch_3a(t0, NT):
            return load_mt(0)
        pre3a = {}
        wu = Rot([sb(f"wu{i}", [128, 8, 512], BF16, es5) for i in range(2)])
        wd = Rot([sb(f"wd{i}", [128, 4, D], BF16, es5) for i in range(2)])

        def load_fg(fg):
            wu_, wu_r = wu.next()
            wd_, wd_r = wd.next()
            S.dma(pool, wu_[:], w_up[:, fg * 512:(fg + 1) * 512].rearrange("(kc p) n -> p kc n", p=128), [], [wu_r])
            S.dma(pool, wd_[:], w_dn[fg * 512:(fg + 1) * 512, :].rearrange("(f p) n -> p f n", p=128), [], [wd_r])
            return (wu_, wu_r, wd_, wd_r)

        for (t0, ntl) in ((0, 9), (9, 8)):
            NT = ntl * 128
            chunks = [(c0, min(512, NT - c0)) for c0 in range(0, NT, 512)]
            es6 = push()
            otc = sb("otc", [64, 8, NTH], BF16, es6)
            hgc = sb("hgc", [128, 4, NTH], BF16, es6)
            otc_r, hgc_r = Res(), Res()
            S.dma(sp, otc[:, :, 0:NT], OT_d[:, :, t0 * 128:t0 * 128 + NT], [R["OT_d", hp_] for hp_ in range(4)], [otc_r])
            S.dma(sp, hgc[:, :, 0:NT], HG_d[:, :, t0 * 128:t0 * 128 + NT], [R["HG_d"]], [hgc_r])
            mTh = sb("mTh", [128, 8, NTH], BF16, es6)
            sga = Rot([sb(f"sga{i}", [128, 512], F32, es6) for i in range(2)])
            sgb = Rot([sb(f"sgb{i}", [128, 512], F32, es6) for i in range(2)])
            mixs = Rot([sb(f"mixs{i}", [128, D], F32, es6) for i in range(2)])
            x1s = Rot([sb(f"x1s{i}", [128, D], F32, es6) for i in range(2)])
            mT_r = RD()
            nxt = pre3a.pop(t0) if t0 in pre3a else prefetch_3a(t0, NT)
            xnH, xnH_r = xnHs[t0], xnH_rs[t0]
            if t0 == 0:
                for tl in range(ntl):
                    front_tile(0, tl)
            for mt in range(8):
                (a_, a_r, b_, b_r, wa_, wa_r, wb_2, wb_r) = nxt
                if mt < 7:
                    nxt = load_mt(mt + 1)
                for (c0, n) in chunks:
                    xr = [xnH_r[tl] for tl in range(c0 // 128, (c0 + n) // 128)]
                    bkA, bkA_r = banks.next()
                    for kc in range(8):
                        mm(bkA[:, 0:n], a_[:, kc, :], xnH[:, kc, c0:c0 + n], kc == 0, kc == 7, [a_r] + xr, [bkA_r])
                    sa, sa_r = sga.next()
                    act_fn(sa[:, 0:n], bkA[:, 0:n], AF.Sigmoid, [bkA_r], [sa_r])
                    bkB, bkB_r = banks.next()
                    for kc in range(8):
                        mm(bkB[:, 0:n], b_[:, kc, :], xnH[:, kc, c0:c0 + n], kc == 0, kc == 7, [b_r] + xr, [bkB_r])
                    sb_, sb_r = sgb.next()
                    act_fn(sb_[:, 0:n], bkB[:, 0:n], AF.Sigmoid, [bkB_r], [sb_r])
                    bkY, bkY_r = banks.next()
                    for h in range(8):
                        mm(bkY[:, 0:n], wa_[0:64, h, :], otc[0:64, h, c0:c0 + n], h == 0, h == 7, [wa_r, otc_r], [bkY_r])
                    tt(dve, sa[:, 0:n], sa[:, 0:n], bkY[:, 0:n], ALU.mult, [sa_r, bkY_r], [sa_r])
                    bkZ, bkZ_r = banks.next()
                    for ct in range(4):
                        mm(bkZ[:, 0:n], wb_2[:, ct, :], hgc[:, ct, c0:c0 + n], ct == 0, ct == 3, [wb_r, hgc_r], [bkZ_r])
                    tt(dve, sb_[:, 0:n], sb_[:, 0:n], bkZ[:, 0:n], ALU.mult, [sb_r, bkZ_r], [sb_r])
                    tt(dve, mTh[:, mt, c0:c0 + n], sa[:, 0:n], sb_[:, 0:n], ALU.add, [sa_r, sb_r], [mT_r[c0]])
            allmT = [mT_r[c0] for (c0, n) in chunks]
            nxt_fg = load_fg(0)
            for tl in range(ntl):
                i = t0 + tl
                mx, mx_r = mixs.next()
                for half in range(2):
                    bk, bk_r = banks.next()
                    for kc in range(8):
                        mm(bk[:, :], mTh[:, kc, tl * 128:(tl + 1) * 128], wo[:, kc, half * 512:(half + 1) * 512], kc == 0, kc == 7,
                           allmT + [wo_r], [bk_r])
                    cp(act, mx[:, half * 512:(half + 1) * 512], bk[:, :], [bk_r], [mx_r])
                st, st_r = norm_stats(mx[:], mx_r, GPOST, None, None)
                stt(mx[:], mx[:], st[:, 2:3], gts[:, GPOST, :], ALU.mult, ALU.mult, [mx_r, st_r, C], [mx_r])
                xt, xt_r = xts.next()
                S.dma(sp, xt[:], own_rows(i), [], [xt_r])
                x1, x1_r = x1s.next()
                tt(dve, x1[:], mx[:], xt[:], ALU.add, [mx_r, xt_r], [x1_r])
                S.dma(sp, X1_d[i, :, :], x1[:], [x1_r], [R["X1_d", i]])
                xb, xb_r = xsb.next()
                norm_stats(x1[:], x1_r, GPREF, xb[:], xb_r)
                transpose_to(xb, xb_r, xnH[:, :, tl * 128:(tl + 1) * 128], xnH_r[tl])
            barrier()
            pop(es6)
            es7 = push()
            facc = sb("facc", [128, 9, D], F32, es7)
            rl = Rot([sb(f"rl{i}", [128, 512], F32, es7) for i in range(2)])
            hT = Rot([sb(f"hT{i}", [128, 4, 512], BF16, es7) for i in range(2)])
            facc_r = RD()

            nxt = nxt_fg
            if t0 == 0:
                pre3a[9] = prefetch_3a(9, 8 * 128)
            for fg in range(8):
                (wu_, wu_r, wd_, wd_r) = nxt
                if fg < 7:
                    nxt = load_fg(fg + 1)
                for (c0, n) in chunks:
                    xr = [xnH_r[tl] for tl in range(c0 // 128, (c0 + n) // 128)]
                    h_, h_r = hT.next()
                    for f4 in range(4):
                        bk, bk_r = banks.next()
                        for kc in range(8):
                            mm(bk[:, 0:n], wu_[:, kc, f4 * 128:(f4 + 1) * 128], xnH[:, kc, c0:c0 + n], kc == 0, kc == 7, [wu_r] + xr, [bk_r])
                        r_, r_r = rl.next()
                        act_fn(r_[:, 0:n], bk[:, 0:n], AF.Relu, [bk_r], [r_r])
                        tt(pool, h_[:, f4, 0:n], r_[:, 0:n], r_[:, 0:n], ALU.mult, [r_r], [h_r])
                    for tq_ in range(n // 128):
                        tl = c0 // 128 + tq_
                        for half in range(2):
                            bk, bk_r = banks.next()
                            for f4 in range(4):
                                mm(bk[:, :], h_[:, f4, tq_ * 128:(tq_ + 1) * 128], wd_[:, f4, half * 512:(half + 1) * 512], f4 == 0, f4 == 3,
                                   [h_r, wd_r], [bk_r])
                            dst = facc[:, tl, half * 512:(half + 1) * 512]
                            if fg == 0:
                                cp(act, dst, bk[:, :], [bk_r], [facc_r[tl]])
                            else:
                                tt(dve, dst, dst, bk[:, :], ALU.add, [bk_r, facc_r[tl]], [facc_r[tl]])
                if t0 == 0:
                    front_tile(9, fg)
            for tl in range(ntl):
                i = t0 + tl
                st, st_r = norm_stats(facc[:, tl, :], facc_r[tl], GPOSTF, None, None)
                stt(facc[:, tl, :], facc[:, tl, :], st[:, 2:3], gts[:, GPOSTF, :], ALU.mult, ALU.mult, [facc_r[tl], st_r, C], [facc_r[tl]])
                xt, xt_r = xts.next()
                S.dma(sp, xt[:], X1_d[i, :, :], [R["X1_d", i]], [xt_r])
                tt(dve, xt[:], xt[:], facc[:, tl, :], ALU.add, [xt_r, facc_r[tl]], [xt_r])
                dst = y_s[:, :] if i == 16 else y_own[i * 128:(i + 1) * 128, :]
                S.dma(sp, dst, xt[:], [xt_r], [xt_r], final=True)
            if t0 == 0:
                stop("phase3_c0")
            barrier()
            pop(es7)
        pop(es5)


_NC_CACHE = {}


def kernel(x_prompt, x_sample, cache_k, cache_v, state_conv, state_h, w_in, g_pre_mix, w_conv, b_conv, w_r, b_r, w_i, b_i,
           lam, w_a_out, w_b_out, w_o, g_post_mix, g_pre_ffn, w_up, w_down, g_post_ffn):
    if "nc" not in _NC_CACHE:
        _NC_CACHE["nc"] = build()
    nc = _NC_CACHE["nc"]
    in_maps = prep_inputs(x_prompt, x_sample, cache_k, cache_v, state_conv, state_h, w_in, g_pre_mix, w_conv, b_conv, w_r, b_r,
                          w_i, b_i, lam, w_a_out, w_b_out, w_o, g_post_mix, g_pre_ffn, w_up, w_down, g_post_ffn)
    res = run_bass_kernel_spmd(nc, in_maps, core_ids=list(range(8)))
    return assemble(res.results)


def prep_inputs(x_prompt, x_sample, cache_k, cache_v, state_conv, state_h, w_in, g_pre_mix, w_conv, b_conv, w_r, b_r, w_i, b_i,
                lam, w_a_out, w_b_out, w_o, g_post_mix, g_pre_ffn, w_up, w_down, g_post_ffn):
    f = lambda a: np.ascontiguousarray(np.asarray(a, dtype=np.float32))
    x_prompt, x_sample = f(x_prompt), f(x_sample)
    cache_k, cache_v, state_conv, state_h = f(cache_k), f(cache_v), f(state_conv), f(state_h)

    def chT(v):
        return np.ascontiguousarray(f(v).reshape(4, 128).T)

    gvv = np.stack([np.broadcast_to(f(g)[0][None, :], (128, D)) for g in (g_pre_mix, g_post_mix, g_pre_ffn, g_post_ffn)])
    gvv = np.ascontiguousarray(gvv)
    wc = f(w_conv)[0]
    cvec = np.stack([chT(wc[0]), chT(wc[1]), chT(wc[2]), chT(wc[3]), chT(f(b_conv)[0]), chT(f(b_r)[0]), chT(f(b_i)[0]),
                     chT(f(lam)[0])], axis=-1)
    cvec = np.ascontiguousarray(cvec)

    def bd(w):
        w = f(w)[0]
        out = np.zeros((4, 128, 128), np.float32)
        for ct in range(4):
            out[ct, 0:64, 0:64] = w[2 * ct]
            out[ct, 64:128, 64:128] = w[2 * ct + 1]
        return out

    common = dict(gv=gvv, cvec=cvec, wrbd=bd(w_r), wibd=bd(w_i), w_in=f(w_in)[0], w_a=f(w_a_out)[0], w_b=f(w_b_out)[0],
                  w_o=f(w_o)[0], w_up=f(w_up)[0], w_dn=f(w_down)[0])
    in_maps = []
    for c in range(8):
        b, j = c // 4, c % 4
        pad = 384 - 128 * j
        xfr = np.zeros((8192, D), np.float32)
        xfr[pad:] = x_prompt[b, 0:8192 - pad]
        s0 = 2 * c
        valid = np.zeros((128, 4), np.float32)
        for i in range(4):
            valid[:, i] = 1.0 if i >= 3 - j else 0.0
        sc = state_conv[0, s0:s0 + 2]
        scT = np.ascontiguousarray(sc.reshape(2, 3, 4, 128).transpose(3, 2, 0, 1))
        shT = np.ascontiguousarray(state_h[0, s0:s0 + 2].reshape(2, 4, 128).transpose(2, 1, 0))
        m = dict(common)
        m.update(xf=xfr, xsm=np.ascontiguousarray(x_sample[s0:s0 + 2].reshape(128, D)),
                 ck=np.ascontiguousarray(cache_k[0, s0:s0 + 2].reshape(2, 1024, 512)),
                 cv=np.ascontiguousarray(cache_v[0, s0:s0 + 2].reshape(2, 1024, 512)),
                 sconvT=scT, shT=shT, valid=valid)
        in_maps.append(m)
    return in_maps


def assemble(rs):
    y_prompt = np.zeros((2, 8192, D), np.float32)
    k_prompt = np.zeros((1, 2, 8192, 8, 64), np.float32)
    v_prompt = np.zeros((1, 2, 8192, 8, 64), np.float32)
    conv_prompt = np.zeros((1, 2, 3, 512), np.float32)
    h_prompt = np.zeros((1, 2, 512), np.float32)
    y_sample = np.zeros((16, 64, D), np.float32)
    k_sample = np.zeros((1, 16, 64, 8, 64), np.float32)
    v_sample = np.zeros((1, 16, 64, 8, 64), np.float32)
    conv_sample = np.zeros((1, 16, 3, 512), np.float32)
    h_sample = np.zeros((1, 16, 512), np.float32)
    for c in range(8):
        b, j = c // 4, c % 4
        r = rs[c]
        for m in range(16):
            g0 = (4 * m + j) * 128
            y_prompt[b, g0:g0 + 128] = r["y_own"][m * 128:(m + 1) * 128]
            k_prompt[0, b, g0:g0 + 128] = r["k_own"][m * 128:(m + 1) * 128].reshape(128, 8, 64)
            v_prompt[0, b, g0:g0 + 128] = r["v_own"][m * 128:(m + 1) * 128].reshape(128, 8, 64)
        if j == 3:
            conv_prompt[0, b] = r["convT_p"].transpose(2, 1, 0).reshape(3, 512)
            h_prompt[0, b] = r["hT_p"].T.reshape(512)
        for s in range(2):
            sid = 2 * c + s
            y_sample[sid] = r["y_s"][s * 64:(s + 1) * 64]
            k_sample[0, sid] = r["k_s"][s * 64:(s + 1) * 64].reshape(64, 8, 64)
            v_sample[0, sid] = r["v_s"][s * 64:(s + 1) * 64].reshape(64, 8, 64)
            conv_sample[0, sid] = r["convT_s"][:, :, s, :].transpose(2, 1, 0).reshape(3, 512)
            h_sample[0, sid] = r["hT_s"][:, :, s].T.reshape(512)
    return (y_prompt, y_sample, k_prompt, v_prompt, conv_prompt, h_prompt, k_sample, v_sample, conv_sample, h_sample)
```
